# Optimizing a Trainium2 kernel written in Bass

```python
import jax, jax.numpy as jnp
from jax import lax
import numpy as np

D_MODEL = 1024
BATCH = 8
SEQ = 4096
DEPTH = 1
DEC_BATCH = 4
DEC_SEQ = 4096
PAST_LEN = 128

HEAD_DIM = 64
GRID_W = 64
NA_HEADS = 8
NA_KH = 8
NA_KW = 16
NA_WIDTH = NA_HEADS * HEAD_DIM
SW_HEADS = 8
SW_KV_HEADS = 2
SW_GROUP = SW_HEADS // SW_KV_HEADS
SW_WIDTH = SW_HEADS * HEAD_DIM
SW_KV_WIDTH = SW_KV_HEADS * HEAD_DIM
WINDOW = 128
WIN_BLOCK = 128
ROPE_THETA = 10000.0
NORM_EPS = 1e-6
SPLIT_SIZES = (NA_WIDTH, NA_WIDTH, NA_WIDTH, NA_WIDTH,
               SW_WIDTH, SW_KV_WIDTH, SW_KV_WIDTH, SW_WIDTH,
               D_MODEL, D_MODEL)
IN_WIDTH = sum(SPLIT_SIZES)

kernel_name = "hybrid_na_swa_gated_encoder"


def rms_norm(x, g):
    xf = x.astype(jnp.float32)
    y = xf * lax.rsqrt(jnp.mean(xf * xf, axis=-1, keepdims=True) + NORM_EPS)
    return (y * g.astype(jnp.float32)).astype(x.dtype)


def rotary(x):
    S, d = x.shape[1], x.shape[-1]
    half = d // 2
    inv = ROPE_THETA ** (-jnp.arange(half, dtype=jnp.float32) / half)
    ang = jnp.arange(S, dtype=jnp.float32)[:, None] * inv[None, :]
    cos = jnp.cos(ang)[None, :, None, :]
    sin = jnp.sin(ang)[None, :, None, :]
    xf = x.astype(jnp.float32)
    x1, x2 = xf[..., :half], xf[..., half:]
    out = jnp.concatenate([x1 * cos - x2 * sin, x2 * cos + x1 * sin], axis=-1)
    return out.astype(x.dtype)


def neighbourhood_attention(q, k, v, rpb):
    B, S, H, d = q.shape
    rows = S // GRID_W
    kh = min(NA_KH, rows)
    r = jnp.arange(rows)
    krow = jnp.clip(r - kh // 2, 0, rows - kh)[:, None] + jnp.arange(kh)[None, :]
    c = jnp.arange(GRID_W)
    cs = jnp.clip(c - NA_KW // 2, 0, GRID_W - NA_KW)
    col_in = (c[None, :] >= cs[:, None]) & (c[None, :] < cs[:, None] + NA_KW)
    dr = krow - r[:, None] + (NA_KH - 1)
    dc = jnp.clip(c[None, :] - c[:, None], -(NA_KW - 1), NA_KW - 1) + (NA_KW - 1)
    bias = rpb[:, dr[:, None, :, None], dc[None, :, None, :]].astype(jnp.float32)
    q5 = q.reshape(B, rows, GRID_W, H, d)
    kg = k.reshape(B, rows, GRID_W, H, d)[:, krow]
    vg = v.reshape(B, rows, GRID_W, H, d)[:, krow]
    s = jnp.einsum('brqhd,brkwhd->bhrqkw', q5, kg).astype(jnp.float32) * (d ** -0.5)
    s = s + bias[None]
    s = jnp.where(col_in[:, None, :], s, -jnp.inf)
    p = jax.nn.softmax(s, axis=(-2, -1))
    o = jnp.einsum('bhrqkw,brkwhd->brqhd', p.astype(v.dtype), vg)
    return o.reshape(B, S, H * d)


def window_attention(q, k, v, sink):
    B, S, H, d = q.shape
    kvh = k.shape[2]
    g = H // kvh
    nb = S // WIN_BLOCK
    pad = ((0, 0), (WIN_BLOCK, WIN_BLOCK), (0, 0), (0, 0))
    kp = jnp.pad(k, pad).reshape(B, nb + 2, WIN_BLOCK, kvh, d)
    vp = jnp.pad(v, pad).reshape(B, nb + 2, WIN_BLOCK, kvh, d)
    kb = jnp.concatenate([kp[:, :-2], kp[:, 1:-1], kp[:, 2:]], axis=2)
    vb = jnp.concatenate([vp[:, :-2], vp[:, 1:-1], vp[:, 2:]], axis=2)
    qb = q.reshape(B, nb, WIN_BLOCK, kvh, g, d)
    s = jnp.einsum('bnqkgd,bnckd->bnkgqc', qb, kb).astype(jnp.float32) * (d ** -0.5)
    qi = jnp.arange(WIN_BLOCK)
    kj = jnp.arange(3 * WIN_BLOCK)
    rel = kj[None, :] - WIN_BLOCK - qi[:, None]
    kpos = jnp.arange(nb)[:, None] * WIN_BLOCK + kj[None, :] - WIN_BLOCK
    mask = (jnp.abs(rel) <= WINDOW)[None] & ((kpos >= 0) & (kpos < S))[:, None, :]
    s = jnp.where(mask[None, :, None, None], s, -jnp.inf)
    sk = sink.reshape(kvh, g).astype(jnp.float32)[None, None, :, :, None, None]
    m = jnp.maximum(jnp.max(s, axis=-1, keepdims=True), sk)
    p = jnp.exp(s - m)
    l = jnp.sum(p, axis=-1, keepdims=True) + jnp.exp(sk - m)
    o = jnp.einsum('bnkgqc,bnckd->bnqkgd', (p / l).astype(v.dtype), vb)
    return o.reshape(B, S, H * d)


def layer(x, norm_g, w_in, qn_a, kn_a, rpb_a, qn_b, kn_b, sink_b, w_out_a, w_out_b, w_o):
    B, S, _ = x.shape
    h = rms_norm(x, norm_g)
    proj = h @ w_in
    offsets = []
    acc = 0
    for n in SPLIT_SIZES[:-1]:
        acc += n
        offsets.append(acc)
    q_a, k_a, v_a, z_a, q_b, k_b, v_b, z_b, g_a, g_b = jnp.split(proj, offsets, axis=-1)
    q_a = rms_norm(q_a.reshape(B, S, NA_HEADS, HEAD_DIM), qn_a)
    k_a = rms_norm(k_a.reshape(B, S, NA_HEADS, HEAD_DIM), kn_a)
    v_a = v_a.reshape(B, S, NA_HEADS, HEAD_DIM)
    y_a = neighbourhood_attention(q_a, k_a, v_a, rpb_a) * jax.nn.silu(z_a)
    q_b = rotary(rms_norm(q_b.reshape(B, S, SW_HEADS, HEAD_DIM), qn_b))
    k_b = rotary(rms_norm(k_b.reshape(B, S, SW_KV_HEADS, HEAD_DIM), kn_b))
    v_b = v_b.reshape(B, S, SW_KV_HEADS, HEAD_DIM)
    y_b = window_attention(q_b, k_b, v_b, sink_b) * jax.nn.silu(z_b)
    merged = jax.nn.sigmoid(g_a) * (y_a @ w_out_a) + jax.nn.sigmoid(g_b) * (y_b @ w_out_b)
    return x + merged @ w_o


def setup_inputs(seed: int = 0) -> dict:
    key = jax.random.key(seed)
    ks = jax.random.split(key, 14)
    f = jnp.float32
    return {
        "x_prompt": jax.random.normal(ks[0], (BATCH, SEQ, D_MODEL), f),
        "x_sample": jax.random.normal(ks[1], (DEC_BATCH, DEC_SEQ, D_MODEL), f),
        "norm_g": 1.0 + 0.05 * jax.random.normal(ks[2], (DEPTH, D_MODEL), f),
        "w_in": jax.random.normal(ks[3], (DEPTH, D_MODEL, IN_WIDTH), f) * D_MODEL ** -0.5,
        "qn_a": 1.0 + 0.05 * jax.random.normal(ks[4], (DEPTH, HEAD_DIM), f),
        "kn_a": 1.0 + 0.05 * jax.random.normal(ks[5], (DEPTH, HEAD_DIM), f),
        "rpb_a": 0.1 * jax.random.normal(ks[6], (DEPTH, NA_HEADS, 2 * NA_KH - 1, 2 * NA_KW - 1), f),
        "qn_b": 1.0 + 0.05 * jax.random.normal(ks[7], (DEPTH, HEAD_DIM), f),
        "kn_b": 1.0 + 0.05 * jax.random.normal(ks[8], (DEPTH, HEAD_DIM), f),
        "sink_b": 0.5 * jax.random.normal(ks[9], (DEPTH, SW_HEADS), f),
        "w_out_a": jax.random.normal(ks[10], (DEPTH, NA_WIDTH, D_MODEL), f) * NA_WIDTH ** -0.5,
        "w_out_b": jax.random.normal(ks[11], (DEPTH, SW_WIDTH, D_MODEL), f) * SW_WIDTH ** -0.5,
        "w_o": jax.random.normal(ks[12], (DEPTH, D_MODEL, D_MODEL), f) * D_MODEL ** -0.5,
    }


def reference(x_prompt, x_sample, norm_g, w_in, qn_a, kn_a, rpb_a, qn_b, kn_b, sink_b, w_out_a, w_out_b, w_o):
    y_prompt = x_prompt
    y_sample = x_sample
    for l in range(DEPTH):
        y_prompt = layer(y_prompt, norm_g[l], w_in[l], qn_a[l], kn_a[l], rpb_a[l], qn_b[l], kn_b[l],
                         sink_b[l], w_out_a[l], w_out_b[l], w_o[l])
        y_sample = layer(y_sample, norm_g[l], w_in[l], qn_a[l], kn_a[l], rpb_a[l], qn_b[l], kn_b[l],
                         sink_b[l], w_out_a[l], w_out_b[l], w_o[l])
    return (y_prompt, y_sample)
```

```python
import contextlib
import numpy as np
import ml_dtypes
import concourse.bass as bass
import concourse.mybir as mybir
from concourse.bass_utils import run_bass_kernel_spmd

F32 = mybir.dt.float32
BF16 = mybir.dt.bfloat16
AF = mybir.ActivationFunctionType
ALU = mybir.AluOpType

D = 1024
KC = 8
SEQ = 4096
GW = 64
NROWS = 64
RS = 16
EXTR = RS + 8
EXT = EXTR * GW
QS = RS * GW
NQB = QS // 512
NEB = EXT // 512
NT = EXT // 128
NCORES = 8
NSLOT = (12 * NROWS // RS) // NCORES
NVAR = 23
NVALID = 8 * 6 + 2
EPS = 1e-6

C_QA, C_KA, C_VA, C_ZA, C_QB, C_KB, C_VB, C_ZB, C_GA, C_GB = 0, 512, 1024, 1536, 2048, 2560, 2688, 2816, 3328, 4352

T_KA = [0, 1]
T_KBVB = 2
T_VA = [3, 4]
T_QA = [5, 6]
T_QB = [7, 8]
T_ZA = [9, 10]
T_ZB = [11, 12]
T_GA = [13, 14, 15, 16]
T_GB = [17, 18, 19, 20]
T_WAB = [21, 22, 23, 24]
T_WO = [25, 26, 27, 28]
NTILES = 29


class Sem:
    def __init__(self, name, is_dma):
        self.name = name
        self.is_dma = is_dma
        self.handle = None
        self.count = 0


class Tok:
    __slots__ = ("sem", "order", "value", "op")

    def __init__(self, sem, order, value=None, op=None):
        self.sem = sem
        self.order = order
        self.value = value
        self.op = op


class Buf:
    def __init__(self, name):
        self.name = name
        self.writers = []
        self.readers = []
        self.dsem = None


class Op:
    __slots__ = ("fn", "deps", "tok", "signal", "is_dma", "key", "idx", "eng", "info")

    def __init__(self, fn, deps, tok, is_dma, key, idx, eng):
        self.fn = fn
        self.deps = deps
        self.tok = tok
        self.signal = False
        self.is_dma = is_dma
        self.key = key
        self.idx = idx
        self.eng = eng


class Rec:
    ENGS = ("pe", "act", "dve", "pool", "sp")

    def __init__(self):
        self.ops = {e: [] for e in self.ENGS}
        self.sems = {e: Sem("s_" + e, False) for e in self.ENGS}
        self.dsems = []
        self.store_toks = []
        self.t = 0.0
        self.nrec = 0

    def _merge(self, deps, t):
        k = id(t.sem)
        if k not in deps or t.order > deps[k].order:
            deps[k] = t

    def _trim(self, lst, t):
        for x in lst:
            if x.sem is t.sem and x.order > t.order:
                return
        lst[:] = [x for x in lst if x.sem is not t.sem]
        lst.append(t)

    def op(self, eng, fn, reads=(), writes=(), dma_owner=None, extra=(), lag=0.0):
        deps = {}
        for b in reads:
            for t in b.writers:
                self._merge(deps, t)
        for b in writes:
            for t in b.readers:
                self._merge(deps, t)
            for t in b.writers:
                self._merge(deps, t)
        for t in extra:
            self._merge(deps, t)
        if dma_owner is not None:
            if dma_owner.dsem is None:
                dma_owner.dsem = Sem("d_" + dma_owner.name, True)
                self.dsems.append(dma_owner.dsem)
            s = dma_owner.dsem
            s.count += 16
            tok = Tok(s, (float(s.count), 0), s.count)
            self.nrec += 1
        else:
            self.nrec += 1
            tok = Tok(self.sems[eng], (self.t + lag, self.nrec))
        o = Op(fn, list(deps.values()), tok, dma_owner is not None, self.t + lag, self.nrec, eng)
        tok.op = o
        o.info = ([b.name for b in reads], [b.name for b in writes])
        self.ops[eng].append(o)
        wset = set(id(b) for b in writes)
        for b in writes:
            if b.readers:
                b.writers = [tok]
                b.readers = []
            else:
                self._trim(b.writers, tok)
        for b in reads:
            if id(b) not in wset:
                self._trim(b.readers, tok)
        return tok

    def barrier(self, bufs):
        pass

    def finalize(self):
        for e in self.ENGS:
            self.ops[e].sort(key=lambda o: (o.key, o.idx))
            for o in self.ops[e]:
                for t in o.deps:
                    if t.op is not None:
                        assert (t.op.key, t.op.idx) < (o.key, o.idx), ("non-monotone dep", e, o.key, o.idx, o.info, t.op.eng, t.op.key, t.op.idx, t.op.info)
        for e in self.ENGS:
            for o in self.ops[e]:
                for t in o.deps:
                    if t.op is not None and not t.op.is_dma:
                        if not (e == "pe" and t.sem is self.sems["pe"]):
                            t.op.signal = True
        for e in self.ENGS:
            c = 0
            for o in self.ops[e]:
                if not o.is_dma and o.signal:
                    c += 1
                    o.tok.value = c

    def emit(self, eng_name, eng):
        seen = {}
        own = self.sems[eng_name]
        n_wait = 0
        for o in self.ops[eng_name]:
            for t in o.deps:
                if eng_name == "pe" and t.sem is own:
                    continue
                v = t.value
                assert v is not None
                if seen.get(id(t.sem), 0) >= v:
                    continue
                seen[id(t.sem)] = v
                eng.wait_ge(t.sem.handle, v)
                n_wait += 1
            ins = o.fn(eng)
            if o.is_dma:
                ins.then_inc(o.tok.sem.handle, 16)
            elif o.signal:
                ins.then_inc(own.handle, 1)
        return n_wait


def build_program(nslot=NSLOT, dbg=False):
    nc = bass.Bass("TRN2", target_bir_lowering=False)
    R = Rec()
    es = contextlib.ExitStack()

    def dram_in(name, shape, dt=F32):
        return nc.dram_tensor(name, list(shape), dt, kind="ExternalInput").ap()

    xs = dram_in("xs", [nslot, EXT, D])
    w_in = dram_in("w_in", [D, 5376])
    w_oa = dram_in("w_out_a", [512, D])
    w_ob = dram_in("w_out_b", [512, D])
    w_o = dram_in("w_o", [D, D])
    gbc_d = dram_in("gbc", [128, D])
    gains_d = dram_in("gains", [128, 4])
    sink_d = dram_in("sinkb", [128, 8])
    Gt_d = dram_in("Gt", [128, 8 * NVAR * 64])
    Mk_d = dram_in("Mk", [128, NVAR * 64])
    cbf_d = dram_in("cbf", [128, 3 * 128], BF16)
    swam_d = dram_in("swam", [128, 256])
    valid_d = dram_in("valid", [128, nslot * NVALID])
    cs_d = dram_in("cs", [nslot, 128, 2, EXT])
    y = nc.dram_tensor("y", [nslot, QS, D], F32, kind="ExternalOutput").ap()
    wscr = nc.dram_tensor("wscr", [NTILES, 128, 2048], BF16).ap()

    def sb(name, shape, dt):
        return es.enter_context(nc.sbuf_tensor(name, list(shape), dt))

    def ps(name, shape, dt):
        return es.enter_context(nc.psum_tensor(name, list(shape), dt))

    hT = sb("hT", [128, KC, EXT], BF16)
    KaT = sb("KaT", [128, 4, EXT], BF16)
    KbT = sb("KbT", [128, EXT], BF16)
    Va = sb("Va", [128, NT, 768], BF16)
    Vb = sb("Vb", [128, NT, 320], BF16)
    Et = sb("Et", [128, 8, NVAR, 64], BF16)
    cbf = sb("cbf_s", [128, 3 * 128], BF16)
    ident = cbf[:, 0:128]
    BDm = cbf[:, 128:256]
    Perm = cbf[:, 256:384]
    swam = sb("swam_s", [128, 256], F32)
    gbc = sb("gbc_s", [128, D], F32)
    gains = sb("gains_s", [128, 4], F32)
    esink = sb("esink", [128, 8], F32)
    valid = sb("valid_s", [128, nslot * NVALID], F32)
    epsb = sb("epsb", [128, 1], F32)
    cpow = sb("cpow", [128, 3], F32)
    wbuf = [sb(f"wbuf{i}", [128, 2048], BF16) for i in range(3)]
    arena = sb("arena", [128, 4096], F32)
    xt = [sb(f"xt{i}", [128, D], F32) for i in range(4)]
    xn = [sb(f"xn{i}", [128, D], BF16) for i in range(2)]
    stat = [sb(f"stat{i}", [128, 4], F32) for i in range(2)]
    sq = [sb(f"sq{i}", [128, 512], BF16) for i in range(2)]
    sd = [sb(f"sd{i}", [128, 512], F32) for i in range(2)]
    qbn = [sb(f"qbn{i}", [128, 512], BF16) for i in range(1)]
    cst = [sb(f"cst{i}", [128, 2, 512], F32) for i in range(1)]
    QaT = sb("QaT", [128, 4, 512], BF16)
    QbT = sb("QbT", [128, 4, 512], BF16)
    PT = [sb(f"PT{i}", [128, 512], BF16) for i in range(6)]
    szb = [sb(f"sz{i}", [128, 512], F32) for i in range(2)]
    rdb = [sb(f"rd{i}", [128, 512], F32) for i in range(2)]
    yaT2 = [sb(f"yaT{i}", [128, 4, 512], BF16) for i in range(2)]
    ybT2 = [sb(f"ybT{i}", [128, 4, 512], BF16) for i in range(2)]
    mT = sb("mT", [128, KC, 512], BF16)
    exb = [arena[:, 0:512], arena[:, 512:1024]]
    rt0 = sb("rt0", [128, 512], F32)
    rt1 = sb("rt1", [128, 512], F32)
    rtb = [rt0[:], rt1[:]]
    sga = [arena[:, 2048:2560], arena[:, 2560:3072]]
    sgb = [arena[:, 3072:3584], arena[:, 3584:4096]]
    m1b = arena[:, 1024:1536]
    m2b = arena[:, 1536:2048]

    psA = [ps(f"psA{i}", [128, 512], F32) for i in range(2)]
    psS = [ps(f"psS{i}", [128, 512], F32) for i in range(3)]
    psO = [ps(f"psO{i}", [128, 512], F32) for i in range(2)]
    psTf = ps("psT0", [128, 512], F32)
    psT = [psTf[:].bitcast(BF16)]

    def record(R, wseq_in):
        B = {}

        def buf(name):
            if name not in B:
                B[name] = Buf(name)
            return B[name]

        def blk_bufs(prefix, t0, t1):
            return [buf(f"{prefix}{e}") for e in range(t0 // 512, (t1 - 1) // 512 + 1)]

        class Pool:
            def __init__(self, name, tensors, bufs=None):
                self.t = tensors
                self.b = bufs if bufs is not None else [buf(f"{name}{i}") for i in range(len(tensors))]
                self.i = 0

            def next(self):
                k = self.i % len(self.t)
                self.i += 1
                return self.t[k], self.b[k]

        pZ = Pool("psA", psA)
        pS = Pool("psS", psS)
        pO = Pool("psO", psO)
        pA = Pool("psW", psA + psS + psO, pZ.b + pS.b + pO.b)
        pT = Pool("psT", psT)
        p_xt = Pool("xt", xt)
        p_xn = Pool("xn", xn)
        p_stat = Pool("stat", stat)
        p_sq = Pool("sq", sq)
        p_sd = Pool("sd", sd)
        p_qbn = Pool("qbn", qbn)
        p_cst = Pool("cst", cst)
        p_PT = Pool("PT", PT)
        p_sz = Pool("sz", szb)
        p_rd = Pool("rd", rdb)
        p_ex = Pool("ex", exb)
        p_rt = Pool("rt", rtb)
        p_w = Pool("wbuf", wbuf)
        b_const = buf("const")
        b_Et = buf("Et")
        b_ones = buf("ones")
        b_sga = [buf("sga0"), buf("sga1")]
        b_sgb = [buf("sgb0"), buf("sgb1")]
        b_m1, b_m2 = buf("m1"), buf("m2")
        b_QaT, b_QbT = buf("QaT"), buf("QbT")
        b_yaT2 = [[buf(f"yaT{k}_{i}") for i in range(4)] for k in range(2)]
        b_ybT2 = [[buf(f"ybT{k}_{i}") for i in range(4)] for k in range(2)]
        b_mT = [buf(f"mT{i}") for i in range(KC)]
        b_wscr = [buf(f"wscr{i}") for i in range(NTILES)]
        b_setup = buf("setup")

        dma_rr = [0]

        def dma(out, in_, reads, writes, owner, queue="sp", lag=0.0):
            return R.op(queue, lambda e, o=out, i=in_: e.dma_start(out=o, in_=i), reads=reads, writes=writes, dma_owner=owner, lag=lag)

        R.t = -100.0
        dma(cbf[:], cbf_d[:, :], [], [b_const], b_const)
        dma(swam[:], swam_d[:, :], [], [b_const], b_const)
        dma(gbc[:], gbc_d[:, :], [], [b_const], b_const)
        dma(gains[:], gains_d[:, :], [], [b_const], b_const)
        dma(esink[:], sink_d[:, :], [], [b_const], b_const)
        dma(valid[:], valid_d[:, :], [], [b_const], b_const)
        R.op("dve", lambda e: e.memset(epsb[:], EPS), writes=[b_const])
        R.op("dve", lambda e: e.memset(cpow[:, 0:1], -0.5), writes=[b_const])
        R.op("dve", lambda e: e.memset(cpow[:, 1:2], -1.0), writes=[b_const])
        R.op("dve", lambda e: e.memset(cpow[:, 2:3], 1.0), writes=[b_const])
        R.op("act", lambda e: e.activation(out=esink[:], in_=esink[:], func=AF.Exp), reads=[b_const], writes=[b_const])
        R.op("pool", lambda e: e.memset(Va[:], 1.0), writes=[b_ones])
        R.op("pool", lambda e: e.memset(Vb[:], 1.0), writes=[b_ones])
        b_st = [buf("stage0"), buf("stage1")]
        stg = [arena[:, 0:2048], arena[:, 2048:4096]]
        NE = NVAR * 64
        dma(stg[1][:, 0:NE], Mk_d[:, :], [], [b_st[1]], b_st[1])
        for h in range(8):
            dma(stg[0][:, 0:NE], Gt_d[:, h * NE:(h + 1) * NE], [], [b_st[0]], b_st[0])
            R.op("act", lambda e: e.activation(out=stg[0][:, 0:NE], in_=stg[0][:, 0:NE], func=AF.Exp),
                 reads=[b_st[0]], writes=[b_st[0]])
            R.op("dve", lambda e, h=h: e.tensor_tensor(out=Et[:, h].rearrange("p v c -> p (v c)"), in0=stg[0][:, 0:NE],
                                                       in1=stg[1][:, 0:NE], op=ALU.mult),
                 reads=[b_st[0], b_st[1]], writes=[b_Et])

        w_in_r = w_in.rearrange("(kc p) c -> p kc c", p=128)
        w_o_r = w_o.rearrange("(kc p) c -> p kc c", p=128)
        w_oa_r = w_oa.rearrange("(kc p) c -> p kc c", p=128)
        w_ob_r = w_ob.rearrange("(kc p) c -> p kc c", p=128)

        def tile_srcs(t):
            def win(c0, n=256, dst0=0):
                return [(("k8", dst0, n), w_in_r[:, :, c0:c0 + n])]
            if t in T_KA:
                return win(C_KA + 256 * T_KA.index(t))
            if t == T_KBVB:
                return win(C_KB)
            if t in T_VA:
                return win(C_VA + 256 * T_VA.index(t))
            if t in T_QA:
                return win(C_QA + 256 * T_QA.index(t))
            if t in T_QB:
                i = T_QB.index(t)
                out = []
                for cc in range(2):
                    m = 2 * i + cc
                    out.append((("k8", cc * 128, 64), w_in_r[:, :, C_QB + 64 * m:C_QB + 64 * m + 64]))
                    out.append((("k8", cc * 128 + 64, 64), w_in_r[:, :, C_QB + 64 * (4 + m):C_QB + 64 * (4 + m) + 64]))
                return out
            if t in T_ZA:
                return win(C_ZA + 256 * T_ZA.index(t))
            if t in T_ZB:
                return win(C_ZB + 256 * T_ZB.index(t))
            if t in T_GA:
                return win(C_GA + 256 * T_GA.index(t))
            if t in T_GB:
                return win(C_GB + 256 * T_GB.index(t))
            if t in T_WAB:
                i = T_WAB.index(t)
                return [(("ab", 0), w_oa_r[:, :, 256 * i:256 * i + 256]), (("ab", 1), w_ob_r[:, :, 256 * i:256 * i + 256])]
            if t in T_WO:
                i = T_WO.index(t)
                return [(("k8", 0, 256), w_o_r[:, :, 256 * i:256 * i + 256])]
            raise ValueError

        cvt_eng = ["dve", "act"]
        b_cv = [buf("cv0"), buf("cv1")]
        for t in range(NTILES):
            s = t % 2
            st = stg[s]
            ckey = -100.0 if t < 5 else 2.0 + 1.4 * (t - 5)
            R.t = ckey - 1.4 if t >= 7 else ckey
            for (lay, src) in tile_srcs(t):
                if lay[0] == "k8":
                    dst = st.rearrange("p (k c) -> p k c", k=8)[:, :, lay[1]:lay[1] + lay[2]]
                else:
                    dst = st.rearrange("p (a k c) -> p a k c", a=2, k=4)[:, lay[1]]
                dma(dst, src, [], [b_st[s]], b_st[s])
            R.t = ckey
            wt = mT[:, 4 * s:4 * s + 4, :].rearrange("p k t -> p (k t)")
            wb = b_cv[s]
            ce = cvt_eng[t % 2]
            if ce == "act":
                R.op("act", lambda e, wt=wt, st=st: e.activation(out=wt, in_=st, func=AF.Copy), reads=[b_st[s]], writes=[wb])
            else:
                R.op(ce, lambda e, wt=wt, st=st: e.tensor_copy(out=wt, in_=st), reads=[b_st[s]], writes=[wb])
            dma(wscr[t], wt, [wb], [b_wscr[t]], wb)

        setup_bufs = [b_const, b_Et, b_ones, b_st[0], b_st[1]] + b_cv + b_wscr
        for e in ("pe", "act", "dve", "pool", "sp"):
            pass
        R.t = 2.0 + 1.4 * (NTILES - 5) + 0.5
        bar_tok = R.op("sp", lambda e: e.nop(), reads=setup_bufs, writes=[b_setup])
        arena_bufs = p_ex.b + b_sga + b_sgb + [b_m1, b_m2] + b_mT
        for bb in arena_bufs:
            bb.writers = [bar_tok]

        wseq = wseq_in
        wcalls = []
        wstate = {"issued": 0, "used": 0, "slots": []}

        def w_issue(t):
            wt, wb = p_w.next()
            dma(wt[:], wscr[t], [b_wscr[t]], [wb], wb)
            wstate["slots"].append((wt, wb, t))
            wstate["issued"] += 1

        def w_next(expect):
            wcalls.append(expect)
            if wseq is None:
                w_issue(expect)
            else:
                while wstate["issued"] < min(len(wseq), wstate["used"] + 2):
                    w_issue(wseq[wstate["issued"]])
            wt, wb, t = wstate["slots"][wstate["used"]]
            assert t == expect, (t, expect)
            wstate["used"] += 1
            return wt, wb

        clk = [0.0]

        def tick(d=1.0):
            clk[0] += d
            R.t = clk[0]
            return clk[0]

        def proj_fm(wt, wb, col0, hsl, hb, pst, psb, lag=0.0):
            wv = wt.rearrange("p (k c) -> p k c", k=8)
            for kc in range(KC):
                R.op("pe", lambda e, kc=kc: e.matmul(pst[:], lhsT=wv[:, kc, col0:col0 + 128], rhs=hsl(kc),
                                                   start=(kc == 0), stop=(kc == KC - 1)),
                     reads=[wb] + hb, writes=[psb], lag=lag)

        L1 = 1.5
        XLAG = -2.0

        def qknorm(pst, psb, gcol, out_ap, out_bufs, pool):
            sqt, sqb = p_sq.next()
            R.op("act", lambda e: e.activation(out=sqt[:], in_=pst[:], func=AF.Square), reads=[psb], writes=[sqb])
            ms, msb = pool.next()
            R.op("pe", lambda e: e.matmul(ms[:], lhsT=BDm, rhs=sqt[:], start=True, stop=True), reads=[sqb, b_const], writes=[msb], lag=L1)
            sdt, sdb = p_sd.next()
            R.op("act", lambda e: e.activation(out=sdt[:], in_=ms[:], func=AF.Ln, bias=epsb[:, 0:1]), reads=[msb, b_const],
                 writes=[sdb], lag=L1)
            R.op("act", lambda e: e.activation(out=sdt[:], in_=sdt[:], func=AF.Exp, scale=-0.5), reads=[sdb], writes=[sdb], lag=L1)
            R.op("dve", lambda e: e.scalar_tensor_tensor(out=out_ap, in0=pst[:], scalar=gains[:, gcol:gcol + 1], in1=sdt[:],
                                                         op0=ALU.mult, op1=ALU.mult),
                 reads=[psb, sdb, b_const], writes=out_bufs, lag=L1)

        def rotary(qn_t, qn_b, cs_t, cs_b, out_ap, out_bufs, pool):
            rp, rpb_ = pool.next()
            R.op("pe", lambda e: e.matmul(rp[:], lhsT=Perm, rhs=qn_t[:], start=True, stop=True), reads=[qn_b, b_const], writes=[rpb_], lag=L1 + 1)
            t1, t1b = p_rt.next()
            t2, t2b = p_rt.next()
            R.op("pool", lambda e: e.tensor_tensor(out=t1, in0=qn_t[:], in1=cs_t[:, 0, :], op=ALU.mult), reads=[qn_b, cs_b], writes=[t1b], lag=L1 + 1)
            R.op("dve", lambda e: e.tensor_tensor(out=t2, in0=rp[:], in1=cs_t[:, 1, :], op=ALU.mult), reads=[rpb_, cs_b], writes=[t2b], lag=L1 + 1)
            R.op("pool", lambda e: e.tensor_tensor(out=out_ap, in0=t1, in1=t2, op=ALU.add), reads=[t1b, t2b], writes=out_bufs, lag=L1 + 1)

        def xnorm_tile(s, e_, tt):
            tok = e_ * 512 + tt * 128
            hbuf = buf(f"hT{e_}")
            xtile, xb = p_xt.next()
            dma(xtile[:], xs[s, tok:tok + 128, :], [], [xb], xb, lag=XLAG)
            stt, stb = p_stat.next()
            xnt, xnb = p_xn.next()
            R.op("dve", lambda e: e.memset(stt[:], 0.0), writes=[stb])
            R.op("act", lambda e: e.activation(out=xnt[:], in_=xtile[:], func=AF.Square, accum_out=stt[:, 0:1]),
                 reads=[xb], writes=[xnb, stb])
            R.op("act", lambda e: e.activation(out=stt[:, 1:2], in_=stt[:, 0:1], func=AF.Ln, bias=epsb[:, 0:1], scale=1.0 / D),
                 reads=[stb, b_const], writes=[stb])
            R.op("act", lambda e: e.activation(out=stt[:, 2:3], in_=stt[:, 1:2], func=AF.Exp, scale=-0.5), reads=[stb], writes=[stb])
            R.op("dve", lambda e: e.scalar_tensor_tensor(out=xnt[:], in0=xtile[:], scalar=stt[:, 2:3], in1=gbc[:], op0=ALU.mult, op1=ALU.mult),
                 reads=[xb, stb, b_const], writes=[xnb])
            tp, tpb = pT.next()
            for kc in range(KC):
                R.op("pe", lambda e, kc=kc: e.transpose(tp[:, kc * 128:(kc + 1) * 128], xnt[:, kc * 128:(kc + 1) * 128], ident),
                     reads=[xnb, b_const], writes=[tpb], lag=L1)
            R.op("dve", lambda e: e.tensor_copy(out=hT[:, :, tok:tok + 128], in_=tp[:].rearrange("p (k t) -> p k t", k=KC)),
                 reads=[tpb], writes=[hbuf], lag=L1)

        def kv_items(s, e_, pool, xn_next):
            t0 = e_ * 512
            hbuf = buf(f"hT{e_}")
            hsl = lambda kc: hT[:, kc, t0:t0 + 512]
            tick()
            cst_t, cst_b = p_cst.next()
            dma(cst_t[:], cs_d[s, :, :, t0:t0 + 512], [], [cst_b], cst_b)
            for i in range(2):
                for cc in range(2):
                    wt, wb = w_next(T_KA[i])
                    c = 2 * i + cc
                    tick()
                    if xn_next:
                        xnorm_tile(s, e_ + 1, c)
                    pst, psb = pool.next()
                    proj_fm(wt, wb, cc * 128, hsl, [hbuf], pst, psb)
                    qknorm(pst, psb, 1, KaT[:, c, t0:t0 + 512], [buf(f"KaT{e_}")], pool)
                    yield
            tick()
            wt, wb = w_next(T_KBVB)
            pst, psb = pool.next()
            proj_fm(wt, wb, 0, hsl, [hbuf], pst, psb)
            qt, qb_ = p_qbn.next()
            qknorm(pst, psb, 3, qt[:], [qb_], pool)
            rotary(qt, qb_, cst_t, cst_b, KbT[:, t0:t0 + 512], [buf(f"KbT{e_}")], pool)
            wv = wt.rearrange("p (k c) -> p k c", k=8)
            tick(3)
            for tt in range(4):
                tick()
                tok = t0 + tt * 128
                pst, psb = pool.next()
                for kc in range(KC):
                    R.op("pe", lambda e, kc=kc, pst=pst, tok=tok, wv=wv: e.matmul(pst[:, 0:128], lhsT=hT[:, kc, tok:tok + 128],
                                                                                 rhs=wv[:, kc, 128:256], start=(kc == 0),
                                                                                 stop=(kc == KC - 1)),
                         reads=[wb, hbuf], writes=[psb])
                ti = tok // 128
                R.op("dve", lambda e, pst=pst, ti=ti: e.tensor_copy(
                    out=Vb[:, ti, 64:320].rearrange("p (a c) -> p a c", a=2)[:, :, 0:64],
                    in_=pst[:, 0:128].rearrange("p (a c) -> p a c", a=2)),
                    reads=[psb, b_ones], writes=[buf(f"Vb{e_}")])
            yield
            tick()
            wts = [w_next(T_VA[0]), w_next(T_VA[1])]
            for tt in range(4):
                tick()
                tok = t0 + tt * 128
                pst, psb = pool.next()
                for i in range(2):
                    wv = wts[i][0].rearrange("p (k c) -> p k c", k=8)
                    for kc in range(KC):
                        R.op("pe", lambda e, kc=kc, pst=pst, tok=tok, wv=wv, i=i: e.matmul(
                            pst[:, i * 256:(i + 1) * 256], lhsT=hT[:, kc, tok:tok + 128], rhs=wv[:, kc, :],
                            start=(kc == 0), stop=(kc == KC - 1)), reads=[wts[i][1], hbuf], writes=[psb])
                ti = tok // 128
                R.op("dve", lambda e, pst=pst, ti=ti: e.tensor_copy(
                    out=Va[:, ti, :].rearrange("p (hp a c) -> p hp a c", hp=4, a=3)[:, :, 0, :],
                    in_=pst[:].rearrange("p (hp a c) -> p hp a c", hp=4, a=2)[:, :, 0, :]),
                    reads=[psb, b_ones], writes=[buf(f"Va{e_}")])
                R.op("pool" if False else "act", lambda e, pst=pst, ti=ti: e.activation(
                    out=Va[:, ti, :].rearrange("p (hp a c) -> p hp a c", hp=4, a=3)[:, :, 2, :],
                    in_=pst[:].rearrange("p (hp a c) -> p hp a c", hp=4, a=2)[:, :, 1, :], func=AF.Copy),
                    reads=[psb, b_ones], writes=[buf(f"Va{e_}")])
            yield

        def run_all(gen):
            for _ in gen:
                pass

        def qproj(s, b_):
            q0 = 256 + 512 * b_
            hb = blk_bufs("hT", q0, q0 + 512)
            hsl = lambda kc: hT[:, kc, q0:q0 + 512]
            tick()
            cst_t, cst_b = p_cst.next()
            dma(cst_t[:], cs_d[s, :, :, q0:q0 + 512], [], [cst_b], cst_b)
            for i in range(2):
                wt, wb = w_next(T_QA[i])
                for cc in range(2):
                    c = 2 * i + cc
                    tick()
                    pst, psb = pA.next()
                    proj_fm(wt, wb, cc * 128, hsl, hb, pst, psb)
                    qknorm(pst, psb, 0, QaT[:, c, :], [b_QaT], pA)
            for i in range(2):
                wt, wb = w_next(T_QB[i])
                for cc in range(2):
                    m = 2 * i + cc
                    tick()
                    pst, psb = pA.next()
                    proj_fm(wt, wb, cc * 128, hsl, hb, pst, psb)
                    qt, qb_ = p_qbn.next()
                    qknorm(pst, psb, 2, qt[:], [qb_], pA)
                    rotary(qt, qb_, cst_t, cst_b, QbT[:, m, :], [b_QbT], pA)
            tick(3)

        def keep_warm(n):
            for _ in range(n):
                R.op("pe", lambda e: e.matmul(psTf[:, 0:384], lhsT=ident, rhs=cbf[:, 0:384], start=True, stop=True),
                     reads=[b_const], writes=[pT.b[0]])

        def attention(s, b_, filler):
            vbase = s * NVALID
            q0 = 256 + 512 * b_
            hb = blk_bufs("hT", q0, q0 + 512)
            hsl = lambda kc: hT[:, kc, q0:q0 + 512]
            yaT = yaT2[b_ % 2]
            ybT = ybT2[b_ % 2]
            b_yaT = b_yaT2[b_ % 2]
            b_ybT = b_ybT2[b_ % 2]

            def zproj(wt, wb, cc):
                pst, psb = pZ.next()
                proj_fm(wt, wb, cc * 128, hsl, hb, pst, psb)
                szt, szb_ = p_sz.next()
                ZL = 3.0
                R.op("act", lambda e: e.activation(out=szt[:], in_=pst[:], func=AF.Exp, scale=-1.0), reads=[psb], writes=[szb_], lag=ZL)
                R.op("act", lambda e: e.activation(out=szt[:], in_=szt[:], func=AF.Ln, bias=cpow[:, 2:3]), reads=[szb_, b_const], writes=[szb_], lag=ZL)
                R.op("act", lambda e: e.activation(out=szt[:], in_=szt[:], func=AF.Exp, scale=-1.0), reads=[szb_], writes=[szb_], lag=ZL)
                R.op("dve", lambda e: e.tensor_tensor(out=szt[:], in0=szt[:], in1=pst[:], op=ALU.mult),
                     reads=[szb_, psb], writes=[szb_], lag=ZL)
                return szt, szb_

            LAG = 6.0

            def row_groups(kap):
                bnd = None
                interior = []
                for rr in range(8):
                    r = 8 * b_ + rr
                    if r < 4:
                        if kap <= 5:
                            bnd = (0, 3)
                    elif r >= RS - 4:
                        if kap >= 2:
                            bnd = (4, 7)
                    else:
                        if 2 * kap - 7 <= rr <= 2 * kap + 1:
                            interior.append(rr)
                ig = (interior[0], interior[-1]) if interior else None
                return bnd, ig

            sz_next = None
            for i in range(2):
                for cc in range(2):
                    hp = 2 * i + cc
                    tick()
                    if sz_next is None:
                        wt, wb = w_next(T_ZA[i])
                        sz_next = zproj(wt, wb, cc)
                    szt, szb_ = sz_next
                    obank = [pO.next(), pO.next()]
                    for hh in range(2):
                        if hh == 1 and hp < 3:
                            wt, wb = w_next(T_ZA[(hp + 1) // 2])
                            sz_next = zproj(wt, wb, (hp + 1) % 2)
                        h = 2 * hp + hh
                        pb = 64 * hh
                        ot, ob = obank[hh]
                        first_pv = True
                        for kap in range(8):
                            tick()
                            bg, ig = row_groups(kap)
                            groups = [g_ for g_ in (bg, ig) if g_ is not None]
                            if not groups:
                                continue
                            ra = min(g_[0] for g_ in groups)
                            rb = max(g_[1] for g_ in groups)
                            nr = rb - ra + 1
                            assert sum(g_[1] - g_[0] + 1 for g_ in groups) == nr
                            kt = 512 * b_ + 128 * kap
                            ti = kt // 128
                            kb_ = blk_bufs("KaT", kt, kt + 128)
                            vb_ = blk_bufs("Va", kt, kt + 128)
                            keep_warm(5)
                            st_, stb_ = pS.next()
                            R.op("pe", lambda e, st_=st_, pb=pb, kt=kt, hp=hp, ra=ra, rb=rb, nr=nr: e.matmul(
                                st_[:, 0:64 * nr], lhsT=KaT[pb:pb + 64, hp, kt:kt + 128],
                                rhs=QaT[pb:pb + 64, hp, ra * 64:(rb + 1) * 64], start=True, stop=True),
                                reads=kb_ + [b_QaT], writes=[stb_])
                            ext_, exb_ = p_ex.next()
                            R.op("act", lambda e, st_=st_, ext_=ext_, nr=nr: e.activation(out=ext_[:, 0:nr * 64], in_=st_[:, 0:nr * 64],
                                                                                         func=AF.Exp, scale=0.125),
                                 reads=[stb_], writes=[exb_])
                            ptt, ptb = p_PT.next()
                            for (ga, gb_) in groups:
                                gn = gb_ - ga + 1
                                c0 = (ga - ra) * 64
                                dr0 = 2 * kap + 3 - ga
                                ex3 = ext_[:, c0:c0 + gn * 64].rearrange("p (j c) -> p j c", j=gn)
                                pt3 = ptt[:, c0:c0 + gn * 64].rearrange("p (j c) -> p j c", j=gn)
                                if (ga, gb_) == bg:
                                    v0 = 13 - dr0
                                    ev = Et[:, h, v0:v0 + gn, :]
                                    R.op("pool", lambda e, ex3=ex3, ev=ev: e.tensor_tensor(out=ex3, in0=ex3, in1=ev, op=ALU.mult),
                                         reads=[exb_, b_Et], writes=[exb_])
                                    kp = kap if ga == 0 else kap - 2
                                    vv = valid[:, vbase:vbase + 48].rearrange("p (r k) -> p r k", k=6)[:, ga:ga + gn, kp]
                                    vv = vv.unsqueeze(2).to_broadcast([128, gn, 64])
                                    R.op("dve", lambda e, pt3=pt3, ex3=ex3, vv=vv: e.tensor_tensor(out=pt3, in0=ex3, in1=vv, op=ALU.mult),
                                         reads=[exb_, b_const], writes=[ptb])
                                else:
                                    assert 2 <= dr0 - (gn - 1) and dr0 <= 10, (dr0, gn)
                                    v0 = 14 + 10 - dr0
                                    ev = Et[:, h, v0:v0 + gn, :]
                                    R.op("dve" if (kap % 2 == 0) else "pool",
                                         lambda e, pt3=pt3, ex3=ex3, ev=ev: e.tensor_tensor(out=pt3, in0=ex3, in1=ev, op=ALU.mult),
                                         reads=[exb_, b_Et], writes=[ptb])
                            R.op("pe", lambda e, ot=ot, ti=ti, hp=hp, hh=hh, ptt=ptt, ra=ra, rb=rb, nr=nr, fp=first_pv: e.matmul(
                                ot[:, ra * 64:(rb + 1) * 64], lhsT=Va[:, ti, hp * 192 + hh * 64:hp * 192 + hh * 64 + 128],
                                rhs=ptt[:, 0:64 * nr], start=fp, stop=False, skip_group_check=True),
                                reads=vb_ + [ptb, b_ones], writes=[ob], lag=LAG)
                            first_pv = False
                        po = 64 - pb
                        rdt, rdb_ = p_rd.next()
                        R.op("act", lambda e, ot=ot, rdt=rdt, pb=pb, po=po: e.activation(out=rdt[pb:pb + 64, :], in_=ot[po:po + 64, :],
                                                                                       func=AF.Ln),
                             reads=[ob], writes=[rdb_], lag=LAG + 0.5)
                        R.op("act", lambda e, rdt=rdt, pb=pb: e.activation(out=rdt[pb:pb + 64, :], in_=rdt[pb:pb + 64, :], func=AF.Exp, scale=-1.0),
                             reads=[rdb_], writes=[rdb_], lag=LAG + 0.5)
                        R.op("pool", lambda e, rdt=rdt, szt=szt, pb=pb: e.tensor_tensor(out=rdt[pb:pb + 64, :], in0=rdt[pb:pb + 64, :],
                                                                                       in1=szt[pb:pb + 64, :], op=ALU.mult),
                             reads=[rdb_, szb_], writes=[rdb_], lag=LAG + 0.5)
                        R.op("dve", lambda e, ot=ot, rdt=rdt, pb=pb, hp=hp: e.tensor_tensor(
                            out=yaT[pb:pb + 64, hp, :], in0=ot[pb:pb + 64, :], in1=rdt[pb:pb + 64, :], op=ALU.mult),
                            reads=[ob, rdb_], writes=[b_yaT[hp]], lag=LAG + 0.5)
                        filler()

            for g in range(2):
                tick(8)
                wt, wb = w_next(T_ZB[g])
                sz2 = [zproj(wt, wb, 0), zproj(wt, wb, 1)]
                pb = 64 * g
                for n_ in range(4):
                    tick(2)
                    qt0 = q0 + 128 * n_
                    pts = []
                    keep_warm(5)
                    for dl in (-1, 0, 1):
                        kt0 = qt0 + 128 * dl
                        st_, stb_ = pS.next()
                        R.op("pe", lambda e, st_=st_, kt0=kt0, pb=pb, n_=n_: e.matmul(
                            st_[:].rearrange("p (m q) -> p m q", m=4), lhsT=KbT[pb:pb + 64, kt0:kt0 + 128],
                            rhs=QbT[pb:pb + 64, :, n_ * 128:(n_ + 1) * 128], start=True, stop=True),
                            reads=blk_bufs("KbT", kt0, kt0 + 128) + [b_QbT], writes=[stb_])
                        ptt, ptb = p_PT.next()
                        if dl == 0:
                            R.op("act", lambda e, st_=st_, ptt=ptt: e.activation(out=ptt[:], in_=st_[:], func=AF.Exp, scale=0.125),
                                 reads=[stb_], writes=[ptb])
                        else:
                            ext_, exb_ = p_ex.next()
                            R.op("act", lambda e, st_=st_, ext_=ext_: e.activation(out=ext_, in_=st_[:], func=AF.Exp, scale=0.125),
                                 reads=[stb_], writes=[exb_])
                            mk = swam[:, 0:128] if dl == -1 else swam[:, 128:256]
                            mk4 = mk.unsqueeze(1).to_broadcast([128, 4, 128])
                            edge = (dl == -1 and b_ == 0 and n_ == 0) or (dl == 1 and b_ == NQB - 1 and n_ == 3)
                            ex3 = ext_.rearrange("p (m q) -> p m q", m=4)
                            pt3 = ptt[:].rearrange("p (m q) -> p m q", m=4)
                            if edge:
                                vc = vbase + 48 + (0 if dl == -1 else 1)
                                R.op("dve", lambda e, ex3=ex3, pt3=pt3, mk4=mk4, vc=vc: e.scalar_tensor_tensor(
                                    out=pt3, in0=ex3, scalar=valid[:, vc:vc + 1], in1=mk4, op0=ALU.mult, op1=ALU.mult),
                                    reads=[exb_, b_const], writes=[ptb])
                            else:
                                R.op("pool" if dl == -1 else "dve",
                                     lambda e, ex3=ex3, pt3=pt3, mk4=mk4: e.tensor_tensor(out=pt3, in0=ex3, in1=mk4, op=ALU.mult),
                                     reads=[exb_, b_const], writes=[ptb])
                        pts.append((ptt, ptb, kt0))
                    ot, ob = pO.next()
                    for par in range(2):
                        vcol = g * 128 + 64 if par == 0 else g * 128
                        for k_, (ptt, ptb, kt0) in enumerate(pts):
                            ti = kt0 // 128
                            rhs = ptt[:].rearrange("p (a b q) -> p a b q", a=2, b=2)[:, :, par, :]
                            R.op("pe", lambda e, ot=ot, ti=ti, vcol=vcol, rhs=rhs, par=par, k_=k_: e.matmul(
                                ot[:, par * 256:(par + 1) * 256].rearrange("p (a q) -> p a q", a=2), lhsT=Vb[:, ti, vcol:vcol + 128],
                                rhs=rhs, start=(k_ == 0), stop=(k_ == 2)),
                                reads=blk_bufs("Vb", kt0, kt0 + 128) + [ptb, b_ones], writes=[ob], lag=2.5)
                    for par in range(2):
                        pbo = 64 * par
                        pde = 64 - pbo
                        rdt, rdb_ = p_rd.next()
                        for a in range(2):
                            h = 4 * g + 2 * a + par
                            R.op("act", lambda e, ot=ot, rdt=rdt, pbo=pbo, pde=pde, par=par, a=a, h=h: e.activation(
                                out=rdt[pbo:pbo + 64, a * 128:(a + 1) * 128],
                                in_=ot[pde:pde + 64, par * 256 + a * 128:par * 256 + (a + 1) * 128],
                                func=AF.Ln, bias=esink[pbo:pbo + 64, h:h + 1]), reads=[ob, b_const], writes=[rdb_], lag=3.0)
                        R.op("act", lambda e, rdt=rdt, pbo=pbo: e.activation(out=rdt[pbo:pbo + 64, 0:256], in_=rdt[pbo:pbo + 64, 0:256],
                                                                            func=AF.Exp, scale=-1.0),
                             reads=[rdb_], writes=[rdb_], lag=3.0)
                        for a in range(2):
                            szt, szb_ = sz2[a]
                            R.op("pool", lambda e, rdt=rdt, szt=szt, pbo=pbo, a=a, n_=n_: e.tensor_tensor(
                                out=rdt[pbo:pbo + 64, a * 128:(a + 1) * 128], in0=rdt[pbo:pbo + 64, a * 128:(a + 1) * 128],
                                in1=szt[pbo:pbo + 64, n_ * 128:(n_ + 1) * 128], op=ALU.mult),
                                reads=[rdb_, szb_], writes=[rdb_], lag=3.0)
                        R.op("dve", lambda e, ot=ot, rdt=rdt, pbo=pbo, par=par, g=g, n_=n_: e.tensor_tensor(
                            out=ybT[pbo:pbo + 64, 2 * g:2 * g + 2, n_ * 128:(n_ + 1) * 128],
                            in0=ot[pbo:pbo + 64, par * 256:(par + 1) * 256].rearrange("p (a q) -> p a q", a=2),
                            in1=rdt[pbo:pbo + 64, 0:256].rearrange("p (a q) -> p a q", a=2), op=ALU.mult),
                            reads=[ob, rdb_], writes=[b_ybT[2 * g], b_ybT[2 * g + 1]], lag=3.0)
                    if n_ % 2 == 1:
                        filler()
            tick(4)

        def merge_items(s, b_, pool):
            q0 = 256 + 512 * b_
            hb = blk_bufs("hT", q0, q0 + 512)
            hsl = lambda kc: hT[:, kc, q0:q0 + 512]
            yaT = yaT2[b_ % 2]
            ybT = ybT2[b_ % 2]
            b_yaT = b_yaT2[b_ % 2]
            b_ybT = b_ybT2[b_ % 2]
            for i in range(4):
                tick()
                wga, wgab = w_next(T_GA[i])
                for cc in range(2):
                    pga, pgab = pool.next()
                    proj_fm(wga, wgab, cc * 128, hsl, hb, pga, pgab)
                    R.op("act", lambda e, pga=pga, cc=cc: e.activation(out=sga[cc], in_=pga[:], func=AF.Exp, scale=-1.0),
                         reads=[pgab], writes=[b_sga[cc]])
                    R.op("act", lambda e, cc=cc: e.activation(out=sga[cc], in_=sga[cc], func=AF.Ln, bias=cpow[:, 2:3]),
                         reads=[b_sga[cc], b_const], writes=[b_sga[cc]])
                    R.op("act", lambda e, cc=cc: e.activation(out=sga[cc], in_=sga[cc], func=AF.Exp, scale=-1.0),
                         reads=[b_sga[cc]], writes=[b_sga[cc]])
                tick()
                wgb, wgbb = w_next(T_GB[i])
                for cc in range(2):
                    pgb, pgbb = pool.next()
                    proj_fm(wgb, wgbb, cc * 128, hsl, hb, pgb, pgbb)
                    R.op("act", lambda e, pgb=pgb, cc=cc: e.activation(out=sgb[cc], in_=pgb[:], func=AF.Exp, scale=-1.0),
                         reads=[pgbb], writes=[b_sgb[cc]])
                    R.op("act", lambda e, cc=cc: e.activation(out=sgb[cc], in_=sgb[cc], func=AF.Ln, bias=cpow[:, 2:3]),
                         reads=[b_sgb[cc], b_const], writes=[b_sgb[cc]])
                    R.op("act", lambda e, cc=cc: e.activation(out=sgb[cc], in_=sgb[cc], func=AF.Exp, scale=-1.0),
                         reads=[b_sgb[cc]], writes=[b_sgb[cc]])
                tick()
                wab, wabb = w_next(T_WAB[i])
                wabv = wab.rearrange("p (a k c) -> p a k c", a=2, k=4)
                for cc in range(2):
                    c = 2 * i + cc
                    pa, pab = pool.next()
                    for kc in range(4):
                        R.op("pe", lambda e, kc=kc, pa=pa, cc=cc, wabv=wabv: e.matmul(pa[:], lhsT=wabv[:, 0, kc, cc * 128:(cc + 1) * 128],
                                                                                     rhs=yaT[:, kc, :], start=(kc == 0), stop=(kc == 3)),
                             reads=[wabb] + b_yaT, writes=[pab])
                    R.op("dve", lambda e, pa=pa, cc=cc: e.tensor_tensor(out=m1b, in0=pa[:], in1=sga[cc], op=ALU.mult),
                         reads=[pab, b_sga[cc]], writes=[b_m1])
                    pb2, pbb = pool.next()
                    for kc in range(4):
                        R.op("pe", lambda e, kc=kc, pb2=pb2, cc=cc, wabv=wabv: e.matmul(pb2[:], lhsT=wabv[:, 1, kc, cc * 128:(cc + 1) * 128],
                                                                                       rhs=ybT[:, kc, :], start=(kc == 0), stop=(kc == 3)),
                             reads=[wabb] + b_ybT, writes=[pbb])
                    R.op("dve", lambda e, pb2=pb2, cc=cc: e.tensor_tensor(out=m2b, in0=pb2[:], in1=sgb[cc], op=ALU.mult),
                         reads=[pbb, b_sgb[cc]], writes=[b_m2])
                    R.op("pool", lambda e, c=c: e.tensor_tensor(out=mT[:, c, :], in0=m1b, in1=m2b, op=ALU.add),
                         reads=[b_m1, b_m2], writes=[b_mT[c]])
                yield
            tick()
            xres = []
            for tt in range(4):
                xtile, xb = p_xt.next()
                dma(xtile[:], xs[s, q0 + tt * 128:q0 + (tt + 1) * 128, :], [], [xb], xb)
                xres.append((xtile, xb))
            for cg in range(4):
                tick()
                wo_t, wo_b = w_next(T_WO[cg])
                wov = wo_t.rearrange("p (k c) -> p k c", k=8)
                for tt in range(4):
                    pst, psb = pool.next()
                    for kc in range(KC):
                        R.op("pe", lambda e, kc=kc, pst=pst, tt=tt, wov=wov: e.matmul(
                            pst[:, 0:256], lhsT=mT[:, kc, tt * 128:(tt + 1) * 128], rhs=wov[:, kc, :],
                            start=(kc == 0), stop=(kc == KC - 1)), reads=[wo_b] + b_mT, writes=[psb])
                    xtile, xb = xres[tt]
                    R.op("dve", lambda e, pst=pst, xtile=xtile, cg=cg: e.tensor_tensor(
                        out=xtile[:, cg * 256:(cg + 1) * 256], in0=pst[:, 0:256], in1=xtile[:, cg * 256:(cg + 1) * 256],
                        op=ALU.add), reads=[psb, xb], writes=[xb])
                yield
            tick()
            for tt in range(4):
                xtile, xb = xres[tt]
                r0 = 512 * b_ + 128 * tt
                tk = dma(y[s, r0:r0 + 128, :], xtile[:], [xb], [], xb)
                R.store_toks.append(tk)
            yield

        clk[0] = 0.0
        for s in range(nslot):
            tick(2)
            for tt in range(4):
                tick()
                xnorm_tile(s, 0, tt)
            tick(2)
            run_all(kv_items(s, 0, pA, True))
            run_all(kv_items(s, 1, pA, True))
            pending = [kv_items(s, 2, pZ, False)]

            def filler():
                while pending:
                    try:
                        next(pending[0])
                        tick(3)
                        return
                    except StopIteration:
                        pending.pop(0)

            def drain():
                while pending:
                    filler()

            for b_ in range(NQB):
                qproj(s, b_)
                attention(s, b_, filler)
                drain()
                if b_ + 1 < NQB:
                    pending.append(merge_items(s, b_, pZ))
                else:
                    run_all(merge_items(s, b_, pA))

        tick(10)

        if dbg and wseq_in is not None:
            allb = list(B.values())
            def dump(name, t, shape, dt):
                o = nc.dram_tensor(name, list(shape), dt, kind="ExternalOutput").ap()
                tk = dma(o, t, allb, [], buf("dbg_" + name))
                R.store_toks.append(tk)
            dump("d_hT", hT[:].rearrange("p k t -> p (k t)"), [128, KC * EXT], BF16)
            dump("d_KaT", KaT[:].rearrange("p k t -> p (k t)"), [128, 4 * EXT], BF16)
            dump("d_KbT", KbT[:], [128, EXT], BF16)
            dump("d_Va", Va[:].rearrange("p k t -> p (k t)"), [128, NT * 768], BF16)
            dump("d_Vb", Vb[:].rearrange("p k t -> p (k t)"), [128, NT * 320], BF16)
            dump("d_QaT", QaT[:].rearrange("p k t -> p (k t)"), [128, 4 * 512], BF16)
            dump("d_QbT", QbT[:].rearrange("p k t -> p (k t)"), [128, 4 * 512], BF16)
            dump("d_yaT", yaT2[(NQB - 1) % 2][:].rearrange("p k t -> p (k t)"), [128, 4 * 512], BF16)
            dump("d_ybT", ybT2[(NQB - 1) % 2][:].rearrange("p k t -> p (k t)"), [128, 4 * 512], BF16)
            dump("d_mT", mT[:].rearrange("p k t -> p (k t)"), [128, KC * 512], BF16)
            dump("d_Et", Et[:].rearrange("p h v c -> p (h v c)"), [128, 8 * NVAR * 64], BF16)

        last = {}
        for t in R.store_toks:
            if id(t.sem) not in last or t.order > last[id(t.sem)].order:
                last[id(t.sem)] = t
        R.op("sp", lambda e: e.nop(), extra=list(last.values()))


        return wcalls

    wseq_real = record(Rec(), None)
    record(R, wseq_real)

    R.finalize()
    for e in R.ENGS:
        R.sems[e].handle = es.enter_context(nc.semaphore(R.sems[e].name))
    for sdm in R.dsems:
        sdm.handle = es.enter_context(nc.semaphore(sdm.name))
    with nc.Block() as block:
        @block.sync
        def _(eng):
            R.emit("sp", eng)

        @block.tensor
        def _(eng):
            R.emit("pe", eng)

        @block.scalar
        def _(eng):
            R.emit("act", eng)

        @block.vector
        def _(eng):
            R.emit("dve", eng)

        @block.gpsimd
        def _(eng):
            R.emit("pool", eng)
    es.close()
    return nc


def _variants():
    v = [(13 - i, 1, 1) for i in range(14)]
    v += [(10, 1, 0)] + [(d, 1, 1) for d in range(9, 2, -1)] + [(2, 0, 1)]
    return v


def _host_constants():
    p = np.arange(128)
    ck = p % 64
    half = p // 64
    cq = np.arange(64)
    cs_ = np.clip(cq - 8, 0, 48)
    col_in = (ck[:, None] >= cs_[None, :]) & (ck[:, None] < cs_[None, :] + 16)
    dc = np.clip(ck[:, None] - cq[None, :], -15, 15) + 15
    var = _variants()
    dr_idx = np.stack([np.clip(d + half, 0, 14) for (d, lo, up) in var], axis=1)
    Mk = np.zeros((128, NVAR, 64), np.float32)
    for vi, (d, lo, up) in enumerate(var):
        hv = np.where(half == 0, lo, up).astype(np.float32)
        Mk[:, vi, :] = col_in.astype(np.float32) * hv[:, None]
    ident = np.eye(128, dtype=np.float32)
    BD = np.zeros((128, 128), np.float32)
    BD[:64, :64] = 1.0 / 64
    BD[64:, 64:] = 1.0 / 64
    partner = np.where((p % 64) < 32, p + 32, p - 32)
    Perm = np.zeros((128, 128), np.float32)
    Perm[partner, p] = 1.0
    cbf = np.concatenate([ident, BD, Perm], axis=1).astype(ml_dtypes.bfloat16)
    j = np.arange(128)[:, None]
    i = np.arange(128)[None, :]
    swam = np.concatenate([(j >= i), (j <= i)], axis=1).astype(np.float32)
    return dict(dr_idx=dr_idx, dc=dc, Mk=Mk.reshape(128, NVAR * 64), cbf=cbf, swam=swam)


def _slot_tables(row0):
    p = np.arange(128)
    half = p // 64
    val = np.zeros((128, NVALID), np.float32)
    for bi in range(8):
        r = bi if bi < 4 else RS - 4 + (bi - 4)
        start_local = -4 if bi < 4 else RS - 8
        R_ = row0 + r
        w0 = min(max(R_ - 4, 0), NROWS - 8)
        for j in range(6):
            kr = row0 + start_local + 2 * j + half
            val[:, bi * 6 + j] = ((kr >= w0) & (kr < w0 + 8)).astype(np.float32)
    val[:, 48] = 1.0 if row0 > 0 else 0.0
    val[:, 49] = 1.0 if row0 + RS < NROWS else 0.0
    pos = (row0 - 4) * GW + np.arange(EXT)
    halfd = 32
    inv = (np.float32(10000.0) ** (-(np.arange(halfd, dtype=np.float32)) / np.float32(halfd))).astype(np.float32)
    ang = pos.astype(np.float32)[None, :] * inv[:, None]
    cos = np.cos(ang).astype(np.float32)
    sin = np.sin(ang).astype(np.float32)
    d = p % 64
    cs = np.zeros((128, 2, EXT), np.float32)
    cs[:, 0, :] = cos[d % 32]
    cs[:, 1, :] = sin[d % 32] * np.where(d < 32, -1.0, 1.0)[:, None].astype(np.float32)
    return val, cs


_PROG = {}


def _make_in_maps(inputs, slot_list_per_core):
    hc = _host_constants()
    x_all = [np.asarray(inputs["x_prompt"], np.float32), np.asarray(inputs["x_sample"], np.float32)]
    seqs = [x_all[0][i] for i in range(x_all[0].shape[0])] + [x_all[1][i] for i in range(x_all[1].shape[0])]
    rpb = np.asarray(inputs["rpb_a"], np.float32)[0]
    Gt = rpb[:, hc["dr_idx"][:, :, None], hc["dc"][:, None, :]]
    Gt = np.ascontiguousarray(np.transpose(Gt, (1, 0, 2, 3))).reshape(128, 8 * NVAR * 64)
    p = np.arange(128)
    gains = np.stack([np.asarray(inputs[k], np.float32)[0][p % 64] for k in ("qn_a", "kn_a", "qn_b", "kn_b")], axis=1)
    common = {
        "w_in": np.ascontiguousarray(np.asarray(inputs["w_in"], np.float32)[0]),
        "w_out_a": np.ascontiguousarray(np.asarray(inputs["w_out_a"], np.float32)[0]),
        "w_out_b": np.ascontiguousarray(np.asarray(inputs["w_out_b"], np.float32)[0]),
        "w_o": np.ascontiguousarray(np.asarray(inputs["w_o"], np.float32)[0]),
        "gbc": np.ascontiguousarray(np.broadcast_to(np.asarray(inputs["norm_g"], np.float32)[0][None, :], (128, D))),
        "gains": np.ascontiguousarray(gains),
        "sinkb": np.ascontiguousarray(np.broadcast_to(np.asarray(inputs["sink_b"], np.float32)[0][None, :], (128, 8))),
        "Gt": Gt, "Mk": hc["Mk"], "cbf": hc["cbf"], "swam": hc["swam"],
    }
    tabs = {}
    in_maps = []
    for slots in slot_list_per_core:
        ns = len(slots)
        xs = np.zeros((ns, EXT, D), np.float32)
        valid = np.zeros((128, ns * NVALID), np.float32)
        cs = np.zeros((ns, 128, 2, EXT), np.float32)
        for si, (sq_, row0) in enumerate(slots):
            lo = (row0 - 4) * GW
            hi = lo + EXT
            a, b = max(lo, 0), min(hi, SEQ)
            xs[si, a - lo:b - lo] = seqs[sq_][a:b]
            if row0 not in tabs:
                tabs[row0] = _slot_tables(row0)
            valid[:, si * NVALID:(si + 1) * NVALID] = tabs[row0][0]
            cs[si] = tabs[row0][1]
        m = dict(common)
        m.update({"xs": xs, "valid": valid, "cs": cs})
        in_maps.append(m)
    return in_maps


def kernel(**inputs):
    nseq_p = np.asarray(inputs["x_prompt"]).shape[0]
    nseq_s = np.asarray(inputs["x_sample"]).shape[0]
    nseq = nseq_p + nseq_s
    qper = NROWS // RS
    all_slots = [(sq_, q * RS) for sq_ in range(nseq) for q in range(qper)]
    assert len(all_slots) == NCORES * NSLOT
    per_core = [all_slots[c * NSLOT:(c + 1) * NSLOT] for c in range(NCORES)]
    in_maps = _make_in_maps(inputs, per_core)
    if "nc" not in _PROG:
        _PROG["nc"] = build_program(NSLOT)
    res = run_bass_kernel_spmd(_PROG["nc"], in_maps, core_ids=list(range(NCORES)))
    outs = [np.zeros((SEQ, D), np.float32) for _ in range(nseq)]
    for c in range(NCORES):
        yc = res.results[c]["y"]
        for si, (sq_, row0) in enumerate(per_core[c]):
            outs[sq_][row0 * GW:(row0 + RS) * GW] = yc[si]
    y_prompt = np.stack(outs[:nseq_p], axis=0)
    y_sample = np.stack(outs[nseq_p:], axis=0)
    return (y_prompt, y_sample)
```

```python
import contextlib
import numpy as np
import ml_dtypes
import concourse.bass as bass
import concourse.mybir as mybir
from concourse.bass_utils import run_bass_kernel_spmd

F32 = mybir.dt.float32
BF16 = mybir.dt.bfloat16
AF = mybir.ActivationFunctionType
ALU = mybir.AluOpType

D = 1024
KC = 8
SEQ = 4096
GW = 64
NROWS = 64
RS = 16
EXTR = RS + 8
EXT = EXTR * GW
QS = RS * GW
NQB = QS // 512
NEB = EXT // 512
NT = EXT // 128
NCORES = 8
NSLOT = (12 * NROWS // RS) // NCORES
NVAR = 23
NVALID = 8 * 6 + 2
EPS = 1e-6

C_QA, C_KA, C_VA, C_ZA, C_QB, C_KB, C_VB, C_ZB, C_GA, C_GB = 0, 512, 1024, 1536, 2048, 2560, 2688, 2816, 3328, 4352

T_KA = [0, 1]
T_KBVB = 2
T_VA = [3, 4]
T_QA = [5, 6]
T_QB = [7, 8]
T_ZA = [9, 10]
T_ZB = [11, 12]
T_GA = [13, 14, 15, 16]
T_GB = [17, 18, 19, 20]
T_WAB = [21, 22, 23, 24]
T_WO = [25, 26, 27, 28]
NTILES = 29


class Sem:
    def __init__(self, name, is_dma):
        self.name = name
        self.is_dma = is_dma
        self.handle = None
        self.count = 0


class Tok:
    __slots__ = ("sem", "order", "value", "op")

    def __init__(self, sem, order, value=None, op=None):
        self.sem = sem
        self.order = order
        self.value = value
        self.op = op


class Buf:
    def __init__(self, name):
        self.name = name
        self.writers = []
        self.readers = []
        self.dsem = None


class Op:
    __slots__ = ("fn", "deps", "tok", "signal", "is_dma", "key", "idx", "eng", "info")

    def __init__(self, fn, deps, tok, is_dma, key, idx, eng):
        self.fn = fn
        self.deps = deps
        self.tok = tok
        self.signal = False
        self.is_dma = is_dma
        self.key = key
        self.idx = idx
        self.eng = eng


class Rec:
    ENGS = ("pe", "act", "dve", "pool", "sp")

    def __init__(self):
        self.ops = {e: [] for e in self.ENGS}
        self.sems = {e: Sem("s_" + e, False) for e in self.ENGS}
        self.dsems = []
        self.store_toks = []
        self.t = 0.0
        self.nrec = 0

    def _merge(self, deps, t):
        k = id(t.sem)
        if k not in deps or t.order > deps[k].order:
            deps[k] = t

    def _trim(self, lst, t):
        for x in lst:
            if x.sem is t.sem and x.order > t.order:
                return
        lst[:] = [x for x in lst if x.sem is not t.sem]
        lst.append(t)

    def op(self, eng, fn, reads=(), writes=(), dma_owner=None, extra=(), lag=0.0):
        deps = {}
        for b in reads:
            for t in b.writers:
                self._merge(deps, t)
        for b in writes:
            for t in b.readers:
                self._merge(deps, t)
            for t in b.writers:
                self._merge(deps, t)
        for t in extra:
            self._merge(deps, t)
        if dma_owner is not None:
            if dma_owner.dsem is None:
                dma_owner.dsem = Sem("d_" + dma_owner.name, True)
                self.dsems.append(dma_owner.dsem)
            s = dma_owner.dsem
            s.count += 16
            tok = Tok(s, (float(s.count), 0), s.count)
            self.nrec += 1
        else:
            self.nrec += 1
            tok = Tok(self.sems[eng], (self.t + lag, self.nrec))
        o = Op(fn, list(deps.values()), tok, dma_owner is not None, self.t + lag, self.nrec, eng)
        tok.op = o
        o.info = ([b.name for b in reads], [b.name for b in writes])
        self.ops[eng].append(o)
        wset = set(id(b) for b in writes)
        for b in writes:
            if b.readers:
                b.writers = [tok]
                b.readers = []
            else:
                self._trim(b.writers, tok)
        for b in reads:
            if id(b) not in wset:
                self._trim(b.readers, tok)
        return tok

    def barrier(self, bufs):
        pass

    def finalize(self):
        for e in self.ENGS:
            self.ops[e].sort(key=lambda o: (o.key, o.idx))
            for o in self.ops[e]:
                for t in o.deps:
                    if t.op is not None:
                        assert (t.op.key, t.op.idx) < (o.key, o.idx), ("non-monotone dep", e, o.key, o.idx, o.info, t.op.eng, t.op.key, t.op.idx, t.op.info)
        for e in self.ENGS:
            for o in self.ops[e]:
                for t in o.deps:
                    if t.op is not None and not t.op.is_dma:
                        if not (e == "pe" and t.sem is self.sems["pe"]):
                            t.op.signal = True
        for e in self.ENGS:
            c = 0
            for o in self.ops[e]:
                if not o.is_dma and o.signal:
                    c += 1
                    o.tok.value = c

    def emit(self, eng_name, eng):
        seen = {}
        own = self.sems[eng_name]
        n_wait = 0
        for o in self.ops[eng_name]:
            for t in o.deps:
                if eng_name == "pe" and t.sem is own:
                    continue
                v = t.value
                assert v is not None
                if seen.get(id(t.sem), 0) >= v:
                    continue
                seen[id(t.sem)] = v
                eng.wait_ge(t.sem.handle, v)
                n_wait += 1
            ins = o.fn(eng)
            if o.is_dma:
                ins.then_inc(o.tok.sem.handle, 16)
            elif o.signal:
                ins.then_inc(own.handle, 1)
        return n_wait


def build_program(nslot=NSLOT, dbg=False):
    nc = bass.Bass("TRN2", target_bir_lowering=False)
    R = Rec()
    es = contextlib.ExitStack()

    def dram_in(name, shape, dt=F32):
        return nc.dram_tensor(name, list(shape), dt, kind="ExternalInput").ap()

    xs = dram_in("xs", [nslot, EXT, D])
    w_in = dram_in("w_in", [D, 5376])
    w_oa = dram_in("w_out_a", [512, D])
    w_ob = dram_in("w_out_b", [512, D])
    w_o = dram_in("w_o", [D, D])
    gbc_d = dram_in("gbc", [128, D])
    gains_d = dram_in("gains", [128, 4])
    sink_d = dram_in("sinkb", [128, 8])
    Gt_d = dram_in("Gt", [128, 8 * NVAR * 64])
    Mk_d = dram_in("Mk", [128, NVAR * 64])
    cbf_d = dram_in("cbf", [128, 3 * 128], BF16)
    swam_d = dram_in("swam", [128, 256])
    valid_d = dram_in("valid", [128, nslot * NVALID])
    cs_d = dram_in("cs", [nslot, 128, 2, EXT])
    y = nc.dram_tensor("y", [nslot, QS, D], F32, kind="ExternalOutput").ap()
    wscr = nc.dram_tensor("wscr", [NTILES, 128, 2048], BF16).ap()

    def sb(name, shape, dt):
        return es.enter_context(nc.sbuf_tensor(name, list(shape), dt))

    def ps(name, shape, dt):
        return es.enter_context(nc.psum_tensor(name, list(shape), dt))

    hT = sb("hT", [128, KC, EXT], BF16)
    KaT = sb("KaT", [128, 4, EXT], BF16)
    KbT = sb("KbT", [128, EXT], BF16)
    Va = sb("Va", [128, NT, 768], BF16)
    Vb = sb("Vb", [128, NT, 320], BF16)
    Et = sb("Et", [128, 8, NVAR, 64], BF16)
    cbf = sb("cbf_s", [128, 3 * 128], BF16)
    ident = cbf[:, 0:128]
    BDm = cbf[:, 128:256]
    Perm = cbf[:, 256:384]
    swam = sb("swam_s", [128, 256], F32)
    gbc = sb("gbc_s", [128, D], F32)
    gains = sb("gains_s", [128, 4], F32)
    esink = sb("esink", [128, 8], F32)
    valid = sb("valid_s", [128, nslot * NVALID], F32)
    epsb = sb("epsb", [128, 1], F32)
    cpow = sb("cpow", [128, 3], F32)
    wbuf = [sb(f"wbuf{i}", [128, 2048], BF16) for i in range(3)]
    arena = sb("arena", [128, 4096], F32)
    xt = [sb(f"xt{i}", [128, D], F32) for i in range(4)]
    xn = [sb(f"xn{i}", [128, D], BF16) for i in range(2)]
    stat = [sb(f"stat{i}", [128, 4], F32) for i in range(2)]
    sq = [sb(f"sq{i}", [128, 512], BF16) for i in range(2)]
    sd = [sb(f"sd{i}", [128, 512], F32) for i in range(2)]
    qbn = [sb(f"qbn{i}", [128, 512], BF16) for i in range(1)]
    cst = [sb(f"cst{i}", [128, 2, 512], F32) for i in range(1)]
    QaT = sb("QaT", [128, 4, 512], BF16)
    QbT = sb("QbT", [128, 4, 512], BF16)
    PT = [sb(f"PT{i}", [128, 512], BF16) for i in range(6)]
    szb = [sb(f"sz{i}", [128, 512], F32) for i in range(2)]
    rdb = [sb(f"rd{i}", [128, 512], F32) for i in range(2)]
    yaT2 = [sb(f"yaT{i}", [128, 4, 512], BF16) for i in range(2)]
    ybT2 = [sb(f"ybT{i}", [128, 4, 512], BF16) for i in range(2)]
    mT = sb("mT", [128, KC, 512], BF16)
    ex2 = sb("ex2", [128, 512], F32)
    exb = [arena[:, 0:512], arena[:, 512:1024], ex2[:]]
    rt0 = sb("rt0", [128, 512], F32)
    rt1 = sb("rt1", [128, 512], F32)
    rtb = [rt0[:], rt1[:]]
    sga = [arena[:, 2048:2560], arena[:, 2560:3072]]
    sgb = [arena[:, 3072:3584], arena[:, 3584:4096]]
    m1b = arena[:, 1024:1536]
    m2b = arena[:, 1536:2048]

    psA = [ps(f"psA{i}", [128, 512], F32) for i in range(2)]
    psS = [ps(f"psS{i}", [128, 512], F32) for i in range(3)]
    psO = [ps(f"psO{i}", [128, 512], F32) for i in range(2)]
    psTf = ps("psT0", [128, 512], F32)
    psT = [psTf[:].bitcast(BF16)]

    def record(R, wseq_in):
        B = {}

        def buf(name):
            if name not in B:
                B[name] = Buf(name)
            return B[name]

        def blk_bufs(prefix, t0, t1):
            return [buf(f"{prefix}{e}") for e in range(t0 // 512, (t1 - 1) // 512 + 1)]

        class Pool:
            def __init__(self, name, tensors, bufs=None):
                self.t = tensors
                self.b = bufs if bufs is not None else [buf(f"{name}{i}") for i in range(len(tensors))]
                self.i = 0

            def next(self):
                k = self.i % len(self.t)
                self.i += 1
                return self.t[k], self.b[k]

        pZ = Pool("psA", psA)
        pS = Pool("psS", psS)
        pO = Pool("psO", psO)
        pA = Pool("psW", psA + psS + psO, pZ.b + pS.b + pO.b)
        pT = Pool("psT", psT)
        p_xt = Pool("xt", xt)
        p_xn = Pool("xn", xn)
        p_stat = Pool("stat", stat)
        p_sq = Pool("sq", sq)
        p_sd = Pool("sd", sd)
        p_qbn = Pool("qbn", qbn)
        p_cst = Pool("cst", cst)
        p_PT = Pool("PT", PT)
        p_sz = Pool("sz", szb)
        p_rd = Pool("rd", rdb)
        p_ex = Pool("ex", exb)
        p_rt = Pool("rt", rtb)
        p_w = Pool("wbuf", wbuf)
        b_const = buf("const")
        b_Et = buf("Et")
        b_ones = buf("ones")
        b_sga = [buf("sga0"), buf("sga1")]
        b_sgb = [buf("sgb0"), buf("sgb1")]
        b_m1, b_m2 = buf("m1"), buf("m2")
        b_QaT, b_QbT = buf("QaT"), buf("QbT")
        b_yaT2 = [[buf(f"yaT{k}_{i}") for i in range(4)] for k in range(2)]
        b_ybT2 = [[buf(f"ybT{k}_{i}") for i in range(4)] for k in range(2)]
        b_mT = [buf(f"mT{i}") for i in range(KC)]
        b_wscr = [buf(f"wscr{i}") for i in range(NTILES)]
        b_setup = buf("setup")

        dma_rr = [0]

        def dma(out, in_, reads, writes, owner, queue="sp", lag=0.0):
            return R.op(queue, lambda e, o=out, i=in_: e.dma_start(out=o, in_=i), reads=reads, writes=writes, dma_owner=owner, lag=lag)

        R.t = -100.0
        dma(cbf[:], cbf_d[:, :], [], [b_const], b_const)
        dma(swam[:], swam_d[:, :], [], [b_const], b_const)
        dma(gbc[:], gbc_d[:, :], [], [b_const], b_const)
        dma(gains[:], gains_d[:, :], [], [b_const], b_const)
        dma(esink[:], sink_d[:, :], [], [b_const], b_const)
        dma(valid[:], valid_d[:, :], [], [b_const], b_const)
        R.op("dve", lambda e: e.memset(epsb[:], EPS), writes=[b_const])
        R.op("dve", lambda e: e.memset(cpow[:, 0:1], -0.5), writes=[b_const])
        R.op("dve", lambda e: e.memset(cpow[:, 1:2], -1.0), writes=[b_const])
        R.op("dve", lambda e: e.memset(cpow[:, 2:3], 1.0), writes=[b_const])
        R.op("act", lambda e: e.activation(out=esink[:], in_=esink[:], func=AF.Exp), reads=[b_const], writes=[b_const])
        R.op("pool", lambda e: e.memset(Va[:], 1.0), writes=[b_ones])
        R.op("pool", lambda e: e.memset(Vb[:], 1.0), writes=[b_ones])
        b_st = [buf("stage0"), buf("stage1")]
        stg = [arena[:, 0:2048], arena[:, 2048:4096]]
        NE = NVAR * 64
        dma(stg[1][:, 0:NE], Mk_d[:, :], [], [b_st[1]], b_st[1])
        for h in range(8):
            dma(stg[0][:, 0:NE], Gt_d[:, h * NE:(h + 1) * NE], [], [b_st[0]], b_st[0])
            R.op("act", lambda e: e.activation(out=stg[0][:, 0:NE], in_=stg[0][:, 0:NE], func=AF.Exp),
                 reads=[b_st[0]], writes=[b_st[0]])
            R.op("dve", lambda e, h=h: e.tensor_tensor(out=Et[:, h].rearrange("p v c -> p (v c)"), in0=stg[0][:, 0:NE],
                                                       in1=stg[1][:, 0:NE], op=ALU.mult),
                 reads=[b_st[0], b_st[1]], writes=[b_Et])

        w_in_r = w_in.rearrange("(kc p) c -> p kc c", p=128)
        w_o_r = w_o.rearrange("(kc p) c -> p kc c", p=128)
        w_oa_r = w_oa.rearrange("(kc p) c -> p kc c", p=128)
        w_ob_r = w_ob.rearrange("(kc p) c -> p kc c", p=128)

        def tile_srcs(t):
            def win(c0, n=256, dst0=0):
                return [(("k8", dst0, n), w_in_r[:, :, c0:c0 + n])]
            if t in T_KA:
                return win(C_KA + 256 * T_KA.index(t))
            if t == T_KBVB:
                return win(C_KB)
            if t in T_VA:
                return win(C_VA + 256 * T_VA.index(t))
            if t in T_QA:
                return win(C_QA + 256 * T_QA.index(t))
            if t in T_QB:
                i = T_QB.index(t)
                out = []
                for cc in range(2):
                    m = 2 * i + cc
                    out.append((("k8", cc * 128, 64), w_in_r[:, :, C_QB + 64 * m:C_QB + 64 * m + 64]))
                    out.append((("k8", cc * 128 + 64, 64), w_in_r[:, :, C_QB + 64 * (4 + m):C_QB + 64 * (4 + m) + 64]))
                return out
            if t in T_ZA:
                return win(C_ZA + 256 * T_ZA.index(t))
            if t in T_ZB:
                return win(C_ZB + 256 * T_ZB.index(t))
            if t in T_GA:
                return win(C_GA + 256 * T_GA.index(t))
            if t in T_GB:
                return win(C_GB + 256 * T_GB.index(t))
            if t in T_WAB:
                i = T_WAB.index(t)
                return [(("ab", 0), w_oa_r[:, :, 256 * i:256 * i + 256]), (("ab", 1), w_ob_r[:, :, 256 * i:256 * i + 256])]
            if t in T_WO:
                i = T_WO.index(t)
                return [(("k8", 0, 256), w_o_r[:, :, 256 * i:256 * i + 256])]
            raise ValueError

        cvt_eng = ["dve", "act"]
        b_cv = [buf("cv0"), buf("cv1")]
        for t in range(NTILES):
            s = t % 2
            st = stg[s]
            ckey = -100.0 if t < 5 else 2.0 + 1.4 * (t - 5)
            R.t = ckey - 1.4 if t >= 7 else ckey
            for (lay, src) in tile_srcs(t):
                if lay[0] == "k8":
                    dst = st.rearrange("p (k c) -> p k c", k=8)[:, :, lay[1]:lay[1] + lay[2]]
                else:
                    dst = st.rearrange("p (a k c) -> p a k c", a=2, k=4)[:, lay[1]]
                dma(dst, src, [], [b_st[s]], b_st[s])
            R.t = ckey
            wt = mT[:, 4 * s:4 * s + 4, :].rearrange("p k t -> p (k t)")
            wb = b_cv[s]
            ce = cvt_eng[t % 2]
            if ce == "act":
                R.op("act", lambda e, wt=wt, st=st: e.activation(out=wt, in_=st, func=AF.Copy), reads=[b_st[s]], writes=[wb])
            else:
                R.op(ce, lambda e, wt=wt, st=st: e.tensor_copy(out=wt, in_=st), reads=[b_st[s]], writes=[wb])
            dma(wscr[t], wt, [wb], [b_wscr[t]], wb)

        setup_bufs = [b_const, b_Et, b_ones, b_st[0], b_st[1]] + b_cv + b_wscr
        for e in ("pe", "act", "dve", "pool", "sp"):
            pass
        R.t = 2.0 + 1.4 * (NTILES - 5) + 0.5
        bar_tok = R.op("sp", lambda e: e.nop(), reads=setup_bufs, writes=[b_setup])
        arena_bufs = p_ex.b + b_sga + b_sgb + [b_m1, b_m2] + b_mT
        for bb in arena_bufs:
            bb.writers = [bar_tok]

        wseq = wseq_in
        wcalls = []
        wstate = {"issued": 0, "used": 0, "slots": []}

        def w_issue(t):
            wt, wb = p_w.next()
            dma(wt[:], wscr[t], [b_wscr[t]], [wb], wb)
            wstate["slots"].append((wt, wb, t))
            wstate["issued"] += 1

        def w_next(expect):
            wcalls.append(expect)
            if wseq is None:
                w_issue(expect)
            else:
                while wstate["issued"] < min(len(wseq), wstate["used"] + 2):
                    w_issue(wseq[wstate["issued"]])
            wt, wb, t = wstate["slots"][wstate["used"]]
            assert t == expect, (t, expect)
            wstate["used"] += 1
            return wt, wb

        clk = [0.0]

        def tick(d=1.0):
            clk[0] += d
            R.t = clk[0]
            return clk[0]

        def proj_fm(wt, wb, col0, hsl, hb, pst, psb, lag=0.0):
            wv = wt.rearrange("p (k c) -> p k c", k=8)
            for kc in range(KC):
                R.op("pe", lambda e, kc=kc: e.matmul(pst[:], lhsT=wv[:, kc, col0:col0 + 128], rhs=hsl(kc),
                                                   start=(kc == 0), stop=(kc == KC - 1)),
                     reads=[wb] + hb, writes=[psb], lag=lag)

        L1 = 1.5
        XLAG = -2.0

        def qknorm(pst, psb, gcol, out_ap, out_bufs, pool):
            sqt, sqb = p_sq.next()
            R.op("act", lambda e: e.activation(out=sqt[:], in_=pst[:], func=AF.Square), reads=[psb], writes=[sqb])
            ms, msb = pool.next()
            R.op("pe", lambda e: e.matmul(ms[:], lhsT=BDm, rhs=sqt[:], start=True, stop=True), reads=[sqb, b_const], writes=[msb], lag=L1)
            sdt, sdb = p_sd.next()
            R.op("act", lambda e: e.activation(out=sdt[:], in_=ms[:], func=AF.Ln, bias=epsb[:, 0:1]), reads=[msb, b_const],
                 writes=[sdb], lag=L1)
            R.op("act", lambda e: e.activation(out=sdt[:], in_=sdt[:], func=AF.Exp, scale=-0.5), reads=[sdb], writes=[sdb], lag=L1)
            R.op("dve", lambda e: e.scalar_tensor_tensor(out=out_ap, in0=pst[:], scalar=gains[:, gcol:gcol + 1], in1=sdt[:],
                                                         op0=ALU.mult, op1=ALU.mult),
                 reads=[psb, sdb, b_const], writes=out_bufs, lag=L1)

        def rotary(qn_t, qn_b, cs_t, cs_b, out_ap, out_bufs, pool):
            rp, rpb_ = pool.next()
            R.op("pe", lambda e: e.matmul(rp[:], lhsT=Perm, rhs=qn_t[:], start=True, stop=True), reads=[qn_b, b_const], writes=[rpb_], lag=L1 + 1)
            t1, t1b = p_rt.next()
            t2, t2b = p_rt.next()
            R.op("pool", lambda e: e.tensor_tensor(out=t1, in0=qn_t[:], in1=cs_t[:, 0, :], op=ALU.mult), reads=[qn_b, cs_b], writes=[t1b], lag=L1 + 1)
            R.op("dve", lambda e: e.tensor_tensor(out=t2, in0=rp[:], in1=cs_t[:, 1, :], op=ALU.mult), reads=[rpb_, cs_b], writes=[t2b], lag=L1 + 1)
            R.op("pool", lambda e: e.tensor_tensor(out=out_ap, in0=t1, in1=t2, op=ALU.add), reads=[t1b, t2b], writes=out_bufs, lag=L1 + 1)

        def xnorm_tile(s, e_, tt):
            tok = e_ * 512 + tt * 128
            hbuf = buf(f"hT{e_}")
            xtile, xb = p_xt.next()
            dma(xtile[:], xs[s, tok:tok + 128, :], [], [xb], xb, lag=XLAG)
            stt, stb = p_stat.next()
            xnt, xnb = p_xn.next()
            R.op("dve", lambda e: e.memset(stt[:], 0.0), writes=[stb])
            R.op("act", lambda e: e.activation(out=xnt[:], in_=xtile[:], func=AF.Square, accum_out=stt[:, 0:1]),
                 reads=[xb], writes=[xnb, stb])
            R.op("act", lambda e: e.activation(out=stt[:, 1:2], in_=stt[:, 0:1], func=AF.Ln, bias=epsb[:, 0:1], scale=1.0 / D),
                 reads=[stb, b_const], writes=[stb])
            R.op("act", lambda e: e.activation(out=stt[:, 2:3], in_=stt[:, 1:2], func=AF.Exp, scale=-0.5), reads=[stb], writes=[stb])
            R.op("dve", lambda e: e.scalar_tensor_tensor(out=xnt[:], in0=xtile[:], scalar=stt[:, 2:3], in1=gbc[:], op0=ALU.mult, op1=ALU.mult),
                 reads=[xb, stb, b_const], writes=[xnb])
            tp, tpb = pT.next()
            for kc in range(KC):
                R.op("pe", lambda e, kc=kc: e.transpose(tp[:, kc * 128:(kc + 1) * 128], xnt[:, kc * 128:(kc + 1) * 128], ident),
                     reads=[xnb, b_const], writes=[tpb], lag=L1)
            R.op("dve", lambda e: e.tensor_copy(out=hT[:, :, tok:tok + 128], in_=tp[:].rearrange("p (k t) -> p k t", k=KC)),
                 reads=[tpb], writes=[hbuf], lag=L1)

        def kv_items(s, e_, pool, xn_next):
            t0 = e_ * 512
            hbuf = buf(f"hT{e_}")
            hsl = lambda kc: hT[:, kc, t0:t0 + 512]
            tick()
            cst_t, cst_b = p_cst.next()
            dma(cst_t[:], cs_d[s, :, :, t0:t0 + 512], [], [cst_b], cst_b)
            for i in range(2):
                for cc in range(2):
                    wt, wb = w_next(T_KA[i])
                    c = 2 * i + cc
                    tick()
                    if xn_next:
                        xnorm_tile(s, e_ + 1, c)
                    pst, psb = pool.next()
                    proj_fm(wt, wb, cc * 128, hsl, [hbuf], pst, psb)
                    qknorm(pst, psb, 1, KaT[:, c, t0:t0 + 512], [buf(f"KaT{e_}")], pool)
                    yield
            tick()
            wt, wb = w_next(T_KBVB)
            pst, psb = pool.next()
            proj_fm(wt, wb, 0, hsl, [hbuf], pst, psb)
            qt, qb_ = p_qbn.next()
            qknorm(pst, psb, 3, qt[:], [qb_], pool)
            rotary(qt, qb_, cst_t, cst_b, KbT[:, t0:t0 + 512], [buf(f"KbT{e_}")], pool)
            wv = wt.rearrange("p (k c) -> p k c", k=8)
            tick(3)
            for tt in range(4):
                tick()
                tok = t0 + tt * 128
                pst, psb = pool.next()
                for kc in range(KC):
                    R.op("pe", lambda e, kc=kc, pst=pst, tok=tok, wv=wv: e.matmul(pst[:, 0:128], lhsT=hT[:, kc, tok:tok + 128],
                                                                                 rhs=wv[:, kc, 128:256], start=(kc == 0),
                                                                                 stop=(kc == KC - 1)),
                         reads=[wb, hbuf], writes=[psb])
                ti = tok // 128
                R.op("dve", lambda e, pst=pst, ti=ti: e.tensor_copy(
                    out=Vb[:, ti, 64:320].rearrange("p (a c) -> p a c", a=2)[:, :, 0:64],
                    in_=pst[:, 0:128].rearrange("p (a c) -> p a c", a=2)),
                    reads=[psb, b_ones], writes=[buf(f"Vb{e_}")])
            yield
            tick()
            wts = [w_next(T_VA[0]), w_next(T_VA[1])]
            for tt in range(4):
                tick()
                tok = t0 + tt * 128
                pst, psb = pool.next()
                for i in range(2):
                    wv = wts[i][0].rearrange("p (k c) -> p k c", k=8)
                    for kc in range(KC):
                        R.op("pe", lambda e, kc=kc, pst=pst, tok=tok, wv=wv, i=i: e.matmul(
                            pst[:, i * 256:(i + 1) * 256], lhsT=hT[:, kc, tok:tok + 128], rhs=wv[:, kc, :],
                            start=(kc == 0), stop=(kc == KC - 1)), reads=[wts[i][1], hbuf], writes=[psb])
                ti = tok // 128
                R.op("dve", lambda e, pst=pst, ti=ti: e.tensor_copy(
                    out=Va[:, ti, :].rearrange("p (hp a c) -> p hp a c", hp=4, a=3)[:, :, 0, :],
                    in_=pst[:].rearrange("p (hp a c) -> p hp a c", hp=4, a=2)[:, :, 0, :]),
                    reads=[psb, b_ones], writes=[buf(f"Va{e_}")])
                R.op("pool" if False else "act", lambda e, pst=pst, ti=ti: e.activation(
                    out=Va[:, ti, :].rearrange("p (hp a c) -> p hp a c", hp=4, a=3)[:, :, 2, :],
                    in_=pst[:].rearrange("p (hp a c) -> p hp a c", hp=4, a=2)[:, :, 1, :], func=AF.Copy),
                    reads=[psb, b_ones], writes=[buf(f"Va{e_}")])
            yield

        def run_all(gen):
            for _ in gen:
                pass

        def qproj(s, b_):
            q0 = 256 + 512 * b_
            hb = blk_bufs("hT", q0, q0 + 512)
            hsl = lambda kc: hT[:, kc, q0:q0 + 512]
            tick()
            cst_t, cst_b = p_cst.next()
            dma(cst_t[:], cs_d[s, :, :, q0:q0 + 512], [], [cst_b], cst_b)
            for i in range(2):
                wt, wb = w_next(T_QA[i])
                for cc in range(2):
                    c = 2 * i + cc
                    tick()
                    pst, psb = pA.next()
                    proj_fm(wt, wb, cc * 128, hsl, hb, pst, psb)
                    qknorm(pst, psb, 0, QaT[:, c, :], [b_QaT], pA)
            for i in range(2):
                wt, wb = w_next(T_QB[i])
                for cc in range(2):
                    m = 2 * i + cc
                    tick()
                    pst, psb = pA.next()
                    proj_fm(wt, wb, cc * 128, hsl, hb, pst, psb)
                    qt, qb_ = p_qbn.next()
                    qknorm(pst, psb, 2, qt[:], [qb_], pA)
                    rotary(qt, qb_, cst_t, cst_b, QbT[:, m, :], [b_QbT], pA)
            tick(3)

        def keep_warm(n):
            for _ in range(n):
                R.op("pe", lambda e: e.matmul(psTf[:, 0:384], lhsT=ident, rhs=cbf[:, 0:384], start=True, stop=True),
                     reads=[b_const], writes=[pT.b[0]])

        def attention(s, b_, filler):
            vbase = s * NVALID
            q0 = 256 + 512 * b_
            hb = blk_bufs("hT", q0, q0 + 512)
            hsl = lambda kc: hT[:, kc, q0:q0 + 512]
            yaT = yaT2[b_ % 2]
            ybT = ybT2[b_ % 2]
            b_yaT = b_yaT2[b_ % 2]
            b_ybT = b_ybT2[b_ % 2]

            def zproj(wt, wb, cc):
                pst, psb = pZ.next()
                proj_fm(wt, wb, cc * 128, hsl, hb, pst, psb)
                szt, szb_ = p_sz.next()
                ZL = 3.0
                R.op("act", lambda e: e.activation(out=szt[:], in_=pst[:], func=AF.Exp, scale=-1.0), reads=[psb], writes=[szb_], lag=ZL)
                R.op("act", lambda e: e.activation(out=szt[:], in_=szt[:], func=AF.Ln, bias=cpow[:, 2:3]), reads=[szb_, b_const], writes=[szb_], lag=ZL)
                R.op("act", lambda e: e.activation(out=szt[:], in_=szt[:], func=AF.Exp, scale=-1.0), reads=[szb_], writes=[szb_], lag=ZL)
                R.op("dve", lambda e: e.tensor_tensor(out=szt[:], in0=szt[:], in1=pst[:], op=ALU.mult),
                     reads=[szb_, psb], writes=[szb_], lag=ZL)
                return szt, szb_

            LAG = 6.0

            def row_groups(kap):
                bnd = None
                interior = []
                for rr in range(8):
                    r = 8 * b_ + rr
                    if r < 4:
                        if kap <= 5:
                            bnd = (0, 3)
                    elif r >= RS - 4:
                        if kap >= 2:
                            bnd = (4, 7)
                    else:
                        if 2 * kap - 7 <= rr <= 2 * kap + 1:
                            interior.append(rr)
                ig = (interior[0], interior[-1]) if interior else None
                return bnd, ig

            sz_next = None
            for i in range(2):
                for cc in range(2):
                    hp = 2 * i + cc
                    tick()
                    if sz_next is None:
                        wt, wb = w_next(T_ZA[i])
                        sz_next = zproj(wt, wb, cc)
                    szt, szb_ = sz_next
                    obank = [pO.next(), pO.next()]
                    for hh in range(2):
                        if hh == 1 and hp < 3:
                            wt, wb = w_next(T_ZA[(hp + 1) // 2])
                            sz_next = zproj(wt, wb, (hp + 1) % 2)
                        h = 2 * hp + hh
                        pb = 64 * hh
                        ot, ob = obank[hh]
                        first_pv = True
                        for kap in range(8):
                            tick()
                            bg, ig = row_groups(kap)
                            groups = [g_ for g_ in (bg, ig) if g_ is not None]
                            if not groups:
                                continue
                            ra = min(g_[0] for g_ in groups)
                            rb = max(g_[1] for g_ in groups)
                            nr = rb - ra + 1
                            assert sum(g_[1] - g_[0] + 1 for g_ in groups) == nr
                            kt = 512 * b_ + 128 * kap
                            ti = kt // 128
                            kb_ = blk_bufs("KaT", kt, kt + 128)
                            vb_ = blk_bufs("Va", kt, kt + 128)
                            keep_warm(2)
                            st_, stb_ = pS.next()
                            R.op("pe", lambda e, st_=st_, pb=pb, kt=kt, hp=hp, ra=ra, rb=rb, nr=nr: e.matmul(
                                st_[:, 0:64 * nr], lhsT=KaT[pb:pb + 64, hp, kt:kt + 128],
                                rhs=QaT[pb:pb + 64, hp, ra * 64:(rb + 1) * 64], start=True, stop=True),
                                reads=kb_ + [b_QaT], writes=[stb_])
                            ext_, exb_ = p_ex.next()
                            R.op("act", lambda e, st_=st_, ext_=ext_, nr=nr: e.activation(out=ext_[:, 0:nr * 64], in_=st_[:, 0:nr * 64],
                                                                                         func=AF.Exp, scale=0.125),
                                 reads=[stb_], writes=[exb_])
                            ptt, ptb = p_PT.next()
                            for (ga, gb_) in groups:
                                gn = gb_ - ga + 1
                                c0 = (ga - ra) * 64
                                dr0 = 2 * kap + 3 - ga
                                ex3 = ext_[:, c0:c0 + gn * 64].rearrange("p (j c) -> p j c", j=gn)
                                pt3 = ptt[:, c0:c0 + gn * 64].rearrange("p (j c) -> p j c", j=gn)
                                if (ga, gb_) == bg:
                                    v0 = 13 - dr0
                                    ev = Et[:, h, v0:v0 + gn, :]
                                    R.op("pool", lambda e, ex3=ex3, ev=ev: e.tensor_tensor(out=ex3, in0=ex3, in1=ev, op=ALU.mult),
                                         reads=[exb_, b_Et], writes=[exb_])
                                    kp = kap if ga == 0 else kap - 2
                                    vv = valid[:, vbase:vbase + 48].rearrange("p (r k) -> p r k", k=6)[:, ga:ga + gn, kp]
                                    vv = vv.unsqueeze(2).to_broadcast([128, gn, 64])
                                    R.op("dve", lambda e, pt3=pt3, ex3=ex3, vv=vv: e.tensor_tensor(out=pt3, in0=ex3, in1=vv, op=ALU.mult),
                                         reads=[exb_, b_const], writes=[ptb])
                                else:
                                    assert 2 <= dr0 - (gn - 1) and dr0 <= 10, (dr0, gn)
                                    v0 = 14 + 10 - dr0
                                    ev = Et[:, h, v0:v0 + gn, :]
                                    R.op("dve" if (kap % 2 == 0) else "pool",
                                         lambda e, pt3=pt3, ex3=ex3, ev=ev: e.tensor_tensor(out=pt3, in0=ex3, in1=ev, op=ALU.mult),
                                         reads=[exb_, b_Et], writes=[ptb])
                            R.op("pe", lambda e, ot=ot, ti=ti, hp=hp, hh=hh, ptt=ptt, ra=ra, rb=rb, nr=nr, fp=first_pv: e.matmul(
                                ot[:, ra * 64:(rb + 1) * 64], lhsT=Va[:, ti, hp * 192 + hh * 64:hp * 192 + hh * 64 + 128],
                                rhs=ptt[:, 0:64 * nr], start=fp, stop=False, skip_group_check=True),
                                reads=vb_ + [ptb, b_ones], writes=[ob], lag=LAG)
                            first_pv = False
                        po = 64 - pb
                        rdt, rdb_ = p_rd.next()
                        R.op("act", lambda e, ot=ot, rdt=rdt, pb=pb, po=po: e.activation(out=rdt[pb:pb + 64, :], in_=ot[po:po + 64, :],
                                                                                       func=AF.Ln),
                             reads=[ob], writes=[rdb_], lag=LAG + 0.5)
                        R.op("act", lambda e, rdt=rdt, pb=pb: e.activation(out=rdt[pb:pb + 64, :], in_=rdt[pb:pb + 64, :], func=AF.Exp, scale=-1.0),
                             reads=[rdb_], writes=[rdb_], lag=LAG + 0.5)
                        R.op("pool", lambda e, rdt=rdt, szt=szt, pb=pb: e.tensor_tensor(out=rdt[pb:pb + 64, :], in0=rdt[pb:pb + 64, :],
                                                                                       in1=szt[pb:pb + 64, :], op=ALU.mult),
                             reads=[rdb_, szb_], writes=[rdb_], lag=LAG + 0.5)
                        R.op("dve", lambda e, ot=ot, rdt=rdt, pb=pb, hp=hp: e.tensor_tensor(
                            out=yaT[pb:pb + 64, hp, :], in0=ot[pb:pb + 64, :], in1=rdt[pb:pb + 64, :], op=ALU.mult),
                            reads=[ob, rdb_], writes=[b_yaT[hp]], lag=LAG + 0.5)
                        filler()

            for g in range(2):
                tick(8)
                wt, wb = w_next(T_ZB[g])
                sz2 = [zproj(wt, wb, 0), zproj(wt, wb, 1)]
                pb = 64 * g
                for n_ in range(4):
                    tick(2)
                    qt0 = q0 + 128 * n_
                    pts = []
                    keep_warm(3)
                    for dl in (-1, 0, 1):
                        kt0 = qt0 + 128 * dl
                        st_, stb_ = pS.next()
                        R.op("pe", lambda e, st_=st_, kt0=kt0, pb=pb, n_=n_: e.matmul(
                            st_[:].rearrange("p (m q) -> p m q", m=4), lhsT=KbT[pb:pb + 64, kt0:kt0 + 128],
                            rhs=QbT[pb:pb + 64, :, n_ * 128:(n_ + 1) * 128], start=True, stop=True),
                            reads=blk_bufs("KbT", kt0, kt0 + 128) + [b_QbT], writes=[stb_])
                        ptt, ptb = p_PT.next()
                        if dl == 0:
                            R.op("act", lambda e, st_=st_, ptt=ptt: e.activation(out=ptt[:], in_=st_[:], func=AF.Exp, scale=0.125),
                                 reads=[stb_], writes=[ptb])
                        else:
                            ext_, exb_ = p_ex.next()
                            R.op("act", lambda e, st_=st_, ext_=ext_: e.activation(out=ext_, in_=st_[:], func=AF.Exp, scale=0.125),
                                 reads=[stb_], writes=[exb_])
                            mk = swam[:, 0:128] if dl == -1 else swam[:, 128:256]
                            mk4 = mk.unsqueeze(1).to_broadcast([128, 4, 128])
                            edge = (dl == -1 and b_ == 0 and n_ == 0) or (dl == 1 and b_ == NQB - 1 and n_ == 3)
                            ex3 = ext_.rearrange("p (m q) -> p m q", m=4)
                            pt3 = ptt[:].rearrange("p (m q) -> p m q", m=4)
                            if edge:
                                vc = vbase + 48 + (0 if dl == -1 else 1)
                                R.op("dve", lambda e, ex3=ex3, pt3=pt3, mk4=mk4, vc=vc: e.scalar_tensor_tensor(
                                    out=pt3, in0=ex3, scalar=valid[:, vc:vc + 1], in1=mk4, op0=ALU.mult, op1=ALU.mult),
                                    reads=[exb_, b_const], writes=[ptb])
                            else:
                                R.op("pool" if dl == -1 else "dve",
                                     lambda e, ex3=ex3, pt3=pt3, mk4=mk4: e.tensor_tensor(out=pt3, in0=ex3, in1=mk4, op=ALU.mult),
                                     reads=[exb_, b_const], writes=[ptb])
                        pts.append((ptt, ptb, kt0))
                    ot, ob = pO.next()
                    for par in range(2):
                        vcol = g * 128 + 64 if par == 0 else g * 128
                        for k_, (ptt, ptb, kt0) in enumerate(pts):
                            ti = kt0 // 128
                            rhs = ptt[:].rearrange("p (a b q) -> p a b q", a=2, b=2)[:, :, par, :]
                            R.op("pe", lambda e, ot=ot, ti=ti, vcol=vcol, rhs=rhs, par=par, k_=k_: e.matmul(
                                ot[:, par * 256:(par + 1) * 256].rearrange("p (a q) -> p a q", a=2), lhsT=Vb[:, ti, vcol:vcol + 128],
                                rhs=rhs, start=(k_ == 0), stop=(k_ == 2)),
                                reads=blk_bufs("Vb", kt0, kt0 + 128) + [ptb, b_ones], writes=[ob], lag=2.5)
                    for par in range(2):
                        pbo = 64 * par
                        pde = 64 - pbo
                        rdt, rdb_ = p_rd.next()
                        for a in range(2):
                            h = 4 * g + 2 * a + par
                            R.op("act", lambda e, ot=ot, rdt=rdt, pbo=pbo, pde=pde, par=par, a=a, h=h: e.activation(
                                out=rdt[pbo:pbo + 64, a * 128:(a + 1) * 128],
                                in_=ot[pde:pde + 64, par * 256 + a * 128:par * 256 + (a + 1) * 128],
                                func=AF.Ln, bias=esink[pbo:pbo + 64, h:h + 1]), reads=[ob, b_const], writes=[rdb_], lag=3.0)
                        R.op("act", lambda e, rdt=rdt, pbo=pbo: e.activation(out=rdt[pbo:pbo + 64, 0:256], in_=rdt[pbo:pbo + 64, 0:256],
                                                                            func=AF.Exp, scale=-1.0),
                             reads=[rdb_], writes=[rdb_], lag=3.0)
                        for a in range(2):
                            szt, szb_ = sz2[a]
                            R.op("pool", lambda e, rdt=rdt, szt=szt, pbo=pbo, a=a, n_=n_: e.tensor_tensor(
                                out=rdt[pbo:pbo + 64, a * 128:(a + 1) * 128], in0=rdt[pbo:pbo + 64, a * 128:(a + 1) * 128],
                                in1=szt[pbo:pbo + 64, n_ * 128:(n_ + 1) * 128], op=ALU.mult),
                                reads=[rdb_, szb_], writes=[rdb_], lag=3.0)
                        R.op("dve", lambda e, ot=ot, rdt=rdt, pbo=pbo, par=par, g=g, n_=n_: e.tensor_tensor(
                            out=ybT[pbo:pbo + 64, 2 * g:2 * g + 2, n_ * 128:(n_ + 1) * 128],
                            in0=ot[pbo:pbo + 64, par * 256:(par + 1) * 256].rearrange("p (a q) -> p a q", a=2),
                            in1=rdt[pbo:pbo + 64, 0:256].rearrange("p (a q) -> p a q", a=2), op=ALU.mult),
                            reads=[ob, rdb_], writes=[b_ybT[2 * g], b_ybT[2 * g + 1]], lag=3.0)
                    if n_ % 2 == 1:
                        filler()
            tick(4)

        def merge_items(s, b_, pool):
            q0 = 256 + 512 * b_
            hb = blk_bufs("hT", q0, q0 + 512)
            hsl = lambda kc: hT[:, kc, q0:q0 + 512]
            yaT = yaT2[b_ % 2]
            ybT = ybT2[b_ % 2]
            b_yaT = b_yaT2[b_ % 2]
            b_ybT = b_ybT2[b_ % 2]
            for i in range(4):
                tick()
                wga, wgab = w_next(T_GA[i])
                for cc in range(2):
                    pga, pgab = pool.next()
                    proj_fm(wga, wgab, cc * 128, hsl, hb, pga, pgab)
                    R.op("act", lambda e, pga=pga, cc=cc: e.activation(out=sga[cc], in_=pga[:], func=AF.Exp, scale=-1.0),
                         reads=[pgab], writes=[b_sga[cc]])
                    R.op("act", lambda e, cc=cc: e.activation(out=sga[cc], in_=sga[cc], func=AF.Ln, bias=cpow[:, 2:3]),
                         reads=[b_sga[cc], b_const], writes=[b_sga[cc]])
                    R.op("act", lambda e, cc=cc: e.activation(out=sga[cc], in_=sga[cc], func=AF.Exp, scale=-1.0),
                         reads=[b_sga[cc]], writes=[b_sga[cc]])
                tick()
                wgb, wgbb = w_next(T_GB[i])
                for cc in range(2):
                    pgb, pgbb = pool.next()
                    proj_fm(wgb, wgbb, cc * 128, hsl, hb, pgb, pgbb)
                    R.op("act", lambda e, pgb=pgb, cc=cc: e.activation(out=sgb[cc], in_=pgb[:], func=AF.Exp, scale=-1.0),
                         reads=[pgbb], writes=[b_sgb[cc]])
                    R.op("act", lambda e, cc=cc: e.activation(out=sgb[cc], in_=sgb[cc], func=AF.Ln, bias=cpow[:, 2:3]),
                         reads=[b_sgb[cc], b_const], writes=[b_sgb[cc]])
                    R.op("act", lambda e, cc=cc: e.activation(out=sgb[cc], in_=sgb[cc], func=AF.Exp, scale=-1.0),
                         reads=[b_sgb[cc]], writes=[b_sgb[cc]])
                tick()
                wab, wabb = w_next(T_WAB[i])
                wabv = wab.rearrange("p (a k c) -> p a k c", a=2, k=4)
                for cc in range(2):
                    c = 2 * i + cc
                    pa, pab = pool.next()
                    for kc in range(4):
                        R.op("pe", lambda e, kc=kc, pa=pa, cc=cc, wabv=wabv: e.matmul(pa[:], lhsT=wabv[:, 0, kc, cc * 128:(cc + 1) * 128],
                                                                                     rhs=yaT[:, kc, :], start=(kc == 0), stop=(kc == 3)),
                             reads=[wabb] + b_yaT, writes=[pab])
                    R.op("dve", lambda e, pa=pa, cc=cc: e.tensor_tensor(out=m1b, in0=pa[:], in1=sga[cc], op=ALU.mult),
                         reads=[pab, b_sga[cc]], writes=[b_m1])
                    pb2, pbb = pool.next()
                    for kc in range(4):
                        R.op("pe", lambda e, kc=kc, pb2=pb2, cc=cc, wabv=wabv: e.matmul(pb2[:], lhsT=wabv[:, 1, kc, cc * 128:(cc + 1) * 128],
                                                                                       rhs=ybT[:, kc, :], start=(kc == 0), stop=(kc == 3)),
                             reads=[wabb] + b_ybT, writes=[pbb])
                    R.op("dve", lambda e, pb2=pb2, cc=cc: e.tensor_tensor(out=m2b, in0=pb2[:], in1=sgb[cc], op=ALU.mult),
                         reads=[pbb, b_sgb[cc]], writes=[b_m2])
                    R.op("pool", lambda e, c=c: e.tensor_tensor(out=mT[:, c, :], in0=m1b, in1=m2b, op=ALU.add),
                         reads=[b_m1, b_m2], writes=[b_mT[c]])
                yield
            tick()
            xres = []
            for tt in range(4):
                xtile, xb = p_xt.next()
                dma(xtile[:], xs[s, q0 + tt * 128:q0 + (tt + 1) * 128, :], [], [xb], xb)
                xres.append((xtile, xb))
            for cg in range(4):
                tick()
                wo_t, wo_b = w_next(T_WO[cg])
                wov = wo_t.rearrange("p (k c) -> p k c", k=8)
                for tt in range(4):
                    pst, psb = pool.next()
                    for kc in range(KC):
                        R.op("pe", lambda e, kc=kc, pst=pst, tt=tt, wov=wov: e.matmul(
                            pst[:, 0:256], lhsT=mT[:, kc, tt * 128:(tt + 1) * 128], rhs=wov[:, kc, :],
                            start=(kc == 0), stop=(kc == KC - 1)), reads=[wo_b] + b_mT, writes=[psb])
                    xtile, xb = xres[tt]
                    R.op("dve", lambda e, pst=pst, xtile=xtile, cg=cg: e.tensor_tensor(
                        out=xtile[:, cg * 256:(cg + 1) * 256], in0=pst[:, 0:256], in1=xtile[:, cg * 256:(cg + 1) * 256],
                        op=ALU.add), reads=[psb, xb], writes=[xb])
                yield
            tick()
            for tt in range(4):
                xtile, xb = xres[tt]
                r0 = 512 * b_ + 128 * tt
                tk = dma(y[s, r0:r0 + 128, :], xtile[:], [xb], [], xb)
                R.store_toks.append(tk)
            yield

        clk[0] = 0.0
        for s in range(nslot):
            tick(2)
            for tt in range(4):
                tick()
                xnorm_tile(s, 0, tt)
            tick(2)
            run_all(kv_items(s, 0, pA, True))
            run_all(kv_items(s, 1, pA, True))
            pending = [kv_items(s, 2, pZ, False)]

            def filler():
                while pending:
                    try:
                        next(pending[0])
                        tick(3)
                        return
                    except StopIteration:
                        pending.pop(0)

            def drain():
                while pending:
                    filler()

            for b_ in range(NQB):
                qproj(s, b_)
                attention(s, b_, filler)
                drain()
                if b_ + 1 < NQB:
                    pending.append(merge_items(s, b_, pZ))
                else:
                    run_all(merge_items(s, b_, pA))

        tick(10)

        if dbg and wseq_in is not None:
            allb = list(B.values())
            def dump(name, t, shape, dt):
                o = nc.dram_tensor(name, list(shape), dt, kind="ExternalOutput").ap()
                tk = dma(o, t, allb, [], buf("dbg_" + name))
                R.store_toks.append(tk)
            dump("d_hT", hT[:].rearrange("p k t -> p (k t)"), [128, KC * EXT], BF16)
            dump("d_KaT", KaT[:].rearrange("p k t -> p (k t)"), [128, 4 * EXT], BF16)
            dump("d_KbT", KbT[:], [128, EXT], BF16)
            dump("d_Va", Va[:].rearrange("p k t -> p (k t)"), [128, NT * 768], BF16)
            dump("d_Vb", Vb[:].rearrange("p k t -> p (k t)"), [128, NT * 320], BF16)
            dump("d_QaT", QaT[:].rearrange("p k t -> p (k t)"), [128, 4 * 512], BF16)
            dump("d_QbT", QbT[:].rearrange("p k t -> p (k t)"), [128, 4 * 512], BF16)
            dump("d_yaT", yaT2[(NQB - 1) % 2][:].rearrange("p k t -> p (k t)"), [128, 4 * 512], BF16)
            dump("d_ybT", ybT2[(NQB - 1) % 2][:].rearrange("p k t -> p (k t)"), [128, 4 * 512], BF16)
            dump("d_mT", mT[:].rearrange("p k t -> p (k t)"), [128, KC * 512], BF16)
            dump("d_Et", Et[:].rearrange("p h v c -> p (h v c)"), [128, 8 * NVAR * 64], BF16)

        last = {}
        for t in R.store_toks:
            if id(t.sem) not in last or t.order > last[id(t.sem)].order:
                last[id(t.sem)] = t
        R.op("sp", lambda e: e.nop(), extra=list(last.values()))


        return wcalls

    wseq_real = record(Rec(), None)
    record(R, wseq_real)

    R.finalize()
    for e in R.ENGS:
        R.sems[e].handle = es.enter_context(nc.semaphore(R.sems[e].name))
    for sdm in R.dsems:
        sdm.handle = es.enter_context(nc.semaphore(sdm.name))
    with nc.Block() as block:
        @block.sync
        def _(eng):
            R.emit("sp", eng)

        @block.tensor
        def _(eng):
            R.emit("pe", eng)

        @block.scalar
        def _(eng):
            R.emit("act", eng)

        @block.vector
        def _(eng):
            R.emit("dve", eng)

        @block.gpsimd
        def _(eng):
            R.emit("pool", eng)
    es.close()
    return nc


def _variants():
    v = [(13 - i, 1, 1) for i in range(14)]
    v += [(10, 1, 0)] + [(d, 1, 1) for d in range(9, 2, -1)] + [(2, 0, 1)]
    return v


def _host_constants():
    p = np.arange(128)
    ck = p % 64
    half = p // 64
    cq = np.arange(64)
    cs_ = np.clip(cq - 8, 0, 48)
    col_in = (ck[:, None] >= cs_[None, :]) & (ck[:, None] < cs_[None, :] + 16)
    dc = np.clip(ck[:, None] - cq[None, :], -15, 15) + 15
    var = _variants()
    dr_idx = np.stack([np.clip(d + half, 0, 14) for (d, lo, up) in var], axis=1)
    Mk = np.zeros((128, NVAR, 64), np.float32)
    for vi, (d, lo, up) in enumerate(var):
        hv = np.where(half == 0, lo, up).astype(np.float32)
        Mk[:, vi, :] = col_in.astype(np.float32) * hv[:, None]
    ident = np.eye(128, dtype=np.float32)
    BD = np.zeros((128, 128), np.float32)
    BD[:64, :64] = 1.0 / 64
    BD[64:, 64:] = 1.0 / 64
    partner = np.where((p % 64) < 32, p + 32, p - 32)
    Perm = np.zeros((128, 128), np.float32)
    Perm[partner, p] = 1.0
    cbf = np.concatenate([ident, BD, Perm], axis=1).astype(ml_dtypes.bfloat16)
    j = np.arange(128)[:, None]
    i = np.arange(128)[None, :]
    swam = np.concatenate([(j >= i), (j <= i)], axis=1).astype(np.float32)
    return dict(dr_idx=dr_idx, dc=dc, Mk=Mk.reshape(128, NVAR * 64), cbf=cbf, swam=swam)


def _slot_tables(row0):
    p = np.arange(128)
    half = p // 64
    val = np.zeros((128, NVALID), np.float32)
    for bi in range(8):
        r = bi if bi < 4 else RS - 4 + (bi - 4)
        start_local = -4 if bi < 4 else RS - 8
        R_ = row0 + r
        w0 = min(max(R_ - 4, 0), NROWS - 8)
        for j in range(6):
            kr = row0 + start_local + 2 * j + half
            val[:, bi * 6 + j] = ((kr >= w0) & (kr < w0 + 8)).astype(np.float32)
    val[:, 48] = 1.0 if row0 > 0 else 0.0
    val[:, 49] = 1.0 if row0 + RS < NROWS else 0.0
    pos = (row0 - 4) * GW + np.arange(EXT)
    halfd = 32
    inv = (np.float32(10000.0) ** (-(np.arange(halfd, dtype=np.float32)) / np.float32(halfd))).astype(np.float32)
    ang = pos.astype(np.float32)[None, :] * inv[:, None]
    cos = np.cos(ang).astype(np.float32)
    sin = np.sin(ang).astype(np.float32)
    d = p % 64
    cs = np.zeros((128, 2, EXT), np.float32)
    cs[:, 0, :] = cos[d % 32]
    cs[:, 1, :] = sin[d % 32] * np.where(d < 32, -1.0, 1.0)[:, None].astype(np.float32)
    return val, cs


_PROG = {}


def _make_in_maps(inputs, slot_list_per_core):
    hc = _host_constants()
    x_all = [np.asarray(inputs["x_prompt"], np.float32), np.asarray(inputs["x_sample"], np.float32)]
    seqs = [x_all[0][i] for i in range(x_all[0].shape[0])] + [x_all[1][i] for i in range(x_all[1].shape[0])]
    rpb = np.asarray(inputs["rpb_a"], np.float32)[0]
    Gt = rpb[:, hc["dr_idx"][:, :, None], hc["dc"][:, None, :]]
    Gt = np.ascontiguousarray(np.transpose(Gt, (1, 0, 2, 3))).reshape(128, 8 * NVAR * 64)
    p = np.arange(128)
    gains = np.stack([np.asarray(inputs[k], np.float32)[0][p % 64] for k in ("qn_a", "kn_a", "qn_b", "kn_b")], axis=1)
    common = {
        "w_in": np.ascontiguousarray(np.asarray(inputs["w_in"], np.float32)[0]),
        "w_out_a": np.ascontiguousarray(np.asarray(inputs["w_out_a"], np.float32)[0]),
        "w_out_b": np.ascontiguousarray(np.asarray(inputs["w_out_b"], np.float32)[0]),
        "w_o": np.ascontiguousarray(np.asarray(inputs["w_o"], np.float32)[0]),
        "gbc": np.ascontiguousarray(np.broadcast_to(np.asarray(inputs["norm_g"], np.float32)[0][None, :], (128, D))),
        "gains": np.ascontiguousarray(gains),
        "sinkb": np.ascontiguousarray(np.broadcast_to(np.asarray(inputs["sink_b"], np.float32)[0][None, :], (128, 8))),
        "Gt": Gt, "Mk": hc["Mk"], "cbf": hc["cbf"], "swam": hc["swam"],
    }
    tabs = {}
    in_maps = []
    for slots in slot_list_per_core:
        ns = len(slots)
        xs = np.zeros((ns, EXT, D), np.float32)
        valid = np.zeros((128, ns * NVALID), np.float32)
        cs = np.zeros((ns, 128, 2, EXT), np.float32)
        for si, (sq_, row0) in enumerate(slots):
            lo = (row0 - 4) * GW
            hi = lo + EXT
            a, b = max(lo, 0), min(hi, SEQ)
            xs[si, a - lo:b - lo] = seqs[sq_][a:b]
            if row0 not in tabs:
                tabs[row0] = _slot_tables(row0)
            valid[:, si * NVALID:(si + 1) * NVALID] = tabs[row0][0]
            cs[si] = tabs[row0][1]
        m = dict(common)
        m.update({"xs": xs, "valid": valid, "cs": cs})
        in_maps.append(m)
    return in_maps


def kernel(**inputs):
    nseq_p = np.asarray(inputs["x_prompt"]).shape[0]
    nseq_s = np.asarray(inputs["x_sample"]).shape[0]
    nseq = nseq_p + nseq_s
    qper = NROWS // RS
    all_slots = [(sq_, q * RS) for sq_ in range(nseq) for q in range(qper)]
    assert len(all_slots) == NCORES * NSLOT
    per_core = [all_slots[c * NSLOT:(c + 1) * NSLOT] for c in range(NCORES)]
    in_maps = _make_in_maps(inputs, per_core)
    if "nc" not in _PROG:
        _PROG["nc"] = build_program(NSLOT)
    res = run_bass_kernel_spmd(_PROG["nc"], in_maps, core_ids=list(range(NCORES)))
    outs = [np.zeros((SEQ, D), np.float32) for _ in range(nseq)]
    for c in range(NCORES):
        yc = res.results[c]["y"]
        for si, (sq_, row0) in enumerate(per_core[c]):
            outs[sq_][row0 * GW:(row0 + RS) * GW] = yc[si]
    y_prompt = np.stack(outs[:nseq_p], axis=0)
    y_sample = np.stack(outs[nseq_p:], axis=0)
    return (y_prompt, y_sample)
```

```python
import contextlib
import numpy as np
import ml_dtypes
import concourse.bass as bass
import concourse.mybir as mybir
from concourse.bass_utils import run_bass_kernel_spmd

F32 = mybir.dt.float32
BF16 = mybir.dt.bfloat16
AF = mybir.ActivationFunctionType
ALU = mybir.AluOpType

D = 1024
KC = 8
SEQ = 4096
GW = 64
NROWS = 64
RS = 16
EXTR = RS + 8
EXT = EXTR * GW
QS = RS * GW
NQB = QS // 512
NEB = EXT // 512
NT = EXT // 128
NCORES = 8
NSLOT = (12 * NROWS // RS) // NCORES
NVAR = 23
NVALID = 8 * 6 + 2
EPS = 1e-6

C_QA, C_KA, C_VA, C_ZA, C_QB, C_KB, C_VB, C_ZB, C_GA, C_GB = 0, 512, 1024, 1536, 2048, 2560, 2688, 2816, 3328, 4352

T_KA = [0, 1]
T_KBVB = 2
T_VA = [3, 4]
T_QA = [5, 6]
T_QB = [7, 8]
T_ZA = [9, 10]
T_ZB = [11, 12]
T_GA = [13, 14, 15, 16]
T_GB = [17, 18, 19, 20]
T_WAB = [21, 22, 23, 24]
T_WO = [25, 26, 27, 28]
NTILES = 29


class Sem:
    def __init__(self, name, is_dma):
        self.name = name
        self.is_dma = is_dma
        self.handle = None
        self.count = 0


class Tok:
    __slots__ = ("sem", "order", "value", "op")

    def __init__(self, sem, order, value=None, op=None):
        self.sem = sem
        self.order = order
        self.value = value
        self.op = op


class Buf:
    def __init__(self, name):
        self.name = name
        self.writers = []
        self.readers = []
        self.dsem = None


class Op:
    __slots__ = ("fn", "deps", "tok", "signal", "is_dma", "key", "idx", "eng", "info")

    def __init__(self, fn, deps, tok, is_dma, key, idx, eng):
        self.fn = fn
        self.deps = deps
        self.tok = tok
        self.signal = False
        self.is_dma = is_dma
        self.key = key
        self.idx = idx
        self.eng = eng


class Rec:
    ENGS = ("pe", "act", "dve", "pool", "sp")

    def __init__(self):
        self.ops = {e: [] for e in self.ENGS}
        self.sems = {e: Sem("s_" + e, False) for e in self.ENGS}
        self.dsems = []
        self.store_toks = []
        self.t = 0.0
        self.nrec = 0

    def _merge(self, deps, t):
        k = id(t.sem)
        if k not in deps or t.order > deps[k].order:
            deps[k] = t

    def _trim(self, lst, t):
        for x in lst:
            if x.sem is t.sem and x.order > t.order:
                return
        lst[:] = [x for x in lst if x.sem is not t.sem]
        lst.append(t)

    def op(self, eng, fn, reads=(), writes=(), dma_owner=None, extra=(), lag=0.0):
        deps = {}
        for b in reads:
            for t in b.writers:
                self._merge(deps, t)
        for b in writes:
            for t in b.readers:
                self._merge(deps, t)
            for t in b.writers:
                self._merge(deps, t)
        for t in extra:
            self._merge(deps, t)
        if dma_owner is not None:
            if dma_owner.dsem is None:
                dma_owner.dsem = Sem("d_" + dma_owner.name, True)
                self.dsems.append(dma_owner.dsem)
            s = dma_owner.dsem
            s.count += 16
            tok = Tok(s, (float(s.count), 0), s.count)
            self.nrec += 1
        else:
            self.nrec += 1
            tok = Tok(self.sems[eng], (self.t + lag, self.nrec))
        o = Op(fn, list(deps.values()), tok, dma_owner is not None, self.t + lag, self.nrec, eng)
        tok.op = o
        o.info = ([b.name for b in reads], [b.name for b in writes])
        self.ops[eng].append(o)
        wset = set(id(b) for b in writes)
        for b in writes:
            if b.readers:
                b.writers = [tok]
                b.readers = []
            else:
                self._trim(b.writers, tok)
        for b in reads:
            if id(b) not in wset:
                self._trim(b.readers, tok)
        return tok

    def barrier(self, bufs):
        pass

    def finalize(self):
        for e in self.ENGS:
            self.ops[e].sort(key=lambda o: (o.key, o.idx))
            for o in self.ops[e]:
                for t in o.deps:
                    if t.op is not None:
                        assert (t.op.key, t.op.idx) < (o.key, o.idx), ("non-monotone dep", e, o.key, o.idx, o.info, t.op.eng, t.op.key, t.op.idx, t.op.info)
        for e in self.ENGS:
            for o in self.ops[e]:
                for t in o.deps:
                    if t.op is not None and not t.op.is_dma:
                        if not (e == "pe" and t.sem is self.sems["pe"]):
                            t.op.signal = True
        for e in self.ENGS:
            c = 0
            for o in self.ops[e]:
                if not o.is_dma and o.signal:
                    c += 1
                    o.tok.value = c

    def emit(self, eng_name, eng):
        seen = {}
        own = self.sems[eng_name]
        n_wait = 0
        for o in self.ops[eng_name]:
            for t in o.deps:
                if eng_name == "pe" and t.sem is own:
                    continue
                v = t.value
                assert v is not None
                if seen.get(id(t.sem), 0) >= v:
                    continue
                seen[id(t.sem)] = v
                eng.wait_ge(t.sem.handle, v)
                n_wait += 1
            ins = o.fn(eng)
            if o.is_dma:
                ins.then_inc(o.tok.sem.handle, 16)
            elif o.signal:
                ins.then_inc(own.handle, 1)
        return n_wait


def build_program(nslot=NSLOT, dbg=False):
    nc = bass.Bass("TRN2", target_bir_lowering=False)
    R = Rec()
    es = contextlib.ExitStack()

    def dram_in(name, shape, dt=F32):
        return nc.dram_tensor(name, list(shape), dt, kind="ExternalInput").ap()

    xs = dram_in("xs", [nslot, EXT, D])
    w_in = dram_in("w_in", [D, 5376])
    w_oa = dram_in("w_out_a", [512, D])
    w_ob = dram_in("w_out_b", [512, D])
    w_o = dram_in("w_o", [D, D])
    gbc_d = dram_in("gbc", [128, D])
    gains_d = dram_in("gains", [128, 4])
    sink_d = dram_in("sinkb", [128, 8])
    Gt_d = dram_in("Gt", [128, 8 * NVAR * 64])
    Mk_d = dram_in("Mk", [128, NVAR * 64])
    cbf_d = dram_in("cbf", [128, 3 * 128], BF16)
    swam_d = dram_in("swam", [128, 256])
    valid_d = dram_in("valid", [128, nslot * NVALID])
    cs_d = dram_in("cs", [nslot, 128, 2, EXT])
    y = nc.dram_tensor("y", [nslot, QS, D], F32, kind="ExternalOutput").ap()
    wscr = nc.dram_tensor("wscr", [NTILES, 128, 2048], BF16).ap()

    def sb(name, shape, dt):
        return es.enter_context(nc.sbuf_tensor(name, list(shape), dt))

    def ps(name, shape, dt):
        return es.enter_context(nc.psum_tensor(name, list(shape), dt))

    hT = sb("hT", [128, KC, EXT], BF16)
    KaT = sb("KaT", [128, 4, EXT], BF16)
    KbT = sb("KbT", [128, EXT], BF16)
    Va = sb("Va", [128, NT, 768], BF16)
    Vb = sb("Vb", [128, NT, 320], BF16)
    Et = sb("Et", [128, 8, NVAR, 64], BF16)
    cbf = sb("cbf_s", [128, 3 * 128], BF16)
    ident = cbf[:, 0:128]
    BDm = cbf[:, 128:256]
    Perm = cbf[:, 256:384]
    swam = sb("swam_s", [128, 256], F32)
    gbc = sb("gbc_s", [128, D], F32)
    gains = sb("gains_s", [128, 4], F32)
    esink = sb("esink", [128, 8], F32)
    valid = sb("valid_s", [128, nslot * NVALID], F32)
    epsb = sb("epsb", [128, 1], F32)
    cpow = sb("cpow", [128, 3], F32)
    wbuf = [sb(f"wbuf{i}", [128, 2048], BF16) for i in range(3)]
    arena = sb("arena", [128, 4096], F32)
    xt = [sb(f"xt{i}", [128, D], F32) for i in range(4)]
    xn = [sb(f"xn{i}", [128, D], BF16) for i in range(2)]
    stat = [sb(f"stat{i}", [128, 4], F32) for i in range(2)]
    sq = [sb(f"sq{i}", [128, 512], BF16) for i in range(2)]
    sd = [sb(f"sd{i}", [128, 512], F32) for i in range(2)]
    qbn = [sb(f"qbn{i}", [128, 512], BF16) for i in range(1)]
    cst = [sb(f"cst{i}", [128, 2, 512], F32) for i in range(1)]
    QaT = sb("QaT", [128, 4, 512], BF16)
    QbT = sb("QbT", [128, 4, 512], BF16)
    PT = [sb(f"PT{i}", [128, 512], BF16) for i in range(6)]
    szb = [sb(f"sz{i}", [128, 512], F32) for i in range(2)]
    rdb = [sb(f"rd{i}", [128, 512], F32) for i in range(2)]
    yaT2 = [sb(f"yaT{i}", [128, 4, 512], BF16) for i in range(2)]
    ybT2 = [sb(f"ybT{i}", [128, 4, 512], BF16) for i in range(2)]
    mT = sb("mT", [128, KC, 512], BF16)
    ex2 = sb("ex2", [128, 512], F32)
    exb = [arena[:, 0:512], arena[:, 512:1024], ex2[:]]
    rt0 = sb("rt0", [128, 512], F32)
    rt1 = sb("rt1", [128, 512], F32)
    rtb = [rt0[:], rt1[:]]
    sga = [arena[:, 2048:2560], arena[:, 2560:3072]]
    sgb = [arena[:, 3072:3584], arena[:, 3584:4096]]
    m1b = arena[:, 1024:1536]
    m2b = arena[:, 1536:2048]

    psA = [ps(f"psA{i}", [128, 512], F32) for i in range(2)]
    psS = [ps(f"psS{i}", [128, 512], F32) for i in range(3)]
    psO = [ps(f"psO{i}", [128, 512], F32) for i in range(2)]
    psTf = ps("psT0", [128, 512], F32)
    psT = [psTf[:].bitcast(BF16)]

    def record(R, wseq_in):
        B = {}

        def buf(name):
            if name not in B:
                B[name] = Buf(name)
            return B[name]

        def blk_bufs(prefix, t0, t1):
            return [buf(f"{prefix}{e}") for e in range(t0 // 512, (t1 - 1) // 512 + 1)]

        class Pool:
            def __init__(self, name, tensors, bufs=None):
                self.t = tensors
                self.b = bufs if bufs is not None else [buf(f"{name}{i}") for i in range(len(tensors))]
                self.i = 0

            def next(self):
                k = self.i % len(self.t)
                self.i += 1
                return self.t[k], self.b[k]

        pZ = Pool("psA", psA)
        pS = Pool("psS", psS)
        pO = Pool("psO", psO)
        pA = Pool("psW", psA + psS + psO, pZ.b + pS.b + pO.b)
        pT = Pool("psT", psT)
        p_xt = Pool("xt", xt)
        p_xn = Pool("xn", xn)
        p_stat = Pool("stat", stat)
        p_sq = Pool("sq", sq)
        p_sd = Pool("sd", sd)
        p_qbn = Pool("qbn", qbn)
        p_cst = Pool("cst", cst)
        p_PT = Pool("PT", PT)
        p_sz = Pool("sz", szb)
        p_rd = Pool("rd", rdb)
        rd_half = [[buf(f"rd{i}h{j}") for j in range(2)] for i in range(2)]
        rd_swa = {"i": 0}
        p_ex = Pool("ex", exb)
        p_rt = Pool("rt", rtb)
        p_w = Pool("wbuf", wbuf)
        b_const = buf("const")
        b_Et = buf("Et")
        b_ones = buf("ones")
        b_sga = [buf("sga0"), buf("sga1")]
        b_sgb = [buf("sgb0"), buf("sgb1")]
        b_m1, b_m2 = buf("m1"), buf("m2")
        b_QaT, b_QbT = buf("QaT"), buf("QbT")
        b_yaT2 = [[buf(f"yaT{k}_{i}") for i in range(4)] for k in range(2)]
        b_ybT2 = [[buf(f"ybT{k}_{i}") for i in range(4)] for k in range(2)]
        b_mT = [buf(f"mT{i}") for i in range(KC)]
        b_wscr = [buf(f"wscr{i}") for i in range(NTILES)]
        b_setup = buf("setup")

        dma_rr = [0]

        def dma(out, in_, reads, writes, owner, queue="sp", lag=0.0):
            return R.op(queue, lambda e, o=out, i=in_: e.dma_start(out=o, in_=i), reads=reads, writes=writes, dma_owner=owner, lag=lag)

        R.t = -100.0
        dma(cbf[:], cbf_d[:, :], [], [b_const], b_const)
        dma(swam[:], swam_d[:, :], [], [b_const], b_const)
        dma(gbc[:], gbc_d[:, :], [], [b_const], b_const)
        dma(gains[:], gains_d[:, :], [], [b_const], b_const)
        dma(esink[:], sink_d[:, :], [], [b_const], b_const)
        dma(valid[:], valid_d[:, :], [], [b_const], b_const)
        R.op("dve", lambda e: e.memset(epsb[:], EPS), writes=[b_const])
        R.op("dve", lambda e: e.memset(cpow[:, 0:1], -0.5), writes=[b_const])
        R.op("dve", lambda e: e.memset(cpow[:, 1:2], -1.0), writes=[b_const])
        R.op("dve", lambda e: e.memset(cpow[:, 2:3], 1.0), writes=[b_const])
        R.op("act", lambda e: e.activation(out=esink[:], in_=esink[:], func=AF.Exp), reads=[b_const], writes=[b_const])
        R.op("pool", lambda e: e.memset(Va[:], 1.0), writes=[b_ones])
        R.op("pool", lambda e: e.memset(Vb[:], 1.0), writes=[b_ones])
        b_st = [buf("stage0"), buf("stage1")]
        stg = [arena[:, 0:2048], arena[:, 2048:4096]]
        NE = NVAR * 64
        dma(stg[1][:, 0:NE], Mk_d[:, :], [], [b_st[1]], b_st[1])
        for h in range(8):
            dma(stg[0][:, 0:NE], Gt_d[:, h * NE:(h + 1) * NE], [], [b_st[0]], b_st[0])
            R.op("act", lambda e: e.activation(out=stg[0][:, 0:NE], in_=stg[0][:, 0:NE], func=AF.Exp),
                 reads=[b_st[0]], writes=[b_st[0]])
            R.op("dve", lambda e, h=h: e.tensor_tensor(out=Et[:, h].rearrange("p v c -> p (v c)"), in0=stg[0][:, 0:NE],
                                                       in1=stg[1][:, 0:NE], op=ALU.mult),
                 reads=[b_st[0], b_st[1]], writes=[b_Et])

        w_in_r = w_in.rearrange("(kc p) c -> p kc c", p=128)
        w_o_r = w_o.rearrange("(kc p) c -> p kc c", p=128)
        w_oa_r = w_oa.rearrange("(kc p) c -> p kc c", p=128)
        w_ob_r = w_ob.rearrange("(kc p) c -> p kc c", p=128)

        def tile_srcs(t):
            def win(c0, n=256, dst0=0):
                return [(("k8", dst0, n), w_in_r[:, :, c0:c0 + n])]
            if t in T_KA:
                return win(C_KA + 256 * T_KA.index(t))
            if t == T_KBVB:
                return win(C_KB)
            if t in T_VA:
                return win(C_VA + 256 * T_VA.index(t))
            if t in T_QA:
                return win(C_QA + 256 * T_QA.index(t))
            if t in T_QB:
                i = T_QB.index(t)
                out = []
                for cc in range(2):
                    m = 2 * i + cc
                    out.append((("k8", cc * 128, 64), w_in_r[:, :, C_QB + 64 * m:C_QB + 64 * m + 64]))
                    out.append((("k8", cc * 128 + 64, 64), w_in_r[:, :, C_QB + 64 * (4 + m):C_QB + 64 * (4 + m) + 64]))
                return out
            if t in T_ZA:
                return win(C_ZA + 256 * T_ZA.index(t))
            if t in T_ZB:
                return win(C_ZB + 256 * T_ZB.index(t))
            if t in T_GA:
                return win(C_GA + 256 * T_GA.index(t))
            if t in T_GB:
                return win(C_GB + 256 * T_GB.index(t))
            if t in T_WAB:
                i = T_WAB.index(t)
                return [(("ab", 0), w_oa_r[:, :, 256 * i:256 * i + 256]), (("ab", 1), w_ob_r[:, :, 256 * i:256 * i + 256])]
            if t in T_WO:
                i = T_WO.index(t)
                return [(("k8", 0, 256), w_o_r[:, :, 256 * i:256 * i + 256])]
            raise ValueError

        cvt_eng = ["dve", "act"]
        b_cv = [buf("cv0"), buf("cv1")]
        for t in range(NTILES):
            s = t % 2
            st = stg[s]
            ckey = -100.0 if t < 5 else 2.0 + 1.4 * (t - 5)
            R.t = ckey - 1.4 if t >= 7 else ckey
            for (lay, src) in tile_srcs(t):
                if lay[0] == "k8":
                    dst = st.rearrange("p (k c) -> p k c", k=8)[:, :, lay[1]:lay[1] + lay[2]]
                else:
                    dst = st.rearrange("p (a k c) -> p a k c", a=2, k=4)[:, lay[1]]
                dma(dst, src, [], [b_st[s]], b_st[s])
            R.t = ckey
            wt = mT[:, 4 * s:4 * s + 4, :].rearrange("p k t -> p (k t)")
            wb = b_cv[s]
            ce = cvt_eng[t % 2]
            if ce == "act":
                R.op("act", lambda e, wt=wt, st=st: e.activation(out=wt, in_=st, func=AF.Copy), reads=[b_st[s]], writes=[wb])
            else:
                R.op(ce, lambda e, wt=wt, st=st: e.tensor_copy(out=wt, in_=st), reads=[b_st[s]], writes=[wb])
            dma(wscr[t], wt, [wb], [b_wscr[t]], wb)

        setup_bufs = [b_const, b_Et, b_ones, b_st[0], b_st[1]] + b_cv + b_wscr
        for e in ("pe", "act", "dve", "pool", "sp"):
            pass
        R.t = 2.0 + 1.4 * (NTILES - 5) + 0.5
        bar_tok = R.op("sp", lambda e: e.nop(), reads=setup_bufs, writes=[b_setup])
        arena_bufs = p_ex.b + b_sga + b_sgb + [b_m1, b_m2] + b_mT
        for bb in arena_bufs:
            bb.writers = [bar_tok]

        wseq = wseq_in
        wcalls = []
        wstate = {"issued": 0, "used": 0, "slots": []}

        def w_issue(t):
            wt, wb = p_w.next()
            dma(wt[:], wscr[t], [b_wscr[t]], [wb], wb)
            wstate["slots"].append((wt, wb, t))
            wstate["issued"] += 1

        def w_next(expect):
            wcalls.append(expect)
            if wseq is None:
                w_issue(expect)
            else:
                while wstate["issued"] < min(len(wseq), wstate["used"] + 2):
                    w_issue(wseq[wstate["issued"]])
            wt, wb, t = wstate["slots"][wstate["used"]]
            assert t == expect, (t, expect)
            wstate["used"] += 1
            return wt, wb

        clk = [0.0]

        def tick(d=1.0):
            clk[0] += d
            R.t = clk[0]
            return clk[0]

        def proj_fm(wt, wb, col0, hsl, hb, pst, psb, lag=0.0):
            wv = wt.rearrange("p (k c) -> p k c", k=8)
            for kc in range(KC):
                R.op("pe", lambda e, kc=kc: e.matmul(pst[:], lhsT=wv[:, kc, col0:col0 + 128], rhs=hsl(kc),
                                                   start=(kc == 0), stop=(kc == KC - 1)),
                     reads=[wb] + hb, writes=[psb], lag=lag)

        L1 = 1.5
        XLAG = -2.0

        def qknorm(pst, psb, gcol, out_ap, out_bufs, pool):
            sqt, sqb = p_sq.next()
            R.op("act", lambda e: e.activation(out=sqt[:], in_=pst[:], func=AF.Square), reads=[psb], writes=[sqb])
            ms, msb = pool.next()
            R.op("pe", lambda e: e.matmul(ms[:], lhsT=BDm, rhs=sqt[:], start=True, stop=True), reads=[sqb, b_const], writes=[msb], lag=L1)
            sdt, sdb = p_sd.next()
            R.op("act", lambda e: e.activation(out=sdt[:], in_=ms[:], func=AF.Ln, bias=epsb[:, 0:1]), reads=[msb, b_const],
                 writes=[sdb], lag=L1)
            R.op("act", lambda e: e.activation(out=sdt[:], in_=sdt[:], func=AF.Exp, scale=-0.5), reads=[sdb], writes=[sdb], lag=L1)
            R.op("dve", lambda e: e.scalar_tensor_tensor(out=out_ap, in0=pst[:], scalar=gains[:, gcol:gcol + 1], in1=sdt[:],
                                                         op0=ALU.mult, op1=ALU.mult),
                 reads=[psb, sdb, b_const], writes=out_bufs, lag=L1)

        def rotary(qn_t, qn_b, cs_t, cs_b, out_ap, out_bufs, pool):
            rp, rpb_ = pool.next()
            R.op("pe", lambda e: e.matmul(rp[:], lhsT=Perm, rhs=qn_t[:], start=True, stop=True), reads=[qn_b, b_const], writes=[rpb_], lag=L1 + 1)
            t1, t1b = p_rt.next()
            t2, t2b = p_rt.next()
            R.op("pool", lambda e: e.tensor_tensor(out=t1, in0=qn_t[:], in1=cs_t[:, 0, :], op=ALU.mult), reads=[qn_b, cs_b], writes=[t1b], lag=L1 + 1)
            R.op("dve", lambda e: e.tensor_tensor(out=t2, in0=rp[:], in1=cs_t[:, 1, :], op=ALU.mult), reads=[rpb_, cs_b], writes=[t2b], lag=L1 + 1)
            R.op("pool", lambda e: e.tensor_tensor(out=out_ap, in0=t1, in1=t2, op=ALU.add), reads=[t1b, t2b], writes=out_bufs, lag=L1 + 1)

        def xnorm_tile(s, e_, tt):
            tok = e_ * 512 + tt * 128
            hbuf = buf(f"hT{e_}")
            xtile, xb = p_xt.next()
            dma(xtile[:], xs[s, tok:tok + 128, :], [], [xb], xb, lag=XLAG)
            stt, stb = p_stat.next()
            xnt, xnb = p_xn.next()
            R.op("dve", lambda e: e.memset(stt[:], 0.0), writes=[stb])
            R.op("act", lambda e: e.activation(out=xnt[:], in_=xtile[:], func=AF.Square, accum_out=stt[:, 0:1]),
                 reads=[xb], writes=[xnb, stb])
            R.op("act", lambda e: e.activation(out=stt[:, 1:2], in_=stt[:, 0:1], func=AF.Ln, bias=epsb[:, 0:1], scale=1.0 / D),
                 reads=[stb, b_const], writes=[stb])
            R.op("act", lambda e: e.activation(out=stt[:, 2:3], in_=stt[:, 1:2], func=AF.Exp, scale=-0.5), reads=[stb], writes=[stb])
            R.op("dve", lambda e: e.scalar_tensor_tensor(out=xnt[:], in0=xtile[:], scalar=stt[:, 2:3], in1=gbc[:], op0=ALU.mult, op1=ALU.mult),
                 reads=[xb, stb, b_const], writes=[xnb])
            tp, tpb = pT.next()
            for kc in range(KC):
                R.op("pe", lambda e, kc=kc: e.transpose(tp[:, kc * 128:(kc + 1) * 128], xnt[:, kc * 128:(kc + 1) * 128], ident),
                     reads=[xnb, b_const], writes=[tpb], lag=L1)
            R.op("dve", lambda e: e.tensor_copy(out=hT[:, :, tok:tok + 128], in_=tp[:].rearrange("p (k t) -> p k t", k=KC)),
                 reads=[tpb], writes=[hbuf], lag=L1)

        def kv_items(s, e_, pool, xn_next):
            t0 = e_ * 512
            hbuf = buf(f"hT{e_}")
            hsl = lambda kc: hT[:, kc, t0:t0 + 512]
            tick()
            cst_t, cst_b = p_cst.next()
            dma(cst_t[:], cs_d[s, :, :, t0:t0 + 512], [], [cst_b], cst_b)
            for i in range(2):
                for cc in range(2):
                    wt, wb = w_next(T_KA[i])
                    c = 2 * i + cc
                    tick()
                    if xn_next:
                        xnorm_tile(s, e_ + 1, c)
                    pst, psb = pool.next()
                    proj_fm(wt, wb, cc * 128, hsl, [hbuf], pst, psb)
                    qknorm(pst, psb, 1, KaT[:, c, t0:t0 + 512], [buf(f"KaT{e_}")], pool)
                    yield
            tick()
            wt, wb = w_next(T_KBVB)
            pst, psb = pool.next()
            proj_fm(wt, wb, 0, hsl, [hbuf], pst, psb)
            qt, qb_ = p_qbn.next()
            qknorm(pst, psb, 3, qt[:], [qb_], pool)
            rotary(qt, qb_, cst_t, cst_b, KbT[:, t0:t0 + 512], [buf(f"KbT{e_}")], pool)
            wv = wt.rearrange("p (k c) -> p k c", k=8)
            tick(3)
            for tt in range(4):
                tick()
                tok = t0 + tt * 128
                pst, psb = pool.next()
                for kc in range(KC):
                    R.op("pe", lambda e, kc=kc, pst=pst, tok=tok, wv=wv: e.matmul(pst[:, 0:128], lhsT=hT[:, kc, tok:tok + 128],
                                                                                 rhs=wv[:, kc, 128:256], start=(kc == 0),
                                                                                 stop=(kc == KC - 1)),
                         reads=[wb, hbuf], writes=[psb])
                ti = tok // 128
                R.op("dve", lambda e, pst=pst, ti=ti: e.tensor_copy(
                    out=Vb[:, ti, 64:320].rearrange("p (a c) -> p a c", a=2)[:, :, 0:64],
                    in_=pst[:, 0:128].rearrange("p (a c) -> p a c", a=2)),
                    reads=[psb, b_ones], writes=[buf(f"Vb{e_}")])
            yield
            tick()
            wts = [w_next(T_VA[0]), w_next(T_VA[1])]
            for tt in range(4):
                tick()
                tok = t0 + tt * 128
                pst, psb = pool.next()
                for i in range(2):
                    wv = wts[i][0].rearrange("p (k c) -> p k c", k=8)
                    for kc in range(KC):
                        R.op("pe", lambda e, kc=kc, pst=pst, tok=tok, wv=wv, i=i: e.matmul(
                            pst[:, i * 256:(i + 1) * 256], lhsT=hT[:, kc, tok:tok + 128], rhs=wv[:, kc, :],
                            start=(kc == 0), stop=(kc == KC - 1)), reads=[wts[i][1], hbuf], writes=[psb])
                ti = tok // 128
                R.op("dve", lambda e, pst=pst, ti=ti: e.tensor_copy(
                    out=Va[:, ti, :].rearrange("p (hp a c) -> p hp a c", hp=4, a=3)[:, :, 0, :],
                    in_=pst[:].rearrange("p (hp a c) -> p hp a c", hp=4, a=2)[:, :, 0, :]),
                    reads=[psb, b_ones], writes=[buf(f"Va{e_}")])
                R.op("pool" if False else "act", lambda e, pst=pst, ti=ti: e.activation(
                    out=Va[:, ti, :].rearrange("p (hp a c) -> p hp a c", hp=4, a=3)[:, :, 2, :],
                    in_=pst[:].rearrange("p (hp a c) -> p hp a c", hp=4, a=2)[:, :, 1, :], func=AF.Copy),
                    reads=[psb, b_ones], writes=[buf(f"Va{e_}")])
            yield

        def run_all(gen):
            for _ in gen:
                pass

        def qproj(s, b_):
            q0 = 256 + 512 * b_
            hb = blk_bufs("hT", q0, q0 + 512)
            hsl = lambda kc: hT[:, kc, q0:q0 + 512]
            tick()
            cst_t, cst_b = p_cst.next()
            dma(cst_t[:], cs_d[s, :, :, q0:q0 + 512], [], [cst_b], cst_b)
            for i in range(2):
                wt, wb = w_next(T_QA[i])
                for cc in range(2):
                    c = 2 * i + cc
                    tick()
                    pst, psb = pA.next()
                    proj_fm(wt, wb, cc * 128, hsl, hb, pst, psb)
                    qknorm(pst, psb, 0, QaT[:, c, :], [b_QaT], pA)
            for i in range(2):
                wt, wb = w_next(T_QB[i])
                for cc in range(2):
                    m = 2 * i + cc
                    tick()
                    pst, psb = pA.next()
                    proj_fm(wt, wb, cc * 128, hsl, hb, pst, psb)
                    qt, qb_ = p_qbn.next()
                    qknorm(pst, psb, 2, qt[:], [qb_], pA)
                    rotary(qt, qb_, cst_t, cst_b, QbT[:, m, :], [b_QbT], pA)
            tick(3)

        def keep_warm(n):
            for _ in range(n):
                R.op("pe", lambda e: e.matmul(psTf[:, 0:384], lhsT=ident, rhs=cbf[:, 0:384], start=True, stop=True),
                     reads=[b_const], writes=[pT.b[0]])

        def attention(s, b_, filler):
            vbase = s * NVALID
            q0 = 256 + 512 * b_
            hb = blk_bufs("hT", q0, q0 + 512)
            hsl = lambda kc: hT[:, kc, q0:q0 + 512]
            yaT = yaT2[b_ % 2]
            ybT = ybT2[b_ % 2]
            b_yaT = b_yaT2[b_ % 2]
            b_ybT = b_ybT2[b_ % 2]

            def zproj(wt, wb, cc):
                pst, psb = pZ.next()
                proj_fm(wt, wb, cc * 128, hsl, hb, pst, psb)
                szt, szb_ = p_sz.next()
                ZL = 3.0
                R.op("act", lambda e: e.activation(out=szt[:], in_=pst[:], func=AF.Exp, scale=-1.0), reads=[psb], writes=[szb_], lag=ZL)
                R.op("act", lambda e: e.activation(out=szt[:], in_=szt[:], func=AF.Ln, bias=cpow[:, 2:3]), reads=[szb_, b_const], writes=[szb_], lag=ZL)
                R.op("act", lambda e: e.activation(out=szt[:], in_=szt[:], func=AF.Exp, scale=-1.0), reads=[szb_], writes=[szb_], lag=ZL)
                R.op("dve", lambda e: e.tensor_tensor(out=szt[:], in0=szt[:], in1=pst[:], op=ALU.mult),
                     reads=[szb_, psb], writes=[szb_], lag=ZL)
                return szt, szb_

            LAG = 6.0

            def row_groups(kap):
                bnd = None
                interior = []
                for rr in range(8):
                    r = 8 * b_ + rr
                    if r < 4:
                        if kap <= 5:
                            bnd = (0, 3)
                    elif r >= RS - 4:
                        if kap >= 2:
                            bnd = (4, 7)
                    else:
                        if 2 * kap - 7 <= rr <= 2 * kap + 1:
                            interior.append(rr)
                ig = (interior[0], interior[-1]) if interior else None
                return bnd, ig

            sz_next = None
            for i in range(2):
                for cc in range(2):
                    hp = 2 * i + cc
                    tick()
                    if sz_next is None:
                        wt, wb = w_next(T_ZA[i])
                        sz_next = zproj(wt, wb, cc)
                    szt, szb_ = sz_next
                    obank = [pO.next(), pO.next()]
                    for hh in range(2):
                        if hh == 1 and hp < 3:
                            wt, wb = w_next(T_ZA[(hp + 1) // 2])
                            sz_next = zproj(wt, wb, (hp + 1) % 2)
                        h = 2 * hp + hh
                        pb = 64 * hh
                        ot, ob = obank[hh]
                        first_pv = True
                        for kap in range(8):
                            tick()
                            bg, ig = row_groups(kap)
                            groups = [g_ for g_ in (bg, ig) if g_ is not None]
                            if not groups:
                                continue
                            ra = min(g_[0] for g_ in groups)
                            rb = max(g_[1] for g_ in groups)
                            nr = rb - ra + 1
                            assert sum(g_[1] - g_[0] + 1 for g_ in groups) == nr
                            kt = 512 * b_ + 128 * kap
                            ti = kt // 128
                            kb_ = blk_bufs("KaT", kt, kt + 128)
                            vb_ = blk_bufs("Va", kt, kt + 128)
                            keep_warm(2)
                            st_, stb_ = pS.next()
                            R.op("pe", lambda e, st_=st_, pb=pb, kt=kt, hp=hp, ra=ra, rb=rb, nr=nr: e.matmul(
                                st_[:, 0:64 * nr], lhsT=KaT[pb:pb + 64, hp, kt:kt + 128],
                                rhs=QaT[pb:pb + 64, hp, ra * 64:(rb + 1) * 64], start=True, stop=True),
                                reads=kb_ + [b_QaT], writes=[stb_])
                            ext_, exb_ = p_ex.next()
                            R.op("act", lambda e, st_=st_, ext_=ext_, nr=nr: e.activation(out=ext_[:, 0:nr * 64], in_=st_[:, 0:nr * 64],
                                                                                         func=AF.Exp, scale=0.125),
                                 reads=[stb_], writes=[exb_])
                            ptt, ptb = p_PT.next()
                            for (ga, gb_) in groups:
                                gn = gb_ - ga + 1
                                c0 = (ga - ra) * 64
                                dr0 = 2 * kap + 3 - ga
                                ex3 = ext_[:, c0:c0 + gn * 64].rearrange("p (j c) -> p j c", j=gn)
                                pt3 = ptt[:, c0:c0 + gn * 64].rearrange("p (j c) -> p j c", j=gn)
                                if (ga, gb_) == bg:
                                    v0 = 13 - dr0
                                    ev = Et[:, h, v0:v0 + gn, :]
                                    R.op("pool", lambda e, ex3=ex3, ev=ev: e.tensor_tensor(out=ex3, in0=ex3, in1=ev, op=ALU.mult),
                                         reads=[exb_, b_Et], writes=[exb_])
                                    kp = kap if ga == 0 else kap - 2
                                    vv = valid[:, vbase:vbase + 48].rearrange("p (r k) -> p r k", k=6)[:, ga:ga + gn, kp]
                                    vv = vv.unsqueeze(2).to_broadcast([128, gn, 64])
                                    R.op("dve", lambda e, pt3=pt3, ex3=ex3, vv=vv: e.tensor_tensor(out=pt3, in0=ex3, in1=vv, op=ALU.mult),
                                         reads=[exb_, b_const], writes=[ptb])
                                else:
                                    assert 2 <= dr0 - (gn - 1) and dr0 <= 10, (dr0, gn)
                                    v0 = 14 + 10 - dr0
                                    ev = Et[:, h, v0:v0 + gn, :]
                                    R.op("dve" if (kap % 2 == 0) else "pool",
                                         lambda e, pt3=pt3, ex3=ex3, ev=ev: e.tensor_tensor(out=pt3, in0=ex3, in1=ev, op=ALU.mult),
                                         reads=[exb_, b_Et], writes=[ptb])
                            R.op("pe", lambda e, ot=ot, ti=ti, hp=hp, hh=hh, ptt=ptt, ra=ra, rb=rb, nr=nr, fp=first_pv: e.matmul(
                                ot[:, ra * 64:(rb + 1) * 64], lhsT=Va[:, ti, hp * 192 + hh * 64:hp * 192 + hh * 64 + 128],
                                rhs=ptt[:, 0:64 * nr], start=fp, stop=False, skip_group_check=True),
                                reads=vb_ + [ptb, b_ones], writes=[ob], lag=LAG)
                            first_pv = False
                        po = 64 - pb
                        rdt, _unused = p_rd.next()
                        rdL = rd_half[rdb.index(rdt)]
                        R.op("act", lambda e, ot=ot, rdt=rdt, pb=pb, po=po: e.activation(out=rdt[pb:pb + 64, :], in_=ot[po:po + 64, :],
                                                                                       func=AF.Ln),
                             reads=[ob], writes=rdL, lag=LAG + 0.5)
                        R.op("act", lambda e, rdt=rdt, pb=pb: e.activation(out=rdt[pb:pb + 64, :], in_=rdt[pb:pb + 64, :], func=AF.Exp, scale=-1.0),
                             reads=rdL, writes=rdL, lag=LAG + 0.5)
                        R.op("pool", lambda e, rdt=rdt, szt=szt, pb=pb: e.tensor_tensor(out=rdt[pb:pb + 64, :], in0=rdt[pb:pb + 64, :],
                                                                                       in1=szt[pb:pb + 64, :], op=ALU.mult),
                             reads=rdL + [szb_], writes=rdL, lag=LAG + 0.5)
                        R.op("dve", lambda e, ot=ot, rdt=rdt, pb=pb, hp=hp: e.tensor_tensor(
                            out=yaT[pb:pb + 64, hp, :], in0=ot[pb:pb + 64, :], in1=rdt[pb:pb + 64, :], op=ALU.mult),
                            reads=[ob] + rdL, writes=[b_yaT[hp]], lag=LAG + 0.5)
                        filler()

            for g in range(2):
                tick(8)
                wt, wb = w_next(T_ZB[g])
                sz2 = [zproj(wt, wb, 0), zproj(wt, wb, 1)]
                pb = 64 * g
                for n_ in range(4):
                    tick(2)
                    qt0 = q0 + 128 * n_
                    pts = []
                    keep_warm(3)
                    for dl in (-1, 0, 1):
                        kt0 = qt0 + 128 * dl
                        st_, stb_ = pS.next()
                        R.op("pe", lambda e, st_=st_, kt0=kt0, pb=pb, n_=n_: e.matmul(
                            st_[:].rearrange("p (m q) -> p m q", m=4), lhsT=KbT[pb:pb + 64, kt0:kt0 + 128],
                            rhs=QbT[pb:pb + 64, :, n_ * 128:(n_ + 1) * 128], start=True, stop=True),
                            reads=blk_bufs("KbT", kt0, kt0 + 128) + [b_QbT], writes=[stb_])
                        ptt, ptb = p_PT.next()
                        if dl == 0:
                            R.op("act", lambda e, st_=st_, ptt=ptt: e.activation(out=ptt[:], in_=st_[:], func=AF.Exp, scale=0.125),
                                 reads=[stb_], writes=[ptb])
                        else:
                            ext_, exb_ = p_ex.next()
                            R.op("act", lambda e, st_=st_, ext_=ext_: e.activation(out=ext_, in_=st_[:], func=AF.Exp, scale=0.125),
                                 reads=[stb_], writes=[exb_])
                            mk = swam[:, 0:128] if dl == -1 else swam[:, 128:256]
                            mk4 = mk.unsqueeze(1).to_broadcast([128, 4, 128])
                            edge = (dl == -1 and b_ == 0 and n_ == 0) or (dl == 1 and b_ == NQB - 1 and n_ == 3)
                            ex3 = ext_.rearrange("p (m q) -> p m q", m=4)
                            pt3 = ptt[:].rearrange("p (m q) -> p m q", m=4)
                            if edge:
                                vc = vbase + 48 + (0 if dl == -1 else 1)
                                R.op("dve", lambda e, ex3=ex3, pt3=pt3, mk4=mk4, vc=vc: e.scalar_tensor_tensor(
                                    out=pt3, in0=ex3, scalar=valid[:, vc:vc + 1], in1=mk4, op0=ALU.mult, op1=ALU.mult),
                                    reads=[exb_, b_const], writes=[ptb])
                            else:
                                R.op("pool" if dl == -1 else "dve",
                                     lambda e, ex3=ex3, pt3=pt3, mk4=mk4: e.tensor_tensor(out=pt3, in0=ex3, in1=mk4, op=ALU.mult),
                                     reads=[exb_, b_const], writes=[ptb])
                        pts.append((ptt, ptb, kt0))
                    ot, ob = pO.next()
                    for par in range(2):
                        vcol = g * 128 + 64 if par == 0 else g * 128
                        for k_, (ptt, ptb, kt0) in enumerate(pts):
                            ti = kt0 // 128
                            rhs = ptt[:].rearrange("p (a b q) -> p a b q", a=2, b=2)[:, :, par, :]
                            R.op("pe", lambda e, ot=ot, ti=ti, vcol=vcol, rhs=rhs, par=par, k_=k_: e.matmul(
                                ot[:, par * 256:(par + 1) * 256].rearrange("p (a q) -> p a q", a=2), lhsT=Vb[:, ti, vcol:vcol + 128],
                                rhs=rhs, start=(k_ == 0), stop=(k_ == 2)),
                                reads=blk_bufs("Vb", kt0, kt0 + 128) + [ptb, b_ones], writes=[ob], lag=2.5)
                    for par in range(2):
                        pbo = 64 * par
                        pde = 64 - pbo
                        k4 = rd_swa["i"] % 4
                        rd_swa["i"] += 1
                        rdt = rdb[k4 // 2]
                        co = 256 * (k4 % 2)
                        rdb_ = rd_half[k4 // 2][k4 % 2]
                        for a in range(2):
                            h = 4 * g + 2 * a + par
                            R.op("act", lambda e, ot=ot, rdt=rdt, pbo=pbo, pde=pde, par=par, a=a, h=h, co=co: e.activation(
                                out=rdt[pbo:pbo + 64, co + a * 128:co + (a + 1) * 128],
                                in_=ot[pde:pde + 64, par * 256 + a * 128:par * 256 + (a + 1) * 128],
                                func=AF.Ln, bias=esink[pbo:pbo + 64, h:h + 1]), reads=[ob, b_const], writes=[rdb_], lag=3.0)
                        R.op("act", lambda e, rdt=rdt, pbo=pbo, co=co: e.activation(out=rdt[pbo:pbo + 64, co:co + 256], in_=rdt[pbo:pbo + 64, co:co + 256],
                                                                            func=AF.Exp, scale=-1.0),
                             reads=[rdb_], writes=[rdb_], lag=3.0)
                        for a in range(2):
                            szt, szb_ = sz2[a]
                            R.op("pool", lambda e, rdt=rdt, szt=szt, pbo=pbo, a=a, n_=n_, co=co: e.tensor_tensor(
                                out=rdt[pbo:pbo + 64, co + a * 128:co + (a + 1) * 128], in0=rdt[pbo:pbo + 64, co + a * 128:co + (a + 1) * 128],
                                in1=szt[pbo:pbo + 64, n_ * 128:(n_ + 1) * 128], op=ALU.mult),
                                reads=[rdb_, szb_], writes=[rdb_], lag=3.0)
                        R.op("dve", lambda e, ot=ot, rdt=rdt, pbo=pbo, par=par, g=g, n_=n_, co=co: e.tensor_tensor(
                            out=ybT[pbo:pbo + 64, 2 * g:2 * g + 2, n_ * 128:(n_ + 1) * 128],
                            in0=ot[pbo:pbo + 64, par * 256:(par + 1) * 256].rearrange("p (a q) -> p a q", a=2),
                            in1=rdt[pbo:pbo + 64, co:co + 256].rearrange("p (a q) -> p a q", a=2), op=ALU.mult),
                            reads=[ob, rdb_], writes=[b_ybT[2 * g], b_ybT[2 * g + 1]], lag=3.0)
                    if n_ % 2 == 1:
                        filler()
            tick(4)

        def merge_items(s, b_, pool):
            q0 = 256 + 512 * b_
            hb = blk_bufs("hT", q0, q0 + 512)
            hsl = lambda kc: hT[:, kc, q0:q0 + 512]
            yaT = yaT2[b_ % 2]
            ybT = ybT2[b_ % 2]
            b_yaT = b_yaT2[b_ % 2]
            b_ybT = b_ybT2[b_ % 2]
            for i in range(4):
                tick()
                wga, wgab = w_next(T_GA[i])
                for cc in range(2):
                    pga, pgab = pool.next()
                    proj_fm(wga, wgab, cc * 128, hsl, hb, pga, pgab)
                    R.op("act", lambda e, pga=pga, cc=cc: e.activation(out=sga[cc], in_=pga[:], func=AF.Exp, scale=-1.0),
                         reads=[pgab], writes=[b_sga[cc]])
                    R.op("act", lambda e, cc=cc: e.activation(out=sga[cc], in_=sga[cc], func=AF.Ln, bias=cpow[:, 2:3]),
                         reads=[b_sga[cc], b_const], writes=[b_sga[cc]])
                    R.op("act", lambda e, cc=cc: e.activation(out=sga[cc], in_=sga[cc], func=AF.Exp, scale=-1.0),
                         reads=[b_sga[cc]], writes=[b_sga[cc]])
                tick()
                wgb, wgbb = w_next(T_GB[i])
                for cc in range(2):
                    pgb, pgbb = pool.next()
                    proj_fm(wgb, wgbb, cc * 128, hsl, hb, pgb, pgbb)
                    R.op("act", lambda e, pgb=pgb, cc=cc: e.activation(out=sgb[cc], in_=pgb[:], func=AF.Exp, scale=-1.0),
                         reads=[pgbb], writes=[b_sgb[cc]])
                    R.op("act", lambda e, cc=cc: e.activation(out=sgb[cc], in_=sgb[cc], func=AF.Ln, bias=cpow[:, 2:3]),
                         reads=[b_sgb[cc], b_const], writes=[b_sgb[cc]])
                    R.op("act", lambda e, cc=cc: e.activation(out=sgb[cc], in_=sgb[cc], func=AF.Exp, scale=-1.0),
                         reads=[b_sgb[cc]], writes=[b_sgb[cc]])
                tick()
                wab, wabb = w_next(T_WAB[i])
                wabv = wab.rearrange("p (a k c) -> p a k c", a=2, k=4)
                for cc in range(2):
                    c = 2 * i + cc
                    pa, pab = pool.next()
                    for kc in range(4):
                        R.op("pe", lambda e, kc=kc, pa=pa, cc=cc, wabv=wabv: e.matmul(pa[:], lhsT=wabv[:, 0, kc, cc * 128:(cc + 1) * 128],
                                                                                     rhs=yaT[:, kc, :], start=(kc == 0), stop=(kc == 3)),
                             reads=[wabb] + b_yaT, writes=[pab])
                    R.op("dve", lambda e, pa=pa, cc=cc: e.tensor_tensor(out=m1b, in0=pa[:], in1=sga[cc], op=ALU.mult),
                         reads=[pab, b_sga[cc]], writes=[b_m1])
                    pb2, pbb = pool.next()
                    for kc in range(4):
                        R.op("pe", lambda e, kc=kc, pb2=pb2, cc=cc, wabv=wabv: e.matmul(pb2[:], lhsT=wabv[:, 1, kc, cc * 128:(cc + 1) * 128],
                                                                                       rhs=ybT[:, kc, :], start=(kc == 0), stop=(kc == 3)),
                             reads=[wabb] + b_ybT, writes=[pbb])
                    R.op("dve", lambda e, pb2=pb2, cc=cc: e.tensor_tensor(out=m2b, in0=pb2[:], in1=sgb[cc], op=ALU.mult),
                         reads=[pbb, b_sgb[cc]], writes=[b_m2])
                    R.op("pool", lambda e, c=c: e.tensor_tensor(out=mT[:, c, :], in0=m1b, in1=m2b, op=ALU.add),
                         reads=[b_m1, b_m2], writes=[b_mT[c]])
                yield
            tick()
            xres = []
            for tt in range(4):
                xtile, xb = p_xt.next()
                dma(xtile[:], xs[s, q0 + tt * 128:q0 + (tt + 1) * 128, :], [], [xb], xb)
                xres.append((xtile, xb))
            for cg in range(4):
                tick()
                wo_t, wo_b = w_next(T_WO[cg])
                wov = wo_t.rearrange("p (k c) -> p k c", k=8)
                for tt in range(4):
                    pst, psb = pool.next()
                    for kc in range(KC):
                        R.op("pe", lambda e, kc=kc, pst=pst, tt=tt, wov=wov: e.matmul(
                            pst[:, 0:256], lhsT=mT[:, kc, tt * 128:(tt + 1) * 128], rhs=wov[:, kc, :],
                            start=(kc == 0), stop=(kc == KC - 1)), reads=[wo_b] + b_mT, writes=[psb])
                    xtile, xb = xres[tt]
                    R.op("dve", lambda e, pst=pst, xtile=xtile, cg=cg: e.tensor_tensor(
                        out=xtile[:, cg * 256:(cg + 1) * 256], in0=pst[:, 0:256], in1=xtile[:, cg * 256:(cg + 1) * 256],
                        op=ALU.add), reads=[psb, xb], writes=[xb])
                yield
            tick()
            for tt in range(4):
                xtile, xb = xres[tt]
                r0 = 512 * b_ + 128 * tt
                tk = dma(y[s, r0:r0 + 128, :], xtile[:], [xb], [], xb)
                R.store_toks.append(tk)
            yield

        clk[0] = 0.0
        for s in range(nslot):
            tick(2)
            for tt in range(4):
                tick()
                xnorm_tile(s, 0, tt)
            tick(2)
            run_all(kv_items(s, 0, pA, True))
            run_all(kv_items(s, 1, pA, True))
            pending = [kv_items(s, 2, pZ, False)]

            def filler():
                while pending:
                    try:
                        next(pending[0])
                        tick(3)
                        return
                    except StopIteration:
                        pending.pop(0)

            def drain():
                while pending:
                    filler()

            for b_ in range(NQB):
                qproj(s, b_)
                attention(s, b_, filler)
                drain()
                if b_ + 1 < NQB:
                    pending.append(merge_items(s, b_, pZ))
                else:
                    run_all(merge_items(s, b_, pA))

        tick(10)

        if dbg and wseq_in is not None:
            allb = list(B.values())
            def dump(name, t, shape, dt):
                o = nc.dram_tensor(name, list(shape), dt, kind="ExternalOutput").ap()
                tk = dma(o, t, allb, [], buf("dbg_" + name))
                R.store_toks.append(tk)
            dump("d_hT", hT[:].rearrange("p k t -> p (k t)"), [128, KC * EXT], BF16)
            dump("d_KaT", KaT[:].rearrange("p k t -> p (k t)"), [128, 4 * EXT], BF16)
            dump("d_KbT", KbT[:], [128, EXT], BF16)
            dump("d_Va", Va[:].rearrange("p k t -> p (k t)"), [128, NT * 768], BF16)
            dump("d_Vb", Vb[:].rearrange("p k t -> p (k t)"), [128, NT * 320], BF16)
            dump("d_QaT", QaT[:].rearrange("p k t -> p (k t)"), [128, 4 * 512], BF16)
            dump("d_QbT", QbT[:].rearrange("p k t -> p (k t)"), [128, 4 * 512], BF16)
            dump("d_yaT", yaT2[(NQB - 1) % 2][:].rearrange("p k t -> p (k t)"), [128, 4 * 512], BF16)
            dump("d_ybT", ybT2[(NQB - 1) % 2][:].rearrange("p k t -> p (k t)"), [128, 4 * 512], BF16)
            dump("d_mT", mT[:].rearrange("p k t -> p (k t)"), [128, KC * 512], BF16)
            dump("d_Et", Et[:].rearrange("p h v c -> p (h v c)"), [128, 8 * NVAR * 64], BF16)

        last = {}
        for t in R.store_toks:
            if id(t.sem) not in last or t.order > last[id(t.sem)].order:
                last[id(t.sem)] = t
        R.op("sp", lambda e: e.nop(), extra=list(last.values()))


        return wcalls

    wseq_real = record(Rec(), None)
    record(R, wseq_real)

    R.finalize()
    for e in R.ENGS:
        R.sems[e].handle = es.enter_context(nc.semaphore(R.sems[e].name))
    for sdm in R.dsems:
        sdm.handle = es.enter_context(nc.semaphore(sdm.name))
    with nc.Block() as block:
        @block.sync
        def _(eng):
            R.emit("sp", eng)

        @block.tensor
        def _(eng):
            R.emit("pe", eng)

        @block.scalar
        def _(eng):
            R.emit("act", eng)

        @block.vector
        def _(eng):
            R.emit("dve", eng)

        @block.gpsimd
        def _(eng):
            R.emit("pool", eng)
    es.close()
    return nc


def _variants():
    v = [(13 - i, 1, 1) for i in range(14)]
    v += [(10, 1, 0)] + [(d, 1, 1) for d in range(9, 2, -1)] + [(2, 0, 1)]
    return v


def _host_constants():
    p = np.arange(128)
    ck = p % 64
    half = p // 64
    cq = np.arange(64)
    cs_ = np.clip(cq - 8, 0, 48)
    col_in = (ck[:, None] >= cs_[None, :]) & (ck[:, None] < cs_[None, :] + 16)
    dc = np.clip(ck[:, None] - cq[None, :], -15, 15) + 15
    var = _variants()
    dr_idx = np.stack([np.clip(d + half, 0, 14) for (d, lo, up) in var], axis=1)
    Mk = np.zeros((128, NVAR, 64), np.float32)
    for vi, (d, lo, up) in enumerate(var):
        hv = np.where(half == 0, lo, up).astype(np.float32)
        Mk[:, vi, :] = col_in.astype(np.float32) * hv[:, None]
    ident = np.eye(128, dtype=np.float32)
    BD = np.zeros((128, 128), np.float32)
    BD[:64, :64] = 1.0 / 64
    BD[64:, 64:] = 1.0 / 64
    partner = np.where((p % 64) < 32, p + 32, p - 32)
    Perm = np.zeros((128, 128), np.float32)
    Perm[partner, p] = 1.0
    cbf = np.concatenate([ident, BD, Perm], axis=1).astype(ml_dtypes.bfloat16)
    j = np.arange(128)[:, None]
    i = np.arange(128)[None, :]
    swam = np.concatenate([(j >= i), (j <= i)], axis=1).astype(np.float32)
    return dict(dr_idx=dr_idx, dc=dc, Mk=Mk.reshape(128, NVAR * 64), cbf=cbf, swam=swam)


def _slot_tables(row0):
    p = np.arange(128)
    half = p // 64
    val = np.zeros((128, NVALID), np.float32)
    for bi in range(8):
        r = bi if bi < 4 else RS - 4 + (bi - 4)
        start_local = -4 if bi < 4 else RS - 8
        R_ = row0 + r
        w0 = min(max(R_ - 4, 0), NROWS - 8)
        for j in range(6):
            kr = row0 + start_local + 2 * j + half
            val[:, bi * 6 + j] = ((kr >= w0) & (kr < w0 + 8)).astype(np.float32)
    val[:, 48] = 1.0 if row0 > 0 else 0.0
    val[:, 49] = 1.0 if row0 + RS < NROWS else 0.0
    pos = (row0 - 4) * GW + np.arange(EXT)
    halfd = 32
    inv = (np.float32(10000.0) ** (-(np.arange(halfd, dtype=np.float32)) / np.float32(halfd))).astype(np.float32)
    ang = pos.astype(np.float32)[None, :] * inv[:, None]
    cos = np.cos(ang).astype(np.float32)
    sin = np.sin(ang).astype(np.float32)
    d = p % 64
    cs = np.zeros((128, 2, EXT), np.float32)
    cs[:, 0, :] = cos[d % 32]
    cs[:, 1, :] = sin[d % 32] * np.where(d < 32, -1.0, 1.0)[:, None].astype(np.float32)
    return val, cs


_PROG = {}


def _make_in_maps(inputs, slot_list_per_core):
    hc = _host_constants()
    x_all = [np.asarray(inputs["x_prompt"], np.float32), np.asarray(inputs["x_sample"], np.float32)]
    seqs = [x_all[0][i] for i in range(x_all[0].shape[0])] + [x_all[1][i] for i in range(x_all[1].shape[0])]
    rpb = np.asarray(inputs["rpb_a"], np.float32)[0]
    Gt = rpb[:, hc["dr_idx"][:, :, None], hc["dc"][:, None, :]]
    Gt = np.ascontiguousarray(np.transpose(Gt, (1, 0, 2, 3))).reshape(128, 8 * NVAR * 64)
    p = np.arange(128)
    gains = np.stack([np.asarray(inputs[k], np.float32)[0][p % 64] for k in ("qn_a", "kn_a", "qn_b", "kn_b")], axis=1)
    common = {
        "w_in": np.ascontiguousarray(np.asarray(inputs["w_in"], np.float32)[0]),
        "w_out_a": np.ascontiguousarray(np.asarray(inputs["w_out_a"], np.float32)[0]),
        "w_out_b": np.ascontiguousarray(np.asarray(inputs["w_out_b"], np.float32)[0]),
        "w_o": np.ascontiguousarray(np.asarray(inputs["w_o"], np.float32)[0]),
        "gbc": np.ascontiguousarray(np.broadcast_to(np.asarray(inputs["norm_g"], np.float32)[0][None, :], (128, D))),
        "gains": np.ascontiguousarray(gains),
        "sinkb": np.ascontiguousarray(np.broadcast_to(np.asarray(inputs["sink_b"], np.float32)[0][None, :], (128, 8))),
        "Gt": Gt, "Mk": hc["Mk"], "cbf": hc["cbf"], "swam": hc["swam"],
    }
    tabs = {}
    in_maps = []
    for slots in slot_list_per_core:
        ns = len(slots)
        xs = np.zeros((ns, EXT, D), np.float32)
        valid = np.zeros((128, ns * NVALID), np.float32)
        cs = np.zeros((ns, 128, 2, EXT), np.float32)
        for si, (sq_, row0) in enumerate(slots):
            lo = (row0 - 4) * GW
            hi = lo + EXT
            a, b = max(lo, 0), min(hi, SEQ)
            xs[si, a - lo:b - lo] = seqs[sq_][a:b]
            if row0 not in tabs:
                tabs[row0] = _slot_tables(row0)
            valid[:, si * NVALID:(si + 1) * NVALID] = tabs[row0][0]
            cs[si] = tabs[row0][1]
        m = dict(common)
        m.update({"xs": xs, "valid": valid, "cs": cs})
        in_maps.append(m)
    return in_maps


def kernel(**inputs):
    nseq_p = np.asarray(inputs["x_prompt"]).shape[0]
    nseq_s = np.asarray(inputs["x_sample"]).shape[0]
    nseq = nseq_p + nseq_s
    qper = NROWS // RS
    all_slots = [(sq_, q * RS) for sq_ in range(nseq) for q in range(qper)]
    assert len(all_slots) == NCORES * NSLOT
    per_core = [all_slots[c * NSLOT:(c + 1) * NSLOT] for c in range(NCORES)]
    in_maps = _make_in_maps(inputs, per_core)
    if "nc" not in _PROG:
        _PROG["nc"] = build_program(NSLOT)
    res = run_bass_kernel_spmd(_PROG["nc"], in_maps, core_ids=list(range(NCORES)))
    outs = [np.zeros((SEQ, D), np.float32) for _ in range(nseq)]
    for c in range(NCORES):
        yc = res.results[c]["y"]
        for si, (sq_, row0) in enumerate(per_core[c]):
            outs[sq_][row0 * GW:(row0 + RS) * GW] = yc[si]
    y_prompt = np.stack(outs[:nseq_p], axis=0)
    y_sample = np.stack(outs[nseq_p:], axis=0)
    return (y_prompt, y_sample)
```

```python
import contextlib
import numpy as np
import ml_dtypes
import concourse.bass as bass
import concourse.mybir as mybir
from concourse.bass_utils import run_bass_kernel_spmd

F32 = mybir.dt.float32
BF16 = mybir.dt.bfloat16
AF = mybir.ActivationFunctionType
ALU = mybir.AluOpType

D = 1024
KC = 8
SEQ = 4096
GW = 64
NROWS = 64
RS = 16
EXTR = RS + 8
EXT = EXTR * GW
QS = RS * GW
NQB = QS // 512
NEB = EXT // 512
NT = EXT // 128
NCORES = 8
NSLOT = (12 * NROWS // RS) // NCORES
NVAR = 23
NVALID = 8 * 6 + 2
EPS = 1e-6

C_QA, C_KA, C_VA, C_ZA, C_QB, C_KB, C_VB, C_ZB, C_GA, C_GB = 0, 512, 1024, 1536, 2048, 2560, 2688, 2816, 3328, 4352

T_KA = [0, 1]
T_KBVB = 2
T_VA = [3, 4]
T_QA = [5, 6]
T_QB = [7, 8]
T_ZA = [9, 10]
T_ZB = [11, 12]
T_GA = [13, 14, 15, 16]
T_GB = [17, 18, 19, 20]
T_WAB = [21, 22, 23, 24]
T_WO = [25, 26, 27, 28]
NTILES = 29


class Sem:
    def __init__(self, name, is_dma):
        self.name = name
        self.is_dma = is_dma
        self.handle = None
        self.count = 0


class Tok:
    __slots__ = ("sem", "order", "value", "op")

    def __init__(self, sem, order, value=None, op=None):
        self.sem = sem
        self.order = order
        self.value = value
        self.op = op


class Buf:
    def __init__(self, name):
        self.name = name
        self.writers = []
        self.readers = []
        self.dsem = None


class Op:
    __slots__ = ("fn", "deps", "tok", "signal", "is_dma", "key", "idx", "eng", "info")

    def __init__(self, fn, deps, tok, is_dma, key, idx, eng):
        self.fn = fn
        self.deps = deps
        self.tok = tok
        self.signal = False
        self.is_dma = is_dma
        self.key = key
        self.idx = idx
        self.eng = eng


class Rec:
    ENGS = ("pe", "act", "dve", "pool", "sp")

    def __init__(self):
        self.ops = {e: [] for e in self.ENGS}
        self.sems = {e: Sem("s_" + e, False) for e in self.ENGS}
        self.dsems = []
        self.store_toks = []
        self.t = 0.0
        self.nrec = 0

    def _merge(self, deps, t):
        k = id(t.sem)
        if k not in deps or t.order > deps[k].order:
            deps[k] = t

    def _trim(self, lst, t):
        for x in lst:
            if x.sem is t.sem and x.order > t.order:
                return
        lst[:] = [x for x in lst if x.sem is not t.sem]
        lst.append(t)

    def op(self, eng, fn, reads=(), writes=(), dma_owner=None, extra=(), lag=0.0):
        deps = {}
        for b in reads:
            for t in b.writers:
                self._merge(deps, t)
        for b in writes:
            for t in b.readers:
                self._merge(deps, t)
            for t in b.writers:
                self._merge(deps, t)
        for t in extra:
            self._merge(deps, t)
        if dma_owner is not None:
            if dma_owner.dsem is None:
                dma_owner.dsem = Sem("d_" + dma_owner.name, True)
                self.dsems.append(dma_owner.dsem)
            s = dma_owner.dsem
            s.count += 16
            tok = Tok(s, (float(s.count), 0), s.count)
            self.nrec += 1
        else:
            self.nrec += 1
            tok = Tok(self.sems[eng], (self.t + lag, self.nrec))
        o = Op(fn, list(deps.values()), tok, dma_owner is not None, self.t + lag, self.nrec, eng)
        tok.op = o
        o.info = ([b.name for b in reads], [b.name for b in writes])
        self.ops[eng].append(o)
        wset = set(id(b) for b in writes)
        for b in writes:
            if b.readers:
                b.writers = [tok]
                b.readers = []
            else:
                self._trim(b.writers, tok)
        for b in reads:
            if id(b) not in wset:
                self._trim(b.readers, tok)
        return tok

    def barrier(self, bufs):
        pass

    def finalize(self):
        for e in self.ENGS:
            self.ops[e].sort(key=lambda o: (o.key, o.idx))
            for o in self.ops[e]:
                for t in o.deps:
                    if t.op is not None:
                        assert (t.op.key, t.op.idx) < (o.key, o.idx), ("non-monotone dep", e, o.key, o.idx, o.info, t.op.eng, t.op.key, t.op.idx, t.op.info)
        for e in self.ENGS:
            for o in self.ops[e]:
                for t in o.deps:
                    if t.op is not None and not t.op.is_dma:
                        if not (e == "pe" and t.sem is self.sems["pe"]):
                            t.op.signal = True
        for e in self.ENGS:
            c = 0
            for o in self.ops[e]:
                if not o.is_dma and o.signal:
                    c += 1
                    o.tok.value = c

    def emit(self, eng_name, eng):
        seen = {}
        own = self.sems[eng_name]
        n_wait = 0
        for o in self.ops[eng_name]:
            for t in o.deps:
                if eng_name == "pe" and t.sem is own:
                    continue
                v = t.value
                assert v is not None
                if seen.get(id(t.sem), 0) >= v:
                    continue
                seen[id(t.sem)] = v
                eng.wait_ge(t.sem.handle, v)
                n_wait += 1
            ins = o.fn(eng)
            if o.is_dma:
                ins.then_inc(o.tok.sem.handle, 16)
            elif o.signal:
                ins.then_inc(own.handle, 1)
        return n_wait


def build_program(nslot=NSLOT, dbg=False):
    nc = bass.Bass("TRN2", target_bir_lowering=False)
    R = Rec()
    es = contextlib.ExitStack()

    def dram_in(name, shape, dt=F32):
        return nc.dram_tensor(name, list(shape), dt, kind="ExternalInput").ap()

    xs = dram_in("xs", [nslot, EXT, D])
    w_in = dram_in("w_in", [D, 5376])
    w_oa = dram_in("w_out_a", [512, D])
    w_ob = dram_in("w_out_b", [512, D])
    w_o = dram_in("w_o", [D, D])
    gcol_d = dram_in("gcol", [128, KC])
    gains_d = dram_in("gains", [128, 4])
    sink_d = dram_in("sinkb", [128, 8])
    Gt_d = dram_in("Gt", [128, 8 * NVAR * 64])
    Mk_d = dram_in("Mk", [128, NVAR * 64])
    cbf_d = dram_in("cbf", [128, 3 * 128], BF16)
    swam_d = dram_in("swam", [128, 256])
    valid_d = dram_in("valid", [128, nslot * NVALID])
    cs_d = dram_in("cs", [nslot, 128, 2, EXT])
    y = nc.dram_tensor("y", [nslot, QS, D], F32, kind="ExternalOutput").ap()
    wscr = nc.dram_tensor("wscr", [NTILES, 128, 2048], BF16).ap()

    def sb(name, shape, dt):
        return es.enter_context(nc.sbuf_tensor(name, list(shape), dt))

    def ps(name, shape, dt):
        return es.enter_context(nc.psum_tensor(name, list(shape), dt))

    hT = sb("hT", [128, KC, EXT], BF16)
    KaT = sb("KaT", [128, 4, EXT], BF16)
    KbT = sb("KbT", [128, EXT], BF16)
    Va = sb("Va", [128, NT, 768], BF16)
    Vb = sb("Vb", [128, NT, 320], BF16)
    Et = sb("Et", [128, 8, NVAR, 64], BF16)
    cbf = sb("cbf_s", [128, 3 * 128], BF16)
    ident = cbf[:, 0:128]
    BDm = cbf[:, 128:256]
    Perm = cbf[:, 256:384]
    swam = sb("swam_s", [128, 256], F32)
    gcol = sb("gcol_s", [128, KC], F32)
    gains = sb("gains_s", [128, 4], F32)
    esink = sb("esink", [128, 8], F32)
    valid = sb("valid_s", [128, nslot * NVALID], F32)
    epsb = sb("epsb", [128, 1], F32)
    cpow = sb("cpow", [128, 3], F32)
    wbuf = [sb(f"wbuf{i}", [128, 2048], BF16) for i in range(3)]
    arena = sb("arena", [128, 4096], F32)
    xt = [sb(f"xt{i}", [128, D], F32) for i in range(4)]
    xn = [sb(f"xn{i}", [128, D], BF16) for i in range(2)]
    stat = [sb(f"stat{i}", [128, 4], F32) for i in range(2)]
    sq = [sb(f"sq{i}", [128, 512], BF16) for i in range(2)]
    sd = [sb(f"sd{i}", [128, 512], F32) for i in range(2)]
    qbn = [sb(f"qbn{i}", [128, 512], BF16) for i in range(2)]
    cst = [sb(f"cst{i}", [128, 2, 512], F32) for i in range(1)]
    QaT = sb("QaT", [128, 4, 512], BF16)
    QbT = sb("QbT", [128, 4, 512], BF16)
    PT = [sb(f"PT{i}", [128, 512], BF16) for i in range(6)]
    szb = [sb(f"sz{i}", [128, 512], F32) for i in range(2)]
    rdb = [sb(f"rd{i}", [128, 512], F32) for i in range(2)]
    yaT2 = [sb(f"yaT{i}", [128, 4, 512], BF16) for i in range(2)]
    ybT2 = [sb(f"ybT{i}", [128, 4, 512], BF16) for i in range(2)]
    mT = sb("mT", [128, KC, 512], BF16)
    ex2 = sb("ex2", [128, 512], F32)
    ex3 = sb("ex3", [128, 512], F32)
    exb = [arena[:, 0:512], arena[:, 512:1024], ex2[:], ex3[:]]
    rt0 = sb("rt0", [128, 512], F32)
    rt1 = sb("rt1", [128, 512], F32)
    rtb = [rt0[:], rt1[:]]
    sga = [arena[:, 2048:2560], arena[:, 2560:3072]]
    sgb = [arena[:, 3072:3584], arena[:, 3584:4096]]
    m1b = arena[:, 1024:1536]
    m2b = arena[:, 1536:2048]

    psA = [ps(f"psA{i}", [128, 512], F32) for i in range(2)]
    psS = [ps(f"psS{i}", [128, 512], F32) for i in range(3)]
    psO = [ps(f"psO{i}", [128, 512], F32) for i in range(2)]
    psTf = ps("psT0", [128, 512], F32)
    psT = [psTf[:].bitcast(BF16)]

    def record(R, wseq_in):
        B = {}

        def buf(name):
            if name not in B:
                B[name] = Buf(name)
            return B[name]

        def blk_bufs(prefix, t0, t1):
            return [buf(f"{prefix}{e}") for e in range(t0 // 512, (t1 - 1) // 512 + 1)]

        class Pool:
            def __init__(self, name, tensors, bufs=None):
                self.t = tensors
                self.b = bufs if bufs is not None else [buf(f"{name}{i}") for i in range(len(tensors))]
                self.i = 0

            def next(self):
                k = self.i % len(self.t)
                self.i += 1
                return self.t[k], self.b[k]

        pZ = Pool("psA", psA)
        pS = Pool("psS", psS)
        pO = Pool("psO", psO)
        pA = Pool("psW", psA + psS + psO, pZ.b + pS.b + pO.b)
        pT = Pool("psT", psT)
        p_xt = Pool("xt", xt)
        p_xn = Pool("xn", xn)
        p_stat = Pool("stat", stat)
        p_sq = Pool("sq", sq)
        p_sd = Pool("sd", sd)
        p_qbn = Pool("qbn", qbn)
        p_cst = Pool("cst", cst)
        p_PT = Pool("PT", PT)
        p_sz = Pool("sz", szb)
        p_rd = Pool("rd", rdb)
        rd_half = [[buf(f"rd{i}h{j}") for j in range(2)] for i in range(2)]
        rd_swa = {"i": 0}
        p_ex = Pool("ex", exb)
        p_rt = Pool("rt", rtb)
        p_w = Pool("wbuf", wbuf)
        b_const = buf("const")
        b_Et = buf("Et")
        b_ones = buf("ones")
        b_sga = [buf("sga0"), buf("sga1")]
        b_sgb = [buf("sgb0"), buf("sgb1")]
        b_m1, b_m2 = buf("m1"), buf("m2")
        b_QaT, b_QbT = buf("QaT"), buf("QbT")
        b_yaT2 = [[buf(f"yaT{k}_{i}") for i in range(4)] for k in range(2)]
        b_ybT2 = [[buf(f"ybT{k}_{i}") for i in range(4)] for k in range(2)]
        b_mT = [buf(f"mT{i}") for i in range(KC)]
        b_wscr = [buf(f"wscr{i}") for i in range(NTILES)]
        b_setup = buf("setup")

        dma_rr = [0]

        def dma(out, in_, reads, writes, owner, queue="sp", lag=0.0):
            return R.op(queue, lambda e, o=out, i=in_: e.dma_start(out=o, in_=i), reads=reads, writes=writes, dma_owner=owner, lag=lag)

        R.t = -100.0
        dma(cbf[:], cbf_d[:, :], [], [b_const], b_const)
        dma(swam[:], swam_d[:, :], [], [b_const], b_const)
        dma(gcol[:], gcol_d[:, :], [], [b_const], b_const)
        dma(gains[:], gains_d[:, :], [], [b_const], b_const)
        dma(esink[:], sink_d[:, :], [], [b_const], b_const)
        dma(valid[:], valid_d[:, :], [], [b_const], b_const)
        R.op("dve", lambda e: e.memset(epsb[:], EPS), writes=[b_const])
        R.op("dve", lambda e: e.memset(cpow[:, 0:1], -0.5), writes=[b_const])
        R.op("dve", lambda e: e.memset(cpow[:, 1:2], -1.0), writes=[b_const])
        R.op("dve", lambda e: e.memset(cpow[:, 2:3], 1.0), writes=[b_const])
        R.op("act", lambda e: e.activation(out=esink[:], in_=esink[:], func=AF.Exp), reads=[b_const], writes=[b_const])
        R.op("pool", lambda e: e.memset(Va[:], 1.0), writes=[b_ones])
        R.op("pool", lambda e: e.memset(Vb[:], 1.0), writes=[b_ones])
        b_st = [buf("stage0"), buf("stage1")]
        stg = [arena[:, 0:2048], arena[:, 2048:4096]]
        NE = NVAR * 64
        dma(stg[1][:, 0:NE], Mk_d[:, :], [], [b_st[1]], b_st[1])
        for h in range(8):
            dma(stg[0][:, 0:NE], Gt_d[:, h * NE:(h + 1) * NE], [], [b_st[0]], b_st[0])
            R.op("act", lambda e: e.activation(out=stg[0][:, 0:NE], in_=stg[0][:, 0:NE], func=AF.Exp),
                 reads=[b_st[0]], writes=[b_st[0]])
            R.op("dve", lambda e, h=h: e.tensor_tensor(out=Et[:, h].rearrange("p v c -> p (v c)"), in0=stg[0][:, 0:NE],
                                                       in1=stg[1][:, 0:NE], op=ALU.mult),
                 reads=[b_st[0], b_st[1]], writes=[b_Et])

        w_in_r = w_in.rearrange("(kc p) c -> p kc c", p=128)
        w_o_r = w_o.rearrange("(kc p) c -> p kc c", p=128)
        w_oa_r = w_oa.rearrange("(kc p) c -> p kc c", p=128)
        w_ob_r = w_ob.rearrange("(kc p) c -> p kc c", p=128)

        def tile_srcs(t):
            def win(c0, n=256, dst0=0):
                return [(("k8", dst0, n), w_in_r[:, :, c0:c0 + n])]
            if t in T_KA:
                return win(C_KA + 256 * T_KA.index(t))
            if t == T_KBVB:
                return win(C_KB)
            if t in T_VA:
                return win(C_VA + 256 * T_VA.index(t))
            if t in T_QA:
                return win(C_QA + 256 * T_QA.index(t))
            if t in T_QB:
                i = T_QB.index(t)
                out = []
                for cc in range(2):
                    m = 2 * i + cc
                    out.append((("k8", cc * 128, 64), w_in_r[:, :, C_QB + 64 * m:C_QB + 64 * m + 64]))
                    out.append((("k8", cc * 128 + 64, 64), w_in_r[:, :, C_QB + 64 * (4 + m):C_QB + 64 * (4 + m) + 64]))
                return out
            if t in T_ZA:
                return win(C_ZA + 256 * T_ZA.index(t))
            if t in T_ZB:
                return win(C_ZB + 256 * T_ZB.index(t))
            if t in T_GA:
                return win(C_GA + 256 * T_GA.index(t))
            if t in T_GB:
                return win(C_GB + 256 * T_GB.index(t))
            if t in T_WAB:
                i = T_WAB.index(t)
                return [(("ab", 0), w_oa_r[:, :, 256 * i:256 * i + 256]), (("ab", 1), w_ob_r[:, :, 256 * i:256 * i + 256])]
            if t in T_WO:
                i = T_WO.index(t)
                return [(("k8", 0, 256), w_o_r[:, :, 256 * i:256 * i + 256])]
            raise ValueError

        cvt_eng = ["dve", "act"]
        b_cv = [buf("cv0"), buf("cv1")]
        for t in range(NTILES):
            s = t % 2
            st = stg[s]
            ckey = -100.0 if t < 5 else 2.0 + 1.4 * (t - 5)
            R.t = ckey - 1.4 if t >= 7 else ckey
            for (lay, src) in tile_srcs(t):
                if lay[0] == "k8":
                    dst = st.rearrange("p (k c) -> p k c", k=8)[:, :, lay[1]:lay[1] + lay[2]]
                else:
                    dst = st.rearrange("p (a k c) -> p a k c", a=2, k=4)[:, lay[1]]
                dma(dst, src, [], [b_st[s]], b_st[s])
            R.t = ckey
            wt = mT[:, 4 * s:4 * s + 4, :].rearrange("p k t -> p (k t)")
            wb = b_cv[s]
            ce = cvt_eng[t % 2]
            if t <= T_GB[-1]:
                for kc in range(KC):
                    wsl = wt.rearrange("p (k c) -> p k c", k=8)[:, kc, :]
                    ssl = st.rearrange("p (k c) -> p k c", k=8)[:, kc, :]
                    if (kc + t) % 2 == 0:
                        R.op("act", lambda e, wsl=wsl, ssl=ssl, kc=kc: e.activation(out=wsl, in_=ssl, func=AF.Copy, scale=gcol[:, kc:kc + 1]),
                             reads=[b_st[s], b_const], writes=[wb])
                    else:
                        R.op("dve", lambda e, wsl=wsl, ssl=ssl, kc=kc: e.tensor_scalar(out=wsl, in0=ssl, scalar1=gcol[:, kc:kc + 1], scalar2=None,
                                                                                     op0=ALU.mult), reads=[b_st[s], b_const], writes=[wb])
            elif ce == "act":
                R.op("act", lambda e, wt=wt, st=st: e.activation(out=wt, in_=st, func=AF.Copy), reads=[b_st[s]], writes=[wb])
            else:
                R.op(ce, lambda e, wt=wt, st=st: e.tensor_copy(out=wt, in_=st), reads=[b_st[s]], writes=[wb])
            dma(wscr[t], wt, [wb], [b_wscr[t]], wb)

        setup_bufs = [b_const, b_Et, b_ones, b_st[0], b_st[1]] + b_cv + b_wscr
        for e in ("pe", "act", "dve", "pool", "sp"):
            pass
        R.t = 2.0 + 1.4 * (NTILES - 5) + 0.5
        bar_tok = R.op("sp", lambda e: e.nop(), reads=setup_bufs, writes=[b_setup])
        arena_bufs = p_ex.b + b_sga + b_sgb + [b_m1, b_m2] + b_mT
        for bb in arena_bufs:
            bb.writers = [bar_tok]

        wseq = wseq_in
        wcalls = []
        wstate = {"issued": 0, "used": 0, "slots": []}

        def w_issue(t):
            wt, wb = p_w.next()
            dma(wt[:], wscr[t], [b_wscr[t]], [wb], wb)
            wstate["slots"].append((wt, wb, t))
            wstate["issued"] += 1

        def w_next(expect):
            wcalls.append(expect)
            if wseq is None:
                w_issue(expect)
            else:
                while wstate["issued"] < min(len(wseq), wstate["used"] + 2):
                    w_issue(wseq[wstate["issued"]])
            wt, wb, t = wstate["slots"][wstate["used"]]
            assert t == expect, (t, expect)
            wstate["used"] += 1
            return wt, wb

        clk = [0.0]

        def tick(d=1.0):
            clk[0] += d
            R.t = clk[0]
            return clk[0]

        def proj_fm(wt, wb, col0, hsl, hb, pst, psb, lag=0.0):
            wv = wt.rearrange("p (k c) -> p k c", k=8)
            for kc in range(KC):
                R.op("pe", lambda e, kc=kc: e.matmul(pst[:], lhsT=wv[:, kc, col0:col0 + 128], rhs=hsl(kc),
                                                   start=(kc == 0), stop=(kc == KC - 1)),
                     reads=[wb] + hb, writes=[psb], lag=lag)

        L1 = 1.5
        XLAG = -2.0

        def qknorm(pst, psb, gcol, out_ap, out_bufs, pool):
            sqt, sqb = p_sq.next()
            R.op("act", lambda e: e.activation(out=sqt[:], in_=pst[:], func=AF.Square), reads=[psb], writes=[sqb])
            ms, msb = pool.next()
            R.op("pe", lambda e: e.matmul(ms[:], lhsT=BDm, rhs=sqt[:], start=True, stop=True), reads=[sqb, b_const], writes=[msb], lag=L1)
            sdt, sdb = p_sd.next()
            R.op("act", lambda e: e.activation(out=sdt[:], in_=ms[:], func=AF.Ln, bias=epsb[:, 0:1]), reads=[msb, b_const],
                 writes=[sdb], lag=L1)
            R.op("act", lambda e: e.activation(out=sdt[:], in_=sdt[:], func=AF.Exp, scale=-0.5), reads=[sdb], writes=[sdb], lag=L1)
            R.op("dve", lambda e: e.scalar_tensor_tensor(out=out_ap, in0=pst[:], scalar=gains[:, gcol:gcol + 1], in1=sdt[:],
                                                         op0=ALU.mult, op1=ALU.mult),
                 reads=[psb, sdb, b_const], writes=out_bufs, lag=L1)

        def rotary(qn_t, qn_b, cs_t, cs_b, out_ap, out_bufs, pool):
            rp, rpb_ = pool.next()
            R.op("pe", lambda e: e.matmul(rp[:], lhsT=Perm, rhs=qn_t[:], start=True, stop=True), reads=[qn_b, b_const], writes=[rpb_], lag=L1 + 1)
            t1, t1b = p_rt.next()
            t2, t2b = p_rt.next()
            R.op("pool", lambda e: e.tensor_tensor(out=t1, in0=qn_t[:], in1=cs_t[:, 0, :], op=ALU.mult), reads=[qn_b, cs_b], writes=[t1b], lag=L1 + 1)
            R.op("dve", lambda e: e.tensor_tensor(out=t2, in0=rp[:], in1=cs_t[:, 1, :], op=ALU.mult), reads=[rpb_, cs_b], writes=[t2b], lag=L1 + 1)
            R.op("pool", lambda e: e.tensor_tensor(out=out_ap, in0=t1, in1=t2, op=ALU.add), reads=[t1b, t2b], writes=out_bufs, lag=L1 + 1)

        def xnorm_tile(s, e_, tt):
            tok = e_ * 512 + tt * 128
            hbuf = buf(f"hT{e_}")
            xtile, xb = p_xt.next()
            dma(xtile[:], xs[s, tok:tok + 128, :], [], [xb], xb, lag=XLAG)
            stt, stb = p_stat.next()
            xnt, xnb = p_xn.next()
            R.op("dve", lambda e: e.memset(stt[:], 0.0), writes=[stb])
            R.op("act", lambda e: e.activation(out=xnt[:], in_=xtile[:], func=AF.Square, accum_out=stt[:, 0:1]),
                 reads=[xb], writes=[xnb, stb])
            R.op("act", lambda e: e.activation(out=stt[:, 1:2], in_=stt[:, 0:1], func=AF.Ln, bias=epsb[:, 0:1], scale=1.0 / D),
                 reads=[stb, b_const], writes=[stb])
            R.op("act", lambda e: e.activation(out=stt[:, 2:3], in_=stt[:, 1:2], func=AF.Exp, scale=-0.5), reads=[stb], writes=[stb])
            R.op("dve", lambda e: e.tensor_scalar(out=xnt[:], in0=xtile[:], scalar1=stt[:, 2:3], scalar2=None, op0=ALU.mult),
                 reads=[xb, stb], writes=[xnb])
            tp, tpb = pT.next()
            for kc in range(KC):
                R.op("pe", lambda e, kc=kc: e.transpose(tp[:, kc * 128:(kc + 1) * 128], xnt[:, kc * 128:(kc + 1) * 128], ident),
                     reads=[xnb, b_const], writes=[tpb], lag=L1)
            R.op("dve", lambda e: e.tensor_copy(out=hT[:, :, tok:tok + 128], in_=tp[:].rearrange("p (k t) -> p k t", k=KC)),
                 reads=[tpb], writes=[hbuf], lag=L1)

        def kv_items(s, e_, pool, xn_next):
            t0 = e_ * 512
            hbuf = buf(f"hT{e_}")
            hsl = lambda kc: hT[:, kc, t0:t0 + 512]
            tick()
            cst_t, cst_b = p_cst.next()
            dma(cst_t[:], cs_d[s, :, :, t0:t0 + 512], [], [cst_b], cst_b)
            for i in range(2):
                for cc in range(2):
                    wt, wb = w_next(T_KA[i])
                    c = 2 * i + cc
                    tick()
                    if xn_next:
                        xnorm_tile(s, e_ + 1, c)
                    pst, psb = pool.next()
                    proj_fm(wt, wb, cc * 128, hsl, [hbuf], pst, psb)
                    qknorm(pst, psb, 1, KaT[:, c, t0:t0 + 512], [buf(f"KaT{e_}")], pool)
                    yield
            tick()
            wt, wb = w_next(T_KBVB)
            pst, psb = pool.next()
            proj_fm(wt, wb, 0, hsl, [hbuf], pst, psb)
            qt, qb_ = p_qbn.next()
            qknorm(pst, psb, 3, qt[:], [qb_], pool)
            rotary(qt, qb_, cst_t, cst_b, KbT[:, t0:t0 + 512], [buf(f"KbT{e_}")], pool)
            wv = wt.rearrange("p (k c) -> p k c", k=8)
            tick(3)
            for tt in range(4):
                tick()
                tok = t0 + tt * 128
                pst, psb = pool.next()
                for kc in range(KC):
                    R.op("pe", lambda e, kc=kc, pst=pst, tok=tok, wv=wv: e.matmul(pst[:, 0:128], lhsT=hT[:, kc, tok:tok + 128],
                                                                                 rhs=wv[:, kc, 128:256], start=(kc == 0),
                                                                                 stop=(kc == KC - 1)),
                         reads=[wb, hbuf], writes=[psb])
                ti = tok // 128
                R.op("dve", lambda e, pst=pst, ti=ti: e.tensor_copy(
                    out=Vb[:, ti, 64:320].rearrange("p (a c) -> p a c", a=2)[:, :, 0:64],
                    in_=pst[:, 0:128].rearrange("p (a c) -> p a c", a=2)),
                    reads=[psb, b_ones], writes=[buf(f"Vb{e_}")])
            yield
            tick()
            wts = [w_next(T_VA[0]), w_next(T_VA[1])]
            for tt in range(4):
                tick()
                tok = t0 + tt * 128
                pst, psb = pool.next()
                for i in range(2):
                    wv = wts[i][0].rearrange("p (k c) -> p k c", k=8)
                    for kc in range(KC):
                        R.op("pe", lambda e, kc=kc, pst=pst, tok=tok, wv=wv, i=i: e.matmul(
                            pst[:, i * 256:(i + 1) * 256], lhsT=hT[:, kc, tok:tok + 128], rhs=wv[:, kc, :],
                            start=(kc == 0), stop=(kc == KC - 1)), reads=[wts[i][1], hbuf], writes=[psb])
                ti = tok // 128
                R.op("dve", lambda e, pst=pst, ti=ti: e.tensor_copy(
                    out=Va[:, ti, :].rearrange("p (hp a c) -> p hp a c", hp=4, a=3)[:, :, 0, :],
                    in_=pst[:].rearrange("p (hp a c) -> p hp a c", hp=4, a=2)[:, :, 0, :]),
                    reads=[psb, b_ones], writes=[buf(f"Va{e_}")])
                R.op("pool" if False else "act", lambda e, pst=pst, ti=ti: e.activation(
                    out=Va[:, ti, :].rearrange("p (hp a c) -> p hp a c", hp=4, a=3)[:, :, 2, :],
                    in_=pst[:].rearrange("p (hp a c) -> p hp a c", hp=4, a=2)[:, :, 1, :], func=AF.Copy),
                    reads=[psb, b_ones], writes=[buf(f"Va{e_}")])
            yield

        def run_all(gen):
            for _ in gen:
                pass

        def qproj(s, b_):
            q0 = 256 + 512 * b_
            hb = blk_bufs("hT", q0, q0 + 512)
            hsl = lambda kc: hT[:, kc, q0:q0 + 512]
            tick()
            cst_t, cst_b = p_cst.next()
            dma(cst_t[:], cs_d[s, :, :, q0:q0 + 512], [], [cst_b], cst_b)
            for i in range(2):
                wt, wb = w_next(T_QA[i])
                for cc in range(2):
                    c = 2 * i + cc
                    tick()
                    pst, psb = pA.next()
                    proj_fm(wt, wb, cc * 128, hsl, hb, pst, psb)
                    qknorm(pst, psb, 0, QaT[:, c, :], [b_QaT], pA)
            for i in range(2):
                wt, wb = w_next(T_QB[i])
                for cc in range(2):
                    m = 2 * i + cc
                    tick()
                    pst, psb = pA.next()
                    proj_fm(wt, wb, cc * 128, hsl, hb, pst, psb)
                    qt, qb_ = p_qbn.next()
                    qknorm(pst, psb, 2, qt[:], [qb_], pA)
                    rotary(qt, qb_, cst_t, cst_b, QbT[:, m, :], [b_QbT], pA)
            tick(3)

        def keep_warm(n):
            for _ in range(n):
                R.op("pe", lambda e: e.matmul(psTf[:, 0:384], lhsT=ident, rhs=cbf[:, 0:384], start=True, stop=True),
                     reads=[b_const], writes=[pT.b[0]])

        def attention(s, b_, filler):
            vbase = s * NVALID
            q0 = 256 + 512 * b_
            hb = blk_bufs("hT", q0, q0 + 512)
            hsl = lambda kc: hT[:, kc, q0:q0 + 512]
            yaT = yaT2[b_ % 2]
            ybT = ybT2[b_ % 2]
            b_yaT = b_yaT2[b_ % 2]
            b_ybT = b_ybT2[b_ % 2]

            def zproj(wt, wb, cc):
                pst, psb = pZ.next()
                proj_fm(wt, wb, cc * 128, hsl, hb, pst, psb)
                szt, szb_ = p_sz.next()
                ZL = 3.0
                R.op("act", lambda e: e.activation(out=szt[:], in_=pst[:], func=AF.Exp, scale=-1.0), reads=[psb], writes=[szb_], lag=ZL)
                R.op("act", lambda e: e.activation(out=szt[:], in_=szt[:], func=AF.Ln, bias=cpow[:, 2:3]), reads=[szb_, b_const], writes=[szb_], lag=ZL)
                R.op("act", lambda e: e.activation(out=szt[:], in_=szt[:], func=AF.Exp, scale=-1.0), reads=[szb_], writes=[szb_], lag=ZL)
                R.op("dve", lambda e: e.tensor_tensor(out=szt[:], in0=szt[:], in1=pst[:], op=ALU.mult),
                     reads=[szb_, psb], writes=[szb_], lag=ZL)
                return szt, szb_

            LAG = 6.0

            def row_groups(kap):
                bnd = None
                interior = []
                for rr in range(8):
                    r = 8 * b_ + rr
                    if r < 4:
                        if kap <= 5:
                            bnd = (0, 3)
                    elif r >= RS - 4:
                        if kap >= 2:
                            bnd = (4, 7)
                    else:
                        if 2 * kap - 7 <= rr <= 2 * kap + 1:
                            interior.append(rr)
                ig = (interior[0], interior[-1]) if interior else None
                return bnd, ig

            sz_next = None
            for i in range(2):
                for cc in range(2):
                    hp = 2 * i + cc
                    tick()
                    if sz_next is None:
                        wt, wb = w_next(T_ZA[i])
                        sz_next = zproj(wt, wb, cc)
                    szt, szb_ = sz_next
                    obank = [pO.next(), pO.next()]
                    for hh in range(2):
                        if hh == 1 and hp < 3:
                            wt, wb = w_next(T_ZA[(hp + 1) // 2])
                            sz_next = zproj(wt, wb, (hp + 1) % 2)
                        h = 2 * hp + hh
                        pb = 64 * hh
                        ot, ob = obank[hh]
                        first_pv = True
                        for kap in range(8):
                            tick()
                            bg, ig = row_groups(kap)
                            groups = [g_ for g_ in (bg, ig) if g_ is not None]
                            if not groups:
                                continue
                            ra = min(g_[0] for g_ in groups)
                            rb = max(g_[1] for g_ in groups)
                            nr = rb - ra + 1
                            assert sum(g_[1] - g_[0] + 1 for g_ in groups) == nr
                            kt = 512 * b_ + 128 * kap
                            ti = kt // 128
                            kb_ = blk_bufs("KaT", kt, kt + 128)
                            vb_ = blk_bufs("Va", kt, kt + 128)
                            keep_warm(2)
                            st_, stb_ = pS.next()
                            R.op("pe", lambda e, st_=st_, pb=pb, kt=kt, hp=hp, ra=ra, rb=rb, nr=nr: e.matmul(
                                st_[:, 0:64 * nr], lhsT=KaT[pb:pb + 64, hp, kt:kt + 128],
                                rhs=QaT[pb:pb + 64, hp, ra * 64:(rb + 1) * 64], start=True, stop=True),
                                reads=kb_ + [b_QaT], writes=[stb_])
                            ext_, exb_ = p_ex.next()
                            R.op("act", lambda e, st_=st_, ext_=ext_, nr=nr: e.activation(out=ext_[:, 0:nr * 64], in_=st_[:, 0:nr * 64],
                                                                                         func=AF.Exp, scale=0.125),
                                 reads=[stb_], writes=[exb_])
                            ptt, ptb = p_PT.next()
                            for (ga, gb_) in groups:
                                gn = gb_ - ga + 1
                                c0 = (ga - ra) * 64
                                dr0 = 2 * kap + 3 - ga
                                ex3 = ext_[:, c0:c0 + gn * 64].rearrange("p (j c) -> p j c", j=gn)
                                pt3 = ptt[:, c0:c0 + gn * 64].rearrange("p (j c) -> p j c", j=gn)
                                if (ga, gb_) == bg:
                                    v0 = 13 - dr0
                                    ev = Et[:, h, v0:v0 + gn, :]
                                    R.op("pool", lambda e, ex3=ex3, ev=ev: e.tensor_tensor(out=ex3, in0=ex3, in1=ev, op=ALU.mult),
                                         reads=[exb_, b_Et], writes=[exb_])
                                    kp = kap if ga == 0 else kap - 2
                                    vv = valid[:, vbase:vbase + 48].rearrange("p (r k) -> p r k", k=6)[:, ga:ga + gn, kp]
                                    vv = vv.unsqueeze(2).to_broadcast([128, gn, 64])
                                    R.op("dve", lambda e, pt3=pt3, ex3=ex3, vv=vv: e.tensor_tensor(out=pt3, in0=ex3, in1=vv, op=ALU.mult),
                                         reads=[exb_, b_const], writes=[ptb])
                                else:
                                    assert 2 <= dr0 - (gn - 1) and dr0 <= 10, (dr0, gn)
                                    v0 = 14 + 10 - dr0
                                    ev = Et[:, h, v0:v0 + gn, :]
                                    R.op("dve" if (kap % 2 == 0) else "pool",
                                         lambda e, pt3=pt3, ex3=ex3, ev=ev: e.tensor_tensor(out=pt3, in0=ex3, in1=ev, op=ALU.mult),
                                         reads=[exb_, b_Et], writes=[ptb])
                            R.op("pe", lambda e, ot=ot, ti=ti, hp=hp, hh=hh, ptt=ptt, ra=ra, rb=rb, nr=nr, fp=first_pv: e.matmul(
                                ot[:, ra * 64:(rb + 1) * 64], lhsT=Va[:, ti, hp * 192 + hh * 64:hp * 192 + hh * 64 + 128],
                                rhs=ptt[:, 0:64 * nr], start=fp, stop=False, skip_group_check=True),
                                reads=vb_ + [ptb, b_ones], writes=[ob], lag=LAG)
                            first_pv = False
                        po = 64 - pb
                        rdt, _unused = p_rd.next()
                        rdL = rd_half[rdb.index(rdt)]
                        R.op("act", lambda e, ot=ot, rdt=rdt, pb=pb, po=po: e.activation(out=rdt[pb:pb + 64, :], in_=ot[po:po + 64, :],
                                                                                       func=AF.Ln),
                             reads=[ob], writes=rdL, lag=LAG + 0.5)
                        R.op("act", lambda e, rdt=rdt, pb=pb: e.activation(out=rdt[pb:pb + 64, :], in_=rdt[pb:pb + 64, :], func=AF.Exp, scale=-1.0),
                             reads=rdL, writes=rdL, lag=LAG + 0.5)
                        R.op("pool", lambda e, rdt=rdt, szt=szt, pb=pb: e.tensor_tensor(out=rdt[pb:pb + 64, :], in0=rdt[pb:pb + 64, :],
                                                                                       in1=szt[pb:pb + 64, :], op=ALU.mult),
                             reads=rdL + [szb_], writes=rdL, lag=LAG + 0.5)
                        R.op("dve", lambda e, ot=ot, rdt=rdt, pb=pb, hp=hp: e.tensor_tensor(
                            out=yaT[pb:pb + 64, hp, :], in0=ot[pb:pb + 64, :], in1=rdt[pb:pb + 64, :], op=ALU.mult),
                            reads=[ob] + rdL, writes=[b_yaT[hp]], lag=LAG + 0.5)
                        filler()

            for g in range(2):
                tick(8)
                wt, wb = w_next(T_ZB[g])
                sz2 = [zproj(wt, wb, 0), zproj(wt, wb, 1)]
                pb = 64 * g
                for n_ in range(4):
                    tick(2)
                    qt0 = q0 + 128 * n_
                    pts = []
                    keep_warm(3)
                    for dl in (-1, 0, 1):
                        kt0 = qt0 + 128 * dl
                        st_, stb_ = pS.next()
                        R.op("pe", lambda e, st_=st_, kt0=kt0, pb=pb, n_=n_: e.matmul(
                            st_[:].rearrange("p (m q) -> p m q", m=4), lhsT=KbT[pb:pb + 64, kt0:kt0 + 128],
                            rhs=QbT[pb:pb + 64, :, n_ * 128:(n_ + 1) * 128], start=True, stop=True),
                            reads=blk_bufs("KbT", kt0, kt0 + 128) + [b_QbT], writes=[stb_])
                        ptt, ptb = p_PT.next()
                        if dl == 0:
                            R.op("act", lambda e, st_=st_, ptt=ptt: e.activation(out=ptt[:], in_=st_[:], func=AF.Exp, scale=0.125),
                                 reads=[stb_], writes=[ptb])
                        else:
                            ext_, exb_ = p_ex.next()
                            R.op("act", lambda e, st_=st_, ext_=ext_: e.activation(out=ext_, in_=st_[:], func=AF.Exp, scale=0.125),
                                 reads=[stb_], writes=[exb_])
                            mk = swam[:, 0:128] if dl == -1 else swam[:, 128:256]
                            mk4 = mk.unsqueeze(1).to_broadcast([128, 4, 128])
                            edge = (dl == -1 and b_ == 0 and n_ == 0) or (dl == 1 and b_ == NQB - 1 and n_ == 3)
                            ex3 = ext_.rearrange("p (m q) -> p m q", m=4)
                            pt3 = ptt[:].rearrange("p (m q) -> p m q", m=4)
                            if edge:
                                vc = vbase + 48 + (0 if dl == -1 else 1)
                                R.op("dve", lambda e, ex3=ex3, pt3=pt3, mk4=mk4, vc=vc: e.scalar_tensor_tensor(
                                    out=pt3, in0=ex3, scalar=valid[:, vc:vc + 1], in1=mk4, op0=ALU.mult, op1=ALU.mult),
                                    reads=[exb_, b_const], writes=[ptb])
                            else:
                                R.op("pool" if dl == -1 else "dve",
                                     lambda e, ex3=ex3, pt3=pt3, mk4=mk4: e.tensor_tensor(out=pt3, in0=ex3, in1=mk4, op=ALU.mult),
                                     reads=[exb_, b_const], writes=[ptb])
                        pts.append((ptt, ptb, kt0))
                    ot, ob = pO.next()
                    for par in range(2):
                        vcol = g * 128 + 64 if par == 0 else g * 128
                        for k_, (ptt, ptb, kt0) in enumerate(pts):
                            ti = kt0 // 128
                            rhs = ptt[:].rearrange("p (a b q) -> p a b q", a=2, b=2)[:, :, par, :]
                            R.op("pe", lambda e, ot=ot, ti=ti, vcol=vcol, rhs=rhs, par=par, k_=k_: e.matmul(
                                ot[:, par * 256:(par + 1) * 256].rearrange("p (a q) -> p a q", a=2), lhsT=Vb[:, ti, vcol:vcol + 128],
                                rhs=rhs, start=(k_ == 0), stop=(k_ == 2)),
                                reads=blk_bufs("Vb", kt0, kt0 + 128) + [ptb, b_ones], writes=[ob], lag=2.5)
                    for par in range(2):
                        pbo = 64 * par
                        pde = 64 - pbo
                        k4 = rd_swa["i"] % 4
                        rd_swa["i"] += 1
                        rdt = rdb[k4 // 2]
                        co = 256 * (k4 % 2)
                        rdb_ = rd_half[k4 // 2][k4 % 2]
                        for a in range(2):
                            h = 4 * g + 2 * a + par
                            R.op("act", lambda e, ot=ot, rdt=rdt, pbo=pbo, pde=pde, par=par, a=a, h=h, co=co: e.activation(
                                out=rdt[pbo:pbo + 64, co + a * 128:co + (a + 1) * 128],
                                in_=ot[pde:pde + 64, par * 256 + a * 128:par * 256 + (a + 1) * 128],
                                func=AF.Ln, bias=esink[pbo:pbo + 64, h:h + 1]), reads=[ob, b_const], writes=[rdb_], lag=3.0)
                        R.op("act", lambda e, rdt=rdt, pbo=pbo, co=co: e.activation(out=rdt[pbo:pbo + 64, co:co + 256], in_=rdt[pbo:pbo + 64, co:co + 256],
                                                                            func=AF.Exp, scale=-1.0),
                             reads=[rdb_], writes=[rdb_], lag=3.0)
                        for a in range(2):
                            szt, szb_ = sz2[a]
                            R.op("pool", lambda e, rdt=rdt, szt=szt, pbo=pbo, a=a, n_=n_, co=co: e.tensor_tensor(
                                out=rdt[pbo:pbo + 64, co + a * 128:co + (a + 1) * 128], in0=rdt[pbo:pbo + 64, co + a * 128:co + (a + 1) * 128],
                                in1=szt[pbo:pbo + 64, n_ * 128:(n_ + 1) * 128], op=ALU.mult),
                                reads=[rdb_, szb_], writes=[rdb_], lag=3.0)
                        R.op("dve", lambda e, ot=ot, rdt=rdt, pbo=pbo, par=par, g=g, n_=n_, co=co: e.tensor_tensor(
                            out=ybT[pbo:pbo + 64, 2 * g:2 * g + 2, n_ * 128:(n_ + 1) * 128],
                            in0=ot[pbo:pbo + 64, par * 256:(par + 1) * 256].rearrange("p (a q) -> p a q", a=2),
                            in1=rdt[pbo:pbo + 64, co:co + 256].rearrange("p (a q) -> p a q", a=2), op=ALU.mult),
                            reads=[ob, rdb_], writes=[b_ybT[2 * g], b_ybT[2 * g + 1]], lag=3.0)
                    if n_ % 2 == 1:
                        filler()
            tick(4)

        def merge_items(s, b_, pool):
            q0 = 256 + 512 * b_
            hb = blk_bufs("hT", q0, q0 + 512)
            hsl = lambda kc: hT[:, kc, q0:q0 + 512]
            yaT = yaT2[b_ % 2]
            ybT = ybT2[b_ % 2]
            b_yaT = b_yaT2[b_ % 2]
            b_ybT = b_ybT2[b_ % 2]
            for i in range(4):
                tick()
                wga, wgab = w_next(T_GA[i])
                for cc in range(2):
                    pga, pgab = pool.next()
                    proj_fm(wga, wgab, cc * 128, hsl, hb, pga, pgab)
                    R.op("act", lambda e, pga=pga, cc=cc: e.activation(out=sga[cc], in_=pga[:], func=AF.Exp, scale=-1.0),
                         reads=[pgab], writes=[b_sga[cc]])
                    R.op("act", lambda e, cc=cc: e.activation(out=sga[cc], in_=sga[cc], func=AF.Ln, bias=cpow[:, 2:3]),
                         reads=[b_sga[cc], b_const], writes=[b_sga[cc]])
                    R.op("act", lambda e, cc=cc: e.activation(out=sga[cc], in_=sga[cc], func=AF.Exp, scale=-1.0),
                         reads=[b_sga[cc]], writes=[b_sga[cc]])
                tick()
                wgb, wgbb = w_next(T_GB[i])
                for cc in range(2):
                    pgb, pgbb = pool.next()
                    proj_fm(wgb, wgbb, cc * 128, hsl, hb, pgb, pgbb)
                    R.op("act", lambda e, pgb=pgb, cc=cc: e.activation(out=sgb[cc], in_=pgb[:], func=AF.Exp, scale=-1.0),
                         reads=[pgbb], writes=[b_sgb[cc]])
                    R.op("act", lambda e, cc=cc: e.activation(out=sgb[cc], in_=sgb[cc], func=AF.Ln, bias=cpow[:, 2:3]),
                         reads=[b_sgb[cc], b_const], writes=[b_sgb[cc]])
                    R.op("act", lambda e, cc=cc: e.activation(out=sgb[cc], in_=sgb[cc], func=AF.Exp, scale=-1.0),
                         reads=[b_sgb[cc]], writes=[b_sgb[cc]])
                tick()
                wab, wabb = w_next(T_WAB[i])
                wabv = wab.rearrange("p (a k c) -> p a k c", a=2, k=4)
                for cc in range(2):
                    c = 2 * i + cc
                    pa, pab = pool.next()
                    for kc in range(4):
                        R.op("pe", lambda e, kc=kc, pa=pa, cc=cc, wabv=wabv: e.matmul(pa[:], lhsT=wabv[:, 0, kc, cc * 128:(cc + 1) * 128],
                                                                                     rhs=yaT[:, kc, :], start=(kc == 0), stop=(kc == 3)),
                             reads=[wabb] + b_yaT, writes=[pab])
                    R.op("dve", lambda e, pa=pa, cc=cc: e.tensor_tensor(out=m1b, in0=pa[:], in1=sga[cc], op=ALU.mult),
                         reads=[pab, b_sga[cc]], writes=[b_m1])
                    pb2, pbb = pool.next()
                    for kc in range(4):
                        R.op("pe", lambda e, kc=kc, pb2=pb2, cc=cc, wabv=wabv: e.matmul(pb2[:], lhsT=wabv[:, 1, kc, cc * 128:(cc + 1) * 128],
                                                                                       rhs=ybT[:, kc, :], start=(kc == 0), stop=(kc == 3)),
                             reads=[wabb] + b_ybT, writes=[pbb])
                    R.op("dve", lambda e, pb2=pb2, cc=cc: e.tensor_tensor(out=m2b, in0=pb2[:], in1=sgb[cc], op=ALU.mult),
                         reads=[pbb, b_sgb[cc]], writes=[b_m2])
                    R.op("pool", lambda e, c=c: e.tensor_tensor(out=mT[:, c, :], in0=m1b, in1=m2b, op=ALU.add),
                         reads=[b_m1, b_m2], writes=[b_mT[c]])
                yield
            tick()
            xres = []
            for tt in range(4):
                xtile, xb = p_xt.next()
                dma(xtile[:], xs[s, q0 + tt * 128:q0 + (tt + 1) * 128, :], [], [xb], xb)
                xres.append((xtile, xb))
            for cg in range(4):
                tick()
                wo_t, wo_b = w_next(T_WO[cg])
                wov = wo_t.rearrange("p (k c) -> p k c", k=8)
                for tt in range(4):
                    pst, psb = pool.next()
                    for kc in range(KC):
                        R.op("pe", lambda e, kc=kc, pst=pst, tt=tt, wov=wov: e.matmul(
                            pst[:, 0:256], lhsT=mT[:, kc, tt * 128:(tt + 1) * 128], rhs=wov[:, kc, :],
                            start=(kc == 0), stop=(kc == KC - 1)), reads=[wo_b] + b_mT, writes=[psb])
                    xtile, xb = xres[tt]
                    R.op("dve", lambda e, pst=pst, xtile=xtile, cg=cg: e.tensor_tensor(
                        out=xtile[:, cg * 256:(cg + 1) * 256], in0=pst[:, 0:256], in1=xtile[:, cg * 256:(cg + 1) * 256],
                        op=ALU.add), reads=[psb, xb], writes=[xb])
                yield
            tick()
            for tt in range(4):
                xtile, xb = xres[tt]
                r0 = 512 * b_ + 128 * tt
                tk = dma(y[s, r0:r0 + 128, :], xtile[:], [xb], [], xb)
                R.store_toks.append(tk)
            yield

        clk[0] = 0.0
        for s in range(nslot):
            tick(2)
            for tt in range(4):
                tick()
                xnorm_tile(s, 0, tt)
            tick(2)
            run_all(kv_items(s, 0, pA, True))
            run_all(kv_items(s, 1, pA, True))
            pending = [kv_items(s, 2, pZ, False)]

            def filler():
                while pending:
                    try:
                        next(pending[0])
                        tick(3)
                        return
                    except StopIteration:
                        pending.pop(0)

            def drain():
                while pending:
                    filler()

            for b_ in range(NQB):
                qproj(s, b_)
                attention(s, b_, filler)
                drain()
                if b_ + 1 < NQB:
                    pending.append(merge_items(s, b_, pZ))
                else:
                    run_all(merge_items(s, b_, pA))

        tick(10)

        if dbg and wseq_in is not None:
            allb = list(B.values())
            def dump(name, t, shape, dt):
                o = nc.dram_tensor(name, list(shape), dt, kind="ExternalOutput").ap()
                tk = dma(o, t, allb, [], buf("dbg_" + name))
                R.store_toks.append(tk)
            dump("d_hT", hT[:].rearrange("p k t -> p (k t)"), [128, KC * EXT], BF16)
            dump("d_KaT", KaT[:].rearrange("p k t -> p (k t)"), [128, 4 * EXT], BF16)
            dump("d_KbT", KbT[:], [128, EXT], BF16)
            dump("d_Va", Va[:].rearrange("p k t -> p (k t)"), [128, NT * 768], BF16)
            dump("d_Vb", Vb[:].rearrange("p k t -> p (k t)"), [128, NT * 320], BF16)
            dump("d_QaT", QaT[:].rearrange("p k t -> p (k t)"), [128, 4 * 512], BF16)
            dump("d_QbT", QbT[:].rearrange("p k t -> p (k t)"), [128, 4 * 512], BF16)
            dump("d_yaT", yaT2[(NQB - 1) % 2][:].rearrange("p k t -> p (k t)"), [128, 4 * 512], BF16)
            dump("d_ybT", ybT2[(NQB - 1) % 2][:].rearrange("p k t -> p (k t)"), [128, 4 * 512], BF16)
            dump("d_mT", mT[:].rearrange("p k t -> p (k t)"), [128, KC * 512], BF16)
            dump("d_Et", Et[:].rearrange("p h v c -> p (h v c)"), [128, 8 * NVAR * 64], BF16)

        last = {}
        for t in R.store_toks:
            if id(t.sem) not in last or t.order > last[id(t.sem)].order:
                last[id(t.sem)] = t
        R.op("sp", lambda e: e.nop(), extra=list(last.values()))


        return wcalls

    wseq_real = record(Rec(), None)
    record(R, wseq_real)

    R.finalize()
    for e in R.ENGS:
        R.sems[e].handle = es.enter_context(nc.semaphore(R.sems[e].name))
    for sdm in R.dsems:
        sdm.handle = es.enter_context(nc.semaphore(sdm.name))
    with nc.Block() as block:
        @block.sync
        def _(eng):
            R.emit("sp", eng)

        @block.tensor
        def _(eng):
            R.emit("pe", eng)

        @block.scalar
        def _(eng):
            R.emit("act", eng)

        @block.vector
        def _(eng):
            R.emit("dve", eng)

        @block.gpsimd
        def _(eng):
            R.emit("pool", eng)
    es.close()
    return nc


def _variants():
    v = [(13 - i, 1, 1) for i in range(14)]
    v += [(10, 1, 0)] + [(d, 1, 1) for d in range(9, 2, -1)] + [(2, 0, 1)]
    return v


def _host_constants():
    p = np.arange(128)
    ck = p % 64
    half = p // 64
    cq = np.arange(64)
    cs_ = np.clip(cq - 8, 0, 48)
    col_in = (ck[:, None] >= cs_[None, :]) & (ck[:, None] < cs_[None, :] + 16)
    dc = np.clip(ck[:, None] - cq[None, :], -15, 15) + 15
    var = _variants()
    dr_idx = np.stack([np.clip(d + half, 0, 14) for (d, lo, up) in var], axis=1)
    Mk = np.zeros((128, NVAR, 64), np.float32)
    for vi, (d, lo, up) in enumerate(var):
        hv = np.where(half == 0, lo, up).astype(np.float32)
        Mk[:, vi, :] = col_in.astype(np.float32) * hv[:, None]
    ident = np.eye(128, dtype=np.float32)
    BD = np.zeros((128, 128), np.float32)
    BD[:64, :64] = 1.0 / 64
    BD[64:, 64:] = 1.0 / 64
    partner = np.where((p % 64) < 32, p + 32, p - 32)
    Perm = np.zeros((128, 128), np.float32)
    Perm[partner, p] = 1.0
    cbf = np.concatenate([ident, BD, Perm], axis=1).astype(ml_dtypes.bfloat16)
    j = np.arange(128)[:, None]
    i = np.arange(128)[None, :]
    swam = np.concatenate([(j >= i), (j <= i)], axis=1).astype(np.float32)
    return dict(dr_idx=dr_idx, dc=dc, Mk=Mk.reshape(128, NVAR * 64), cbf=cbf, swam=swam)


def _slot_tables(row0):
    p = np.arange(128)
    half = p // 64
    val = np.zeros((128, NVALID), np.float32)
    for bi in range(8):
        r = bi if bi < 4 else RS - 4 + (bi - 4)
        start_local = -4 if bi < 4 else RS - 8
        R_ = row0 + r
        w0 = min(max(R_ - 4, 0), NROWS - 8)
        for j in range(6):
            kr = row0 + start_local + 2 * j + half
            val[:, bi * 6 + j] = ((kr >= w0) & (kr < w0 + 8)).astype(np.float32)
    val[:, 48] = 1.0 if row0 > 0 else 0.0
    val[:, 49] = 1.0 if row0 + RS < NROWS else 0.0
    pos = (row0 - 4) * GW + np.arange(EXT)
    halfd = 32
    inv = (np.float32(10000.0) ** (-(np.arange(halfd, dtype=np.float32)) / np.float32(halfd))).astype(np.float32)
    ang = pos.astype(np.float32)[None, :] * inv[:, None]
    cos = np.cos(ang).astype(np.float32)
    sin = np.sin(ang).astype(np.float32)
    d = p % 64
    cs = np.zeros((128, 2, EXT), np.float32)
    cs[:, 0, :] = cos[d % 32]
    cs[:, 1, :] = sin[d % 32] * np.where(d < 32, -1.0, 1.0)[:, None].astype(np.float32)
    return val, cs


_PROG = {}


def _make_in_maps(inputs, slot_list_per_core):
    hc = _host_constants()
    x_all = [np.asarray(inputs["x_prompt"], np.float32), np.asarray(inputs["x_sample"], np.float32)]
    seqs = [x_all[0][i] for i in range(x_all[0].shape[0])] + [x_all[1][i] for i in range(x_all[1].shape[0])]
    rpb = np.asarray(inputs["rpb_a"], np.float32)[0]
    Gt = rpb[:, hc["dr_idx"][:, :, None], hc["dc"][:, None, :]]
    Gt = np.ascontiguousarray(np.transpose(Gt, (1, 0, 2, 3))).reshape(128, 8 * NVAR * 64)
    p = np.arange(128)
    gains = np.stack([np.asarray(inputs[k], np.float32)[0][p % 64] for k in ("qn_a", "kn_a", "qn_b", "kn_b")], axis=1)
    common = {
        "w_in": np.ascontiguousarray(np.asarray(inputs["w_in"], np.float32)[0]),
        "w_out_a": np.ascontiguousarray(np.asarray(inputs["w_out_a"], np.float32)[0]),
        "w_out_b": np.ascontiguousarray(np.asarray(inputs["w_out_b"], np.float32)[0]),
        "w_o": np.ascontiguousarray(np.asarray(inputs["w_o"], np.float32)[0]),
        "gcol": np.ascontiguousarray(np.asarray(inputs["norm_g"], np.float32)[0].reshape(KC, 128).T),
        "gains": np.ascontiguousarray(gains),
        "sinkb": np.ascontiguousarray(np.broadcast_to(np.asarray(inputs["sink_b"], np.float32)[0][None, :], (128, 8))),
        "Gt": Gt, "Mk": hc["Mk"], "cbf": hc["cbf"], "swam": hc["swam"],
    }
    tabs = {}
    in_maps = []
    for slots in slot_list_per_core:
        ns = len(slots)
        xs = np.zeros((ns, EXT, D), np.float32)
        valid = np.zeros((128, ns * NVALID), np.float32)
        cs = np.zeros((ns, 128, 2, EXT), np.float32)
        for si, (sq_, row0) in enumerate(slots):
            lo = (row0 - 4) * GW
            hi = lo + EXT
            a, b = max(lo, 0), min(hi, SEQ)
            xs[si, a - lo:b - lo] = seqs[sq_][a:b]
            if row0 not in tabs:
                tabs[row0] = _slot_tables(row0)
            valid[:, si * NVALID:(si + 1) * NVALID] = tabs[row0][0]
            cs[si] = tabs[row0][1]
        m = dict(common)
        m.update({"xs": xs, "valid": valid, "cs": cs})
        in_maps.append(m)
    return in_maps


def kernel(**inputs):
    nseq_p = np.asarray(inputs["x_prompt"]).shape[0]
    nseq_s = np.asarray(inputs["x_sample"]).shape[0]
    nseq = nseq_p + nseq_s
    qper = NROWS // RS
    all_slots = [(sq_, q * RS) for sq_ in range(nseq) for q in range(qper)]
    assert len(all_slots) == NCORES * NSLOT
    per_core = [all_slots[c * NSLOT:(c + 1) * NSLOT] for c in range(NCORES)]
    in_maps = _make_in_maps(inputs, per_core)
    if "nc" not in _PROG:
        _PROG["nc"] = build_program(NSLOT)
    res = run_bass_kernel_spmd(_PROG["nc"], in_maps, core_ids=list(range(NCORES)))
    outs = [np.zeros((SEQ, D), np.float32) for _ in range(nseq)]
    for c in range(NCORES):
        yc = res.results[c]["y"]
        for si, (sq_, row0) in enumerate(per_core[c]):
            outs[sq_][row0 * GW:(row0 + RS) * GW] = yc[si]
    y_prompt = np.stack(outs[:nseq_p], axis=0)
    y_sample = np.stack(outs[nseq_p:], axis=0)
    return (y_prompt, y_sample)
```

```python
import contextlib
import numpy as np
import ml_dtypes
import concourse.bass as bass
import concourse.mybir as mybir
from concourse.bass_utils import run_bass_kernel_spmd

F32 = mybir.dt.float32
BF16 = mybir.dt.bfloat16
AF = mybir.ActivationFunctionType
ALU = mybir.AluOpType

D = 1024
KC = 8
SEQ = 4096
GW = 64
NROWS = 64
RS = 16
EXTR = RS + 8
EXT = EXTR * GW
QS = RS * GW
NQB = QS // 512
NEB = EXT // 512
NT = EXT // 128
NCORES = 8
NSLOT = (12 * NROWS // RS) // NCORES
NVAR = 23
NVALID = 8 * 6 + 2
EPS = 1e-6

C_QA, C_KA, C_VA, C_ZA, C_QB, C_KB, C_VB, C_ZB, C_GA, C_GB = 0, 512, 1024, 1536, 2048, 2560, 2688, 2816, 3328, 4352

T_KA = [0, 1]
T_KBVB = 2
T_VA = [3, 4]
T_QA = [5, 6]
T_QB = [7, 8]
T_ZA = [9, 10]
T_ZB = [11, 12]
T_GA = [13, 14, 15, 16]
T_GB = [17, 18, 19, 20]
T_WAB = [21, 22, 23, 24]
T_WO = [25, 26, 27, 28]
NTILES = 29


class Sem:
    def __init__(self, name, is_dma):
        self.name = name
        self.is_dma = is_dma
        self.handle = None
        self.count = 0


class Tok:
    __slots__ = ("sem", "order", "value", "op")

    def __init__(self, sem, order, value=None, op=None):
        self.sem = sem
        self.order = order
        self.value = value
        self.op = op


class Buf:
    def __init__(self, name):
        self.name = name
        self.writers = []
        self.readers = []
        self.dsem = None


class Op:
    __slots__ = ("fn", "deps", "tok", "signal", "is_dma", "key", "idx", "eng", "info")

    def __init__(self, fn, deps, tok, is_dma, key, idx, eng):
        self.fn = fn
        self.deps = deps
        self.tok = tok
        self.signal = False
        self.is_dma = is_dma
        self.key = key
        self.idx = idx
        self.eng = eng


class Rec:
    ENGS = ("pe", "act", "dve", "pool", "sp")

    def __init__(self):
        self.ops = {e: [] for e in self.ENGS}
        self.sems = {e: Sem("s_" + e, False) for e in self.ENGS}
        self.dsems = []
        self.store_toks = []
        self.t = 0.0
        self.nrec = 0

    def _merge(self, deps, t):
        k = id(t.sem)
        if k not in deps or t.order > deps[k].order:
            deps[k] = t

    def _trim(self, lst, t):
        for x in lst:
            if x.sem is t.sem and x.order > t.order:
                return
        lst[:] = [x for x in lst if x.sem is not t.sem]
        lst.append(t)

    def op(self, eng, fn, reads=(), writes=(), dma_owner=None, extra=(), lag=0.0):
        deps = {}
        for b in reads:
            for t in b.writers:
                self._merge(deps, t)
        for b in writes:
            for t in b.readers:
                self._merge(deps, t)
            for t in b.writers:
                self._merge(deps, t)
        for t in extra:
            self._merge(deps, t)
        if dma_owner is not None:
            if dma_owner.dsem is None:
                dma_owner.dsem = Sem("d_" + dma_owner.name, True)
                self.dsems.append(dma_owner.dsem)
            s = dma_owner.dsem
            s.count += 16
            tok = Tok(s, (float(s.count), 0), s.count)
            self.nrec += 1
        else:
            self.nrec += 1
            tok = Tok(self.sems[eng], (self.t + lag, self.nrec))
        o = Op(fn, list(deps.values()), tok, dma_owner is not None, self.t + lag, self.nrec, eng)
        tok.op = o
        o.info = ([b.name for b in reads], [b.name for b in writes])
        self.ops[eng].append(o)
        wset = set(id(b) for b in writes)
        for b in writes:
            if b.readers:
                b.writers = [tok]
                b.readers = []
            else:
                self._trim(b.writers, tok)
        for b in reads:
            if id(b) not in wset:
                self._trim(b.readers, tok)
        return tok

    def barrier(self, bufs):
        pass

    def finalize(self):
        for e in self.ENGS:
            self.ops[e].sort(key=lambda o: (o.key, o.idx))
            for o in self.ops[e]:
                for t in o.deps:
                    if t.op is not None:
                        assert (t.op.key, t.op.idx) < (o.key, o.idx), ("non-monotone dep", e, o.key, o.idx, o.info, t.op.eng, t.op.key, t.op.idx, t.op.info)
        for e in self.ENGS:
            for o in self.ops[e]:
                for t in o.deps:
                    if t.op is not None and not t.op.is_dma:
                        if not (e == "pe" and t.sem is self.sems["pe"]):
                            t.op.signal = True
        for e in self.ENGS:
            c = 0
            for o in self.ops[e]:
                if not o.is_dma and o.signal:
                    c += 1
                    o.tok.value = c

    def emit(self, eng_name, eng):
        seen = {}
        own = self.sems[eng_name]
        n_wait = 0
        for o in self.ops[eng_name]:
            for t in o.deps:
                if eng_name == "pe" and t.sem is own:
                    continue
                v = t.value
                assert v is not None
                if seen.get(id(t.sem), 0) >= v:
                    continue
                seen[id(t.sem)] = v
                eng.wait_ge(t.sem.handle, v)
                n_wait += 1
            ins = o.fn(eng)
            if o.is_dma:
                ins.then_inc(o.tok.sem.handle, 16)
            elif o.signal:
                ins.then_inc(own.handle, 1)
        return n_wait


def build_program(nslot=NSLOT, dbg=False):
    nc = bass.Bass("TRN2", target_bir_lowering=False)
    R = Rec()
    es = contextlib.ExitStack()

    def dram_in(name, shape, dt=F32):
        return nc.dram_tensor(name, list(shape), dt, kind="ExternalInput").ap()

    xs = dram_in("xs", [nslot, EXT, D])
    w_in = dram_in("w_in", [D, 5376])
    w_oa = dram_in("w_out_a", [512, D])
    w_ob = dram_in("w_out_b", [512, D])
    w_o = dram_in("w_o", [D, D])
    gcol_d = dram_in("gcol", [128, KC])
    gains_d = dram_in("gains", [128, 4])
    sink_d = dram_in("sinkb", [128, 8])
    Gt_d = dram_in("Gt", [128, 8 * NVAR * 64])
    Mk_d = dram_in("Mk", [128, NVAR * 64])
    cbf_d = dram_in("cbf", [128, 3 * 128], BF16)
    swam_d = dram_in("swam", [128, 256])
    valid_d = dram_in("valid", [128, nslot * NVALID])
    cs_d = dram_in("cs", [nslot, 128, 2, EXT])
    y = nc.dram_tensor("y", [nslot, QS, D], F32, kind="ExternalOutput").ap()
    wscr = nc.dram_tensor("wscr", [NTILES, 128, 2048], BF16).ap()

    def sb(name, shape, dt):
        return es.enter_context(nc.sbuf_tensor(name, list(shape), dt))

    def ps(name, shape, dt):
        return es.enter_context(nc.psum_tensor(name, list(shape), dt))

    hT = sb("hT", [128, KC, EXT], BF16)
    KaT = sb("KaT", [128, 4, EXT], BF16)
    KbT = sb("KbT", [128, EXT], BF16)
    Va = sb("Va", [128, NT, 768], BF16)
    Vb = sb("Vb", [128, NT, 320], BF16)
    Et = sb("Et", [128, 8, NVAR, 64], BF16)
    cbf = sb("cbf_s", [128, 3 * 128], BF16)
    ident = cbf[:, 0:128]
    BDm = cbf[:, 128:256]
    Perm = cbf[:, 256:384]
    swam = sb("swam_s", [128, 256], F32)
    gcol = sb("gcol_s", [128, KC], F32)
    gains = sb("gains_s", [128, 4], F32)
    esink = sb("esink", [128, 8], F32)
    valid = sb("valid_s", [128, nslot * NVALID], F32)
    epsb = sb("epsb", [128, 1], F32)
    cpow = sb("cpow", [128, 3], F32)
    wbuf = [sb(f"wbuf{i}", [128, 2048], BF16) for i in range(3)]
    arena = sb("arena", [128, 4096], F32)
    xt = [sb(f"xt{i}", [128, D], F32) for i in range(4)]
    xn = [sb(f"xn{i}", [128, D], BF16) for i in range(2)]
    stat = [sb(f"stat{i}", [128, 4], F32) for i in range(2)]
    sq = [sb(f"sq{i}", [128, 512], BF16) for i in range(2)]
    sd = [sb(f"sd{i}", [128, 512], F32) for i in range(2)]
    qbn = [sb(f"qbn{i}", [128, 512], BF16) for i in range(2)]
    cst = [sb(f"cst{i}", [128, 2, 512], F32) for i in range(1)]
    QaT = sb("QaT", [128, 4, 512], BF16)
    QbT = sb("QbT", [128, 4, 512], BF16)
    PT = [sb(f"PT{i}", [128, 512], BF16) for i in range(6)]
    szb = [sb(f"sz{i}", [128, 512], F32) for i in range(2)]
    rdb = [sb(f"rd{i}", [128, 512], F32) for i in range(2)]
    yaT2 = [sb(f"yaT{i}", [128, 4, 512], BF16) for i in range(2)]
    ybT2 = [sb(f"ybT{i}", [128, 4, 512], BF16) for i in range(2)]
    mT = sb("mT", [128, KC, 512], BF16)
    ex2 = sb("ex2", [128, 512], F32)
    ex3 = sb("ex3", [128, 512], F32)
    exb = [arena[:, 0:512], arena[:, 512:1024], ex2[:], ex3[:]]
    rt0 = sb("rt0", [128, 512], F32)
    rt1 = sb("rt1", [128, 512], F32)
    rtb = [rt0[:], rt1[:]]
    sga = [arena[:, 2048:2560], arena[:, 2560:3072]]
    sgb = [arena[:, 3072:3584], arena[:, 3584:4096]]
    m1b = arena[:, 1024:1536]
    m2b = arena[:, 1536:2048]

    psA = [ps(f"psA{i}", [128, 512], F32) for i in range(2)]
    psS = [ps(f"psS{i}", [128, 512], F32) for i in range(3)]
    psO = [ps(f"psO{i}", [128, 512], F32) for i in range(2)]
    psTf = ps("psT0", [128, 512], F32)
    psT = [psTf[:].bitcast(BF16)]

    def record(R, wseq_in):
        B = {}

        def buf(name):
            if name not in B:
                B[name] = Buf(name)
            return B[name]

        def blk_bufs(prefix, t0, t1):
            return [buf(f"{prefix}{e}") for e in range(t0 // 512, (t1 - 1) // 512 + 1)]

        class Pool:
            def __init__(self, name, tensors, bufs=None):
                self.t = tensors
                self.b = bufs if bufs is not None else [buf(f"{name}{i}") for i in range(len(tensors))]
                self.i = 0

            def next(self):
                k = self.i % len(self.t)
                self.i += 1
                return self.t[k], self.b[k]

        pZ = Pool("psA", psA)
        pS = Pool("psS", psS)
        pO = Pool("psO", psO)
        pA = Pool("psW", psA + psS + psO, pZ.b + pS.b + pO.b)
        pT = Pool("psT", psT)
        p_xt = Pool("xt", xt)
        p_xn = Pool("xn", xn)
        p_stat = Pool("stat", stat)
        p_sq = Pool("sq", sq)
        p_sd = Pool("sd", sd)
        p_qbn = Pool("qbn", qbn)
        p_cst = Pool("cst", cst)
        p_PT = Pool("PT", PT)
        p_sz = Pool("sz", szb)
        p_rd = Pool("rd", rdb)
        rd_half = [[buf(f"rd{i}h{j}") for j in range(2)] for i in range(2)]
        rd_swa = {"i": 0}
        p_ex = Pool("ex", exb)
        p_rt = Pool("rt", rtb)
        p_w = Pool("wbuf", wbuf)
        b_const = buf("const")
        b_Et = buf("Et")
        b_ones = buf("ones")
        b_sga = [buf("sga0"), buf("sga1")]
        b_sgb = [buf("sgb0"), buf("sgb1")]
        b_m1, b_m2 = buf("m1"), buf("m2")
        b_QaT, b_QbT = buf("QaT"), buf("QbT")
        b_yaT2 = [[buf(f"yaT{k}_{i}") for i in range(4)] for k in range(2)]
        b_ybT2 = [[buf(f"ybT{k}_{i}") for i in range(4)] for k in range(2)]
        b_mT = [buf(f"mT{i}") for i in range(KC)]
        b_wscr = [buf(f"wscr{i}") for i in range(NTILES)]
        b_setup = buf("setup")

        dma_rr = [0]

        def dma(out, in_, reads, writes, owner, queue="sp", lag=0.0):
            return R.op(queue, lambda e, o=out, i=in_: e.dma_start(out=o, in_=i), reads=reads, writes=writes, dma_owner=owner, lag=lag)

        R.t = -100.0
        dma(cbf[:], cbf_d[:, :], [], [b_const], b_const)
        dma(swam[:], swam_d[:, :], [], [b_const], b_const)
        dma(gcol[:], gcol_d[:, :], [], [b_const], b_const)
        dma(gains[:], gains_d[:, :], [], [b_const], b_const)
        dma(esink[:], sink_d[:, :], [], [b_const], b_const)
        dma(valid[:], valid_d[:, :], [], [b_const], b_const)
        R.op("dve", lambda e: e.memset(epsb[:], EPS), writes=[b_const])
        R.op("dve", lambda e: e.memset(cpow[:, 0:1], -0.5), writes=[b_const])
        R.op("dve", lambda e: e.memset(cpow[:, 1:2], -1.0), writes=[b_const])
        R.op("dve", lambda e: e.memset(cpow[:, 2:3], 1.0), writes=[b_const])
        R.op("act", lambda e: e.activation(out=esink[:], in_=esink[:], func=AF.Exp), reads=[b_const], writes=[b_const])
        R.op("pool", lambda e: e.memset(Va[:], 1.0), writes=[b_ones])
        R.op("pool", lambda e: e.memset(Vb[:], 1.0), writes=[b_ones])
        b_st = [buf("stage0"), buf("stage1")]
        stg = [arena[:, 0:2048], arena[:, 2048:4096]]
        NE = NVAR * 64
        dma(stg[1][:, 0:NE], Mk_d[:, :], [], [b_st[1]], b_st[1])
        for h in range(8):
            dma(stg[0][:, 0:NE], Gt_d[:, h * NE:(h + 1) * NE], [], [b_st[0]], b_st[0])
            R.op("act", lambda e: e.activation(out=stg[0][:, 0:NE], in_=stg[0][:, 0:NE], func=AF.Exp),
                 reads=[b_st[0]], writes=[b_st[0]])
            R.op("dve", lambda e, h=h: e.tensor_tensor(out=Et[:, h].rearrange("p v c -> p (v c)"), in0=stg[0][:, 0:NE],
                                                       in1=stg[1][:, 0:NE], op=ALU.mult),
                 reads=[b_st[0], b_st[1]], writes=[b_Et])

        w_in_r = w_in.rearrange("(kc p) c -> p kc c", p=128)
        w_o_r = w_o.rearrange("(kc p) c -> p kc c", p=128)
        w_oa_r = w_oa.rearrange("(kc p) c -> p kc c", p=128)
        w_ob_r = w_ob.rearrange("(kc p) c -> p kc c", p=128)

        def tile_srcs(t):
            def win(c0, n=256, dst0=0):
                return [(("k8", dst0, n), w_in_r[:, :, c0:c0 + n])]
            if t in T_KA:
                return win(C_KA + 256 * T_KA.index(t))
            if t == T_KBVB:
                return win(C_KB)
            if t in T_VA:
                return win(C_VA + 256 * T_VA.index(t))
            if t in T_QA:
                return win(C_QA + 256 * T_QA.index(t))
            if t in T_QB:
                i = T_QB.index(t)
                out = []
                for cc in range(2):
                    m = 2 * i + cc
                    out.append((("k8", cc * 128, 64), w_in_r[:, :, C_QB + 64 * m:C_QB + 64 * m + 64]))
                    out.append((("k8", cc * 128 + 64, 64), w_in_r[:, :, C_QB + 64 * (4 + m):C_QB + 64 * (4 + m) + 64]))
                return out
            if t in T_ZA:
                return win(C_ZA + 256 * T_ZA.index(t))
            if t in T_ZB:
                return win(C_ZB + 256 * T_ZB.index(t))
            if t in T_GA:
                return win(C_GA + 256 * T_GA.index(t))
            if t in T_GB:
                return win(C_GB + 256 * T_GB.index(t))
            if t in T_WAB:
                i = T_WAB.index(t)
                return [(("ab", 0), w_oa_r[:, :, 256 * i:256 * i + 256]), (("ab", 1), w_ob_r[:, :, 256 * i:256 * i + 256])]
            if t in T_WO:
                i = T_WO.index(t)
                return [(("k8", 0, 256), w_o_r[:, :, 256 * i:256 * i + 256])]
            raise ValueError

        cvt_eng = ["dve", "act"]
        b_cv = [buf("cv0"), buf("cv1")]
        for t in range(NTILES):
            s = t % 2
            st = stg[s]
            ckey = -100.0 if t < 5 else 2.0 + 1.4 * (t - 5)
            R.t = ckey - 1.4 if t >= 7 else ckey
            for (lay, src) in tile_srcs(t):
                if lay[0] == "k8":
                    dst = st.rearrange("p (k c) -> p k c", k=8)[:, :, lay[1]:lay[1] + lay[2]]
                else:
                    dst = st.rearrange("p (a k c) -> p a k c", a=2, k=4)[:, lay[1]]
                dma(dst, src, [], [b_st[s]], b_st[s])
            R.t = ckey
            wt = mT[:, 4 * s:4 * s + 4, :].rearrange("p k t -> p (k t)")
            wb = b_cv[s]
            ce = cvt_eng[t % 2]
            if t <= T_GB[-1]:
                for kc in range(KC):
                    wsl = wt.rearrange("p (k c) -> p k c", k=8)[:, kc, :]
                    ssl = st.rearrange("p (k c) -> p k c", k=8)[:, kc, :]
                    if (kc + t) % 2 == 0:
                        R.op("act", lambda e, wsl=wsl, ssl=ssl, kc=kc: e.activation(out=wsl, in_=ssl, func=AF.Copy, scale=gcol[:, kc:kc + 1]),
                             reads=[b_st[s], b_const], writes=[wb])
                    else:
                        R.op("dve", lambda e, wsl=wsl, ssl=ssl, kc=kc: e.tensor_scalar(out=wsl, in0=ssl, scalar1=gcol[:, kc:kc + 1], scalar2=None,
                                                                                     op0=ALU.mult), reads=[b_st[s], b_const], writes=[wb])
            elif ce == "act":
                R.op("act", lambda e, wt=wt, st=st: e.activation(out=wt, in_=st, func=AF.Copy), reads=[b_st[s]], writes=[wb])
            else:
                R.op(ce, lambda e, wt=wt, st=st: e.tensor_copy(out=wt, in_=st), reads=[b_st[s]], writes=[wb])
            dma(wscr[t], wt, [wb], [b_wscr[t]], wb)

        setup_bufs = [b_const, b_Et, b_ones, b_st[0], b_st[1]] + b_cv + b_wscr
        for e in ("pe", "act", "dve", "pool", "sp"):
            pass
        R.t = 2.0 + 1.4 * (NTILES - 5) + 0.5
        bar_tok = R.op("sp", lambda e: e.nop(), reads=setup_bufs, writes=[b_setup])
        arena_bufs = p_ex.b + b_sga + b_sgb + [b_m1, b_m2] + b_mT
        for bb in arena_bufs:
            bb.writers = [bar_tok]

        wseq = wseq_in
        wcalls = []
        wstate = {"issued": 0, "used": 0, "slots": []}

        def w_issue(t):
            wt, wb = p_w.next()
            dma(wt[:], wscr[t], [b_wscr[t]], [wb], wb)
            wstate["slots"].append((wt, wb, t))
            wstate["issued"] += 1

        def w_next(expect):
            wcalls.append(expect)
            if wseq is None:
                w_issue(expect)
            else:
                while wstate["issued"] < min(len(wseq), wstate["used"] + 2):
                    w_issue(wseq[wstate["issued"]])
            wt, wb, t = wstate["slots"][wstate["used"]]
            assert t == expect, (t, expect)
            wstate["used"] += 1
            return wt, wb

        clk = [0.0]

        def tick(d=1.0):
            clk[0] += d
            R.t = clk[0]
            return clk[0]

        def proj_fm(wt, wb, col0, hsl, hb, pst, psb, lag=0.0):
            wv = wt.rearrange("p (k c) -> p k c", k=8)
            for kc in range(KC):
                R.op("pe", lambda e, kc=kc: e.matmul(pst[:], lhsT=wv[:, kc, col0:col0 + 128], rhs=hsl(kc),
                                                   start=(kc == 0), stop=(kc == KC - 1)),
                     reads=[wb] + hb, writes=[psb], lag=lag)

        L1 = 1.5
        XLAG = -2.0

        def qknorm(pst, psb, gcol, out_ap, out_bufs, pool):
            sqt, sqb = p_sq.next()
            R.op("act", lambda e: e.activation(out=sqt[:], in_=pst[:], func=AF.Square), reads=[psb], writes=[sqb])
            ms, msb = pool.next()
            R.op("pe", lambda e: e.matmul(ms[:], lhsT=BDm, rhs=sqt[:], start=True, stop=True), reads=[sqb, b_const], writes=[msb], lag=L1)
            sdt, sdb = p_sd.next()
            R.op("act", lambda e: e.activation(out=sdt[:], in_=ms[:], func=AF.Ln, bias=epsb[:, 0:1]), reads=[msb, b_const],
                 writes=[sdb], lag=L1)
            R.op("act", lambda e: e.activation(out=sdt[:], in_=sdt[:], func=AF.Exp, scale=-0.5), reads=[sdb], writes=[sdb], lag=L1)
            R.op("dve", lambda e: e.scalar_tensor_tensor(out=out_ap, in0=pst[:], scalar=gains[:, gcol:gcol + 1], in1=sdt[:],
                                                         op0=ALU.mult, op1=ALU.mult),
                 reads=[psb, sdb, b_const], writes=out_bufs, lag=L1)

        def rotary(qn_t, qn_b, cs_t, cs_b, out_ap, out_bufs, pool):
            rp, rpb_ = pool.next()
            R.op("pe", lambda e: e.matmul(rp[:], lhsT=Perm, rhs=qn_t[:], start=True, stop=True), reads=[qn_b, b_const], writes=[rpb_], lag=L1 + 1)
            t1, t1b = p_rt.next()
            t2, t2b = p_rt.next()
            R.op("pool", lambda e: e.tensor_tensor(out=t1, in0=qn_t[:], in1=cs_t[:, 0, :], op=ALU.mult), reads=[qn_b, cs_b], writes=[t1b], lag=L1 + 1)
            R.op("dve", lambda e: e.tensor_tensor(out=t2, in0=rp[:], in1=cs_t[:, 1, :], op=ALU.mult), reads=[rpb_, cs_b], writes=[t2b], lag=L1 + 1)
            R.op("pool", lambda e: e.tensor_tensor(out=out_ap, in0=t1, in1=t2, op=ALU.add), reads=[t1b, t2b], writes=out_bufs, lag=L1 + 1)

        def xnorm_tile(s, e_, tt):
            tok = e_ * 512 + tt * 128
            hbuf = buf(f"hT{e_}")
            xtile, xb = p_xt.next()
            dma(xtile[:], xs[s, tok:tok + 128, :], [], [xb], xb, lag=XLAG)
            stt, stb = p_stat.next()
            xnt, xnb = p_xn.next()
            R.op("dve", lambda e: e.memset(stt[:], 0.0), writes=[stb])
            R.op("act", lambda e: e.activation(out=xnt[:], in_=xtile[:], func=AF.Square, accum_out=stt[:, 0:1]),
                 reads=[xb], writes=[xnb, stb])
            R.op("act", lambda e: e.activation(out=stt[:, 1:2], in_=stt[:, 0:1], func=AF.Ln, bias=epsb[:, 0:1], scale=1.0 / D),
                 reads=[stb, b_const], writes=[stb])
            R.op("act", lambda e: e.activation(out=stt[:, 2:3], in_=stt[:, 1:2], func=AF.Exp, scale=-0.5), reads=[stb], writes=[stb])
            R.op("dve", lambda e: e.tensor_scalar(out=xnt[:], in0=xtile[:], scalar1=stt[:, 2:3], scalar2=None, op0=ALU.mult),
                 reads=[xb, stb], writes=[xnb])
            tp, tpb = pT.next()
            for kc in range(KC):
                R.op("pe", lambda e, kc=kc: e.transpose(tp[:, kc * 128:(kc + 1) * 128], xnt[:, kc * 128:(kc + 1) * 128], ident),
                     reads=[xnb, b_const], writes=[tpb], lag=L1)
            R.op("dve", lambda e: e.tensor_copy(out=hT[:, :, tok:tok + 128], in_=tp[:].rearrange("p (k t) -> p k t", k=KC)),
                 reads=[tpb], writes=[hbuf], lag=L1)

        def kv_items(s, e_, pool, xn_next):
            t0 = e_ * 512
            hbuf = buf(f"hT{e_}")
            hsl = lambda kc: hT[:, kc, t0:t0 + 512]
            tick()
            cst_t, cst_b = p_cst.next()
            dma(cst_t[:], cs_d[s, :, :, t0:t0 + 512], [], [cst_b], cst_b)
            for i in range(2):
                for cc in range(2):
                    wt, wb = w_next(T_KA[i])
                    c = 2 * i + cc
                    tick()
                    if xn_next:
                        xnorm_tile(s, e_ + 1, c)
                    pst, psb = pool.next()
                    proj_fm(wt, wb, cc * 128, hsl, [hbuf], pst, psb)
                    qknorm(pst, psb, 1, KaT[:, c, t0:t0 + 512], [buf(f"KaT{e_}")], pool)
                    yield
            tick()
            wt, wb = w_next(T_KBVB)
            pst, psb = pool.next()
            proj_fm(wt, wb, 0, hsl, [hbuf], pst, psb)
            qt, qb_ = p_qbn.next()
            qknorm(pst, psb, 3, qt[:], [qb_], pool)
            rotary(qt, qb_, cst_t, cst_b, KbT[:, t0:t0 + 512], [buf(f"KbT{e_}")], pool)
            wv = wt.rearrange("p (k c) -> p k c", k=8)
            tick(3)
            for tt in range(4):
                tick()
                tok = t0 + tt * 128
                pst, psb = pool.next()
                for kc in range(KC):
                    R.op("pe", lambda e, kc=kc, pst=pst, tok=tok, wv=wv: e.matmul(pst[:, 0:128], lhsT=hT[:, kc, tok:tok + 128],
                                                                                 rhs=wv[:, kc, 128:256], start=(kc == 0),
                                                                                 stop=(kc == KC - 1)),
                         reads=[wb, hbuf], writes=[psb])
                ti = tok // 128
                R.op("dve", lambda e, pst=pst, ti=ti: e.tensor_copy(
                    out=Vb[:, ti, 64:320].rearrange("p (a c) -> p a c", a=2)[:, :, 0:64],
                    in_=pst[:, 0:128].rearrange("p (a c) -> p a c", a=2)),
                    reads=[psb, b_ones], writes=[buf(f"Vb{e_}")])
            yield
            tick()
            wts = [w_next(T_VA[0]), w_next(T_VA[1])]
            for tt in range(4):
                tick()
                tok = t0 + tt * 128
                pst, psb = pool.next()
                for i in range(2):
                    wv = wts[i][0].rearrange("p (k c) -> p k c", k=8)
                    for kc in range(KC):
                        R.op("pe", lambda e, kc=kc, pst=pst, tok=tok, wv=wv, i=i: e.matmul(
                            pst[:, i * 256:(i + 1) * 256], lhsT=hT[:, kc, tok:tok + 128], rhs=wv[:, kc, :],
                            start=(kc == 0), stop=(kc == KC - 1)), reads=[wts[i][1], hbuf], writes=[psb])
                ti = tok // 128
                R.op("dve", lambda e, pst=pst, ti=ti: e.tensor_copy(
                    out=Va[:, ti, :].rearrange("p (hp a c) -> p hp a c", hp=4, a=3)[:, :, 0, :],
                    in_=pst[:].rearrange("p (hp a c) -> p hp a c", hp=4, a=2)[:, :, 0, :]),
                    reads=[psb, b_ones], writes=[buf(f"Va{e_}")])
                R.op("pool" if False else "act", lambda e, pst=pst, ti=ti: e.activation(
                    out=Va[:, ti, :].rearrange("p (hp a c) -> p hp a c", hp=4, a=3)[:, :, 2, :],
                    in_=pst[:].rearrange("p (hp a c) -> p hp a c", hp=4, a=2)[:, :, 1, :], func=AF.Copy),
                    reads=[psb, b_ones], writes=[buf(f"Va{e_}")])
            yield

        def run_all(gen):
            for _ in gen:
                pass

        def qproj(s, b_):
            q0 = 256 + 512 * b_
            hb = blk_bufs("hT", q0, q0 + 512)
            hsl = lambda kc: hT[:, kc, q0:q0 + 512]
            tick()
            cst_t, cst_b = p_cst.next()
            dma(cst_t[:], cs_d[s, :, :, q0:q0 + 512], [], [cst_b], cst_b)
            for i in range(2):
                wt, wb = w_next(T_QA[i])
                for cc in range(2):
                    c = 2 * i + cc
                    tick()
                    pst, psb = pA.next()
                    proj_fm(wt, wb, cc * 128, hsl, hb, pst, psb)
                    qknorm(pst, psb, 0, QaT[:, c, :], [b_QaT], pA)
            for i in range(2):
                wt, wb = w_next(T_QB[i])
                for cc in range(2):
                    m = 2 * i + cc
                    tick()
                    pst, psb = pA.next()
                    proj_fm(wt, wb, cc * 128, hsl, hb, pst, psb)
                    qt, qb_ = p_qbn.next()
                    qknorm(pst, psb, 2, qt[:], [qb_], pA)
                    rotary(qt, qb_, cst_t, cst_b, QbT[:, m, :], [b_QbT], pA)
            tick(3)

        def keep_warm(n):
            for _ in range(n):
                R.op("pe", lambda e: e.matmul(psTf[:, 0:384], lhsT=ident, rhs=cbf[:, 0:384], start=True, stop=True),
                     reads=[b_const], writes=[pT.b[0]])

        def attention(s, b_, filler):
            vbase = s * NVALID
            q0 = 256 + 512 * b_
            hb = blk_bufs("hT", q0, q0 + 512)
            hsl = lambda kc: hT[:, kc, q0:q0 + 512]
            yaT = yaT2[b_ % 2]
            ybT = ybT2[b_ % 2]
            b_yaT = b_yaT2[b_ % 2]
            b_ybT = b_ybT2[b_ % 2]

            def zproj(wt, wb, cc):
                pst, psb = pZ.next()
                proj_fm(wt, wb, cc * 128, hsl, hb, pst, psb)
                szt, szb_ = p_sz.next()
                ZL = 3.0
                R.op("act", lambda e: e.activation(out=szt[:], in_=pst[:], func=AF.Exp, scale=-1.0), reads=[psb], writes=[szb_], lag=ZL)
                R.op("act", lambda e: e.activation(out=szt[:], in_=szt[:], func=AF.Ln, bias=cpow[:, 2:3]), reads=[szb_, b_const], writes=[szb_], lag=ZL)
                R.op("act", lambda e: e.activation(out=szt[:], in_=szt[:], func=AF.Exp, scale=-1.0), reads=[szb_], writes=[szb_], lag=ZL)
                R.op("dve", lambda e: e.tensor_tensor(out=szt[:], in0=szt[:], in1=pst[:], op=ALU.mult),
                     reads=[szb_, psb], writes=[szb_], lag=ZL)
                return szt, szb_

            LAG = 6.0

            def row_groups(kap):
                bnd = None
                interior = []
                for rr in range(8):
                    r = 8 * b_ + rr
                    if r < 4:
                        if kap <= 5:
                            bnd = (0, 3)
                    elif r >= RS - 4:
                        if kap >= 2:
                            bnd = (4, 7)
                    else:
                        if 2 * kap - 7 <= rr <= 2 * kap + 1:
                            interior.append(rr)
                ig = (interior[0], interior[-1]) if interior else None
                return bnd, ig

            sz_next = None
            for i in range(2):
                for cc in range(2):
                    hp = 2 * i + cc
                    tick()
                    if sz_next is None:
                        wt, wb = w_next(T_ZA[i])
                        sz_next = zproj(wt, wb, cc)
                    szt, szb_ = sz_next
                    obank = [pO.next(), pO.next()]
                    for hh in range(2):
                        if hh == 1 and hp < 3:
                            wt, wb = w_next(T_ZA[(hp + 1) // 2])
                            sz_next = zproj(wt, wb, (hp + 1) % 2)
                        h = 2 * hp + hh
                        pb = 64 * hh
                        ot, ob = obank[hh]
                        first_pv = True
                        for kap in range(8):
                            tick()
                            bg, ig = row_groups(kap)
                            groups = [g_ for g_ in (bg, ig) if g_ is not None]
                            if not groups:
                                continue
                            ra = min(g_[0] for g_ in groups)
                            rb = max(g_[1] for g_ in groups)
                            nr = rb - ra + 1
                            assert sum(g_[1] - g_[0] + 1 for g_ in groups) == nr
                            kt = 512 * b_ + 128 * kap
                            ti = kt // 128
                            kb_ = blk_bufs("KaT", kt, kt + 128)
                            vb_ = blk_bufs("Va", kt, kt + 128)
                            keep_warm(2)
                            st_, stb_ = pS.next()
                            R.op("pe", lambda e, st_=st_, pb=pb, kt=kt, hp=hp, ra=ra, rb=rb, nr=nr: e.matmul(
                                st_[:, 0:64 * nr], lhsT=KaT[pb:pb + 64, hp, kt:kt + 128],
                                rhs=QaT[pb:pb + 64, hp, ra * 64:(rb + 1) * 64], start=True, stop=True),
                                reads=kb_ + [b_QaT], writes=[stb_])
                            ext_, exb_ = p_ex.next()
                            R.op("act", lambda e, st_=st_, ext_=ext_, nr=nr: e.activation(out=ext_[:, 0:nr * 64], in_=st_[:, 0:nr * 64],
                                                                                         func=AF.Exp, scale=0.125),
                                 reads=[stb_], writes=[exb_])
                            ptt, ptb = p_PT.next()
                            for (ga, gb_) in groups:
                                gn = gb_ - ga + 1
                                c0 = (ga - ra) * 64
                                dr0 = 2 * kap + 3 - ga
                                ex3 = ext_[:, c0:c0 + gn * 64].rearrange("p (j c) -> p j c", j=gn)
                                pt3 = ptt[:, c0:c0 + gn * 64].rearrange("p (j c) -> p j c", j=gn)
                                if (ga, gb_) == bg:
                                    v0 = 13 - dr0
                                    ev = Et[:, h, v0:v0 + gn, :]
                                    R.op("pool", lambda e, ex3=ex3, ev=ev: e.tensor_tensor(out=ex3, in0=ex3, in1=ev, op=ALU.mult),
                                         reads=[exb_, b_Et], writes=[exb_])
                                    kp = kap if ga == 0 else kap - 2
                                    vv = valid[:, vbase:vbase + 48].rearrange("p (r k) -> p r k", k=6)[:, ga:ga + gn, kp]
                                    vv = vv.unsqueeze(2).to_broadcast([128, gn, 64])
                                    R.op("dve", lambda e, pt3=pt3, ex3=ex3, vv=vv: e.tensor_tensor(out=pt3, in0=ex3, in1=vv, op=ALU.mult),
                                         reads=[exb_, b_const], writes=[ptb])
                                else:
                                    assert 2 <= dr0 - (gn - 1) and dr0 <= 10, (dr0, gn)
                                    v0 = 14 + 10 - dr0
                                    ev = Et[:, h, v0:v0 + gn, :]
                                    R.op("dve" if (kap % 4 != 3) else "pool",
                                         lambda e, pt3=pt3, ex3=ex3, ev=ev: e.tensor_tensor(out=pt3, in0=ex3, in1=ev, op=ALU.mult),
                                         reads=[exb_, b_Et], writes=[ptb])
                            R.op("pe", lambda e, ot=ot, ti=ti, hp=hp, hh=hh, ptt=ptt, ra=ra, rb=rb, nr=nr, fp=first_pv: e.matmul(
                                ot[:, ra * 64:(rb + 1) * 64], lhsT=Va[:, ti, hp * 192 + hh * 64:hp * 192 + hh * 64 + 128],
                                rhs=ptt[:, 0:64 * nr], start=fp, stop=False, skip_group_check=True),
                                reads=vb_ + [ptb, b_ones], writes=[ob], lag=LAG)
                            first_pv = False
                        po = 64 - pb
                        rdt, _unused = p_rd.next()
                        rdL = rd_half[rdb.index(rdt)]
                        R.op("act", lambda e, ot=ot, rdt=rdt, pb=pb, po=po: e.activation(out=rdt[pb:pb + 64, :], in_=ot[po:po + 64, :],
                                                                                       func=AF.Ln),
                             reads=[ob], writes=rdL, lag=LAG + 0.5)
                        R.op("act", lambda e, rdt=rdt, pb=pb: e.activation(out=rdt[pb:pb + 64, :], in_=rdt[pb:pb + 64, :], func=AF.Exp, scale=-1.0),
                             reads=rdL, writes=rdL, lag=LAG + 0.5)
                        R.op("pool", lambda e, rdt=rdt, szt=szt, pb=pb: e.tensor_tensor(out=rdt[pb:pb + 64, :], in0=rdt[pb:pb + 64, :],
                                                                                       in1=szt[pb:pb + 64, :], op=ALU.mult),
                             reads=rdL + [szb_], writes=rdL, lag=LAG + 0.5)
                        R.op("dve", lambda e, ot=ot, rdt=rdt, pb=pb, hp=hp: e.tensor_tensor(
                            out=yaT[pb:pb + 64, hp, :], in0=ot[pb:pb + 64, :], in1=rdt[pb:pb + 64, :], op=ALU.mult),
                            reads=[ob] + rdL, writes=[b_yaT[hp]], lag=LAG + 0.5)
                        filler()

            for g in range(2):
                tick(8)
                wt, wb = w_next(T_ZB[g])
                sz2 = [zproj(wt, wb, 0), zproj(wt, wb, 1)]
                pb = 64 * g
                for n_ in range(4):
                    tick(2)
                    qt0 = q0 + 128 * n_
                    pts = []
                    keep_warm(3)
                    for dl in (-1, 0, 1):
                        kt0 = qt0 + 128 * dl
                        st_, stb_ = pS.next()
                        R.op("pe", lambda e, st_=st_, kt0=kt0, pb=pb, n_=n_: e.matmul(
                            st_[:].rearrange("p (m q) -> p m q", m=4), lhsT=KbT[pb:pb + 64, kt0:kt0 + 128],
                            rhs=QbT[pb:pb + 64, :, n_ * 128:(n_ + 1) * 128], start=True, stop=True),
                            reads=blk_bufs("KbT", kt0, kt0 + 128) + [b_QbT], writes=[stb_])
                        ptt, ptb = p_PT.next()
                        if dl == 0:
                            R.op("act", lambda e, st_=st_, ptt=ptt: e.activation(out=ptt[:], in_=st_[:], func=AF.Exp, scale=0.125),
                                 reads=[stb_], writes=[ptb])
                        else:
                            ext_, exb_ = p_ex.next()
                            R.op("act", lambda e, st_=st_, ext_=ext_: e.activation(out=ext_, in_=st_[:], func=AF.Exp, scale=0.125),
                                 reads=[stb_], writes=[exb_])
                            mk = swam[:, 0:128] if dl == -1 else swam[:, 128:256]
                            mk4 = mk.unsqueeze(1).to_broadcast([128, 4, 128])
                            edge = (dl == -1 and b_ == 0 and n_ == 0) or (dl == 1 and b_ == NQB - 1 and n_ == 3)
                            ex3 = ext_.rearrange("p (m q) -> p m q", m=4)
                            pt3 = ptt[:].rearrange("p (m q) -> p m q", m=4)
                            if edge:
                                vc = vbase + 48 + (0 if dl == -1 else 1)
                                R.op("dve", lambda e, ex3=ex3, pt3=pt3, mk4=mk4, vc=vc: e.scalar_tensor_tensor(
                                    out=pt3, in0=ex3, scalar=valid[:, vc:vc + 1], in1=mk4, op0=ALU.mult, op1=ALU.mult),
                                    reads=[exb_, b_const], writes=[ptb])
                            else:
                                R.op("pool" if dl == -1 else "dve",
                                     lambda e, ex3=ex3, pt3=pt3, mk4=mk4: e.tensor_tensor(out=pt3, in0=ex3, in1=mk4, op=ALU.mult),
                                     reads=[exb_, b_const], writes=[ptb])
                        pts.append((ptt, ptb, kt0))
                    ot, ob = pO.next()
                    for par in range(2):
                        vcol = g * 128 + 64 if par == 0 else g * 128
                        for k_, (ptt, ptb, kt0) in enumerate(pts):
                            ti = kt0 // 128
                            rhs = ptt[:].rearrange("p (a b q) -> p a b q", a=2, b=2)[:, :, par, :]
                            R.op("pe", lambda e, ot=ot, ti=ti, vcol=vcol, rhs=rhs, par=par, k_=k_: e.matmul(
                                ot[:, par * 256:(par + 1) * 256].rearrange("p (a q) -> p a q", a=2), lhsT=Vb[:, ti, vcol:vcol + 128],
                                rhs=rhs, start=(k_ == 0), stop=(k_ == 2)),
                                reads=blk_bufs("Vb", kt0, kt0 + 128) + [ptb, b_ones], writes=[ob], lag=2.5)
                    for par in range(2):
                        pbo = 64 * par
                        pde = 64 - pbo
                        k4 = rd_swa["i"] % 4
                        rd_swa["i"] += 1
                        rdt = rdb[k4 // 2]
                        co = 256 * (k4 % 2)
                        rdb_ = rd_half[k4 // 2][k4 % 2]
                        for a in range(2):
                            h = 4 * g + 2 * a + par
                            R.op("act", lambda e, ot=ot, rdt=rdt, pbo=pbo, pde=pde, par=par, a=a, h=h, co=co: e.activation(
                                out=rdt[pbo:pbo + 64, co + a * 128:co + (a + 1) * 128],
                                in_=ot[pde:pde + 64, par * 256 + a * 128:par * 256 + (a + 1) * 128],
                                func=AF.Ln, bias=esink[pbo:pbo + 64, h:h + 1]), reads=[ob, b_const], writes=[rdb_], lag=3.0)
                        R.op("act", lambda e, rdt=rdt, pbo=pbo, co=co: e.activation(out=rdt[pbo:pbo + 64, co:co + 256], in_=rdt[pbo:pbo + 64, co:co + 256],
                                                                            func=AF.Exp, scale=-1.0),
                             reads=[rdb_], writes=[rdb_], lag=3.0)
                        for a in range(2):
                            szt, szb_ = sz2[a]
                            R.op("pool", lambda e, rdt=rdt, szt=szt, pbo=pbo, a=a, n_=n_, co=co: e.tensor_tensor(
                                out=rdt[pbo:pbo + 64, co + a * 128:co + (a + 1) * 128], in0=rdt[pbo:pbo + 64, co + a * 128:co + (a + 1) * 128],
                                in1=szt[pbo:pbo + 64, n_ * 128:(n_ + 1) * 128], op=ALU.mult),
                                reads=[rdb_, szb_], writes=[rdb_], lag=3.0)
                        R.op("dve", lambda e, ot=ot, rdt=rdt, pbo=pbo, par=par, g=g, n_=n_, co=co: e.tensor_tensor(
                            out=ybT[pbo:pbo + 64, 2 * g:2 * g + 2, n_ * 128:(n_ + 1) * 128],
                            in0=ot[pbo:pbo + 64, par * 256:(par + 1) * 256].rearrange("p (a q) -> p a q", a=2),
                            in1=rdt[pbo:pbo + 64, co:co + 256].rearrange("p (a q) -> p a q", a=2), op=ALU.mult),
                            reads=[ob, rdb_], writes=[b_ybT[2 * g], b_ybT[2 * g + 1]], lag=3.0)
                    if n_ % 2 == 1:
                        filler()
            tick(4)

        def merge_items(s, b_, pool):
            q0 = 256 + 512 * b_
            hb = blk_bufs("hT", q0, q0 + 512)
            hsl = lambda kc: hT[:, kc, q0:q0 + 512]
            yaT = yaT2[b_ % 2]
            ybT = ybT2[b_ % 2]
            b_yaT = b_yaT2[b_ % 2]
            b_ybT = b_ybT2[b_ % 2]
            for i in range(4):
                tick()
                wga, wgab = w_next(T_GA[i])
                for cc in range(2):
                    pga, pgab = pool.next()
                    proj_fm(wga, wgab, cc * 128, hsl, hb, pga, pgab)
                    R.op("act", lambda e, pga=pga, cc=cc: e.activation(out=sga[cc], in_=pga[:], func=AF.Exp, scale=-1.0),
                         reads=[pgab], writes=[b_sga[cc]])
                    R.op("act", lambda e, cc=cc: e.activation(out=sga[cc], in_=sga[cc], func=AF.Ln, bias=cpow[:, 2:3]),
                         reads=[b_sga[cc], b_const], writes=[b_sga[cc]])
                    R.op("act", lambda e, cc=cc: e.activation(out=sga[cc], in_=sga[cc], func=AF.Exp, scale=-1.0),
                         reads=[b_sga[cc]], writes=[b_sga[cc]])
                tick()
                wgb, wgbb = w_next(T_GB[i])
                for cc in range(2):
                    pgb, pgbb = pool.next()
                    proj_fm(wgb, wgbb, cc * 128, hsl, hb, pgb, pgbb)
                    R.op("act", lambda e, pgb=pgb, cc=cc: e.activation(out=sgb[cc], in_=pgb[:], func=AF.Exp, scale=-1.0),
                         reads=[pgbb], writes=[b_sgb[cc]])
                    R.op("act", lambda e, cc=cc: e.activation(out=sgb[cc], in_=sgb[cc], func=AF.Ln, bias=cpow[:, 2:3]),
                         reads=[b_sgb[cc], b_const], writes=[b_sgb[cc]])
                    R.op("act", lambda e, cc=cc: e.activation(out=sgb[cc], in_=sgb[cc], func=AF.Exp, scale=-1.0),
                         reads=[b_sgb[cc]], writes=[b_sgb[cc]])
                tick()
                wab, wabb = w_next(T_WAB[i])
                wabv = wab.rearrange("p (a k c) -> p a k c", a=2, k=4)
                for cc in range(2):
                    c = 2 * i + cc
                    pa, pab = pool.next()
                    for kc in range(4):
                        R.op("pe", lambda e, kc=kc, pa=pa, cc=cc, wabv=wabv: e.matmul(pa[:], lhsT=wabv[:, 0, kc, cc * 128:(cc + 1) * 128],
                                                                                     rhs=yaT[:, kc, :], start=(kc == 0), stop=(kc == 3)),
                             reads=[wabb] + b_yaT, writes=[pab])
                    R.op("dve", lambda e, pa=pa, cc=cc: e.tensor_tensor(out=m1b, in0=pa[:], in1=sga[cc], op=ALU.mult),
                         reads=[pab, b_sga[cc]], writes=[b_m1])
                    pb2, pbb = pool.next()
                    for kc in range(4):
                        R.op("pe", lambda e, kc=kc, pb2=pb2, cc=cc, wabv=wabv: e.matmul(pb2[:], lhsT=wabv[:, 1, kc, cc * 128:(cc + 1) * 128],
                                                                                       rhs=ybT[:, kc, :], start=(kc == 0), stop=(kc == 3)),
                             reads=[wabb] + b_ybT, writes=[pbb])
                    R.op("dve", lambda e, pb2=pb2, cc=cc: e.tensor_tensor(out=m2b, in0=pb2[:], in1=sgb[cc], op=ALU.mult),
                         reads=[pbb, b_sgb[cc]], writes=[b_m2])
                    R.op("pool", lambda e, c=c: e.tensor_tensor(out=mT[:, c, :], in0=m1b, in1=m2b, op=ALU.add),
                         reads=[b_m1, b_m2], writes=[b_mT[c]])
                yield
            tick()
            xres = []
            for tt in range(4):
                xtile, xb = p_xt.next()
                dma(xtile[:], xs[s, q0 + tt * 128:q0 + (tt + 1) * 128, :], [], [xb], xb)
                xres.append((xtile, xb))
            for cg in range(4):
                tick()
                wo_t, wo_b = w_next(T_WO[cg])
                wov = wo_t.rearrange("p (k c) -> p k c", k=8)
                for tt in range(4):
                    pst, psb = pool.next()
                    for kc in range(KC):
                        R.op("pe", lambda e, kc=kc, pst=pst, tt=tt, wov=wov: e.matmul(
                            pst[:, 0:256], lhsT=mT[:, kc, tt * 128:(tt + 1) * 128], rhs=wov[:, kc, :],
                            start=(kc == 0), stop=(kc == KC - 1)), reads=[wo_b] + b_mT, writes=[psb])
                    xtile, xb = xres[tt]
                    R.op("dve", lambda e, pst=pst, xtile=xtile, cg=cg: e.tensor_tensor(
                        out=xtile[:, cg * 256:(cg + 1) * 256], in0=pst[:, 0:256], in1=xtile[:, cg * 256:(cg + 1) * 256],
                        op=ALU.add), reads=[psb, xb], writes=[xb])
                yield
            tick()
            for tt in range(4):
                xtile, xb = xres[tt]
                r0 = 512 * b_ + 128 * tt
                tk = dma(y[s, r0:r0 + 128, :], xtile[:], [xb], [], xb)
                R.store_toks.append(tk)
            yield

        clk[0] = 0.0
        for s in range(nslot):
            tick(2)
            for tt in range(4):
                tick()
                xnorm_tile(s, 0, tt)
            tick(2)
            run_all(kv_items(s, 0, pA, True))
            run_all(kv_items(s, 1, pA, True))
            pending = [kv_items(s, 2, pZ, False)]

            def filler():
                while pending:
                    try:
                        next(pending[0])
                        tick(3)
                        return
                    except StopIteration:
                        pending.pop(0)

            def drain():
                while pending:
                    filler()

            for b_ in range(NQB):
                qproj(s, b_)
                attention(s, b_, filler)
                drain()
                if b_ + 1 < NQB:
                    pending.append(merge_items(s, b_, pZ))
                else:
                    run_all(merge_items(s, b_, pA))

        tick(10)

        if dbg and wseq_in is not None:
            allb = list(B.values())
            def dump(name, t, shape, dt):
                o = nc.dram_tensor(name, list(shape), dt, kind="ExternalOutput").ap()
                tk = dma(o, t, allb, [], buf("dbg_" + name))
                R.store_toks.append(tk)
            dump("d_hT", hT[:].rearrange("p k t -> p (k t)"), [128, KC * EXT], BF16)
            dump("d_KaT", KaT[:].rearrange("p k t -> p (k t)"), [128, 4 * EXT], BF16)
            dump("d_KbT", KbT[:], [128, EXT], BF16)
            dump("d_Va", Va[:].rearrange("p k t -> p (k t)"), [128, NT * 768], BF16)
            dump("d_Vb", Vb[:].rearrange("p k t -> p (k t)"), [128, NT * 320], BF16)
            dump("d_QaT", QaT[:].rearrange("p k t -> p (k t)"), [128, 4 * 512], BF16)
            dump("d_QbT", QbT[:].rearrange("p k t -> p (k t)"), [128, 4 * 512], BF16)
            dump("d_yaT", yaT2[(NQB - 1) % 2][:].rearrange("p k t -> p (k t)"), [128, 4 * 512], BF16)
            dump("d_ybT", ybT2[(NQB - 1) % 2][:].rearrange("p k t -> p (k t)"), [128, 4 * 512], BF16)
            dump("d_mT", mT[:].rearrange("p k t -> p (k t)"), [128, KC * 512], BF16)
            dump("d_Et", Et[:].rearrange("p h v c -> p (h v c)"), [128, 8 * NVAR * 64], BF16)

        last = {}
        for t in R.store_toks:
            if id(t.sem) not in last or t.order > last[id(t.sem)].order:
                last[id(t.sem)] = t
        R.op("sp", lambda e: e.nop(), extra=list(last.values()))


        return wcalls

    wseq_real = record(Rec(), None)
    record(R, wseq_real)

    R.finalize()
    for e in R.ENGS:
        R.sems[e].handle = es.enter_context(nc.semaphore(R.sems[e].name))
    for sdm in R.dsems:
        sdm.handle = es.enter_context(nc.semaphore(sdm.name))
    with nc.Block() as block:
        @block.sync
        def _(eng):
            R.emit("sp", eng)

        @block.tensor
        def _(eng):
            R.emit("pe", eng)

        @block.scalar
        def _(eng):
            R.emit("act", eng)

        @block.vector
        def _(eng):
            R.emit("dve", eng)

        @block.gpsimd
        def _(eng):
            R.emit("pool", eng)
    es.close()
    return nc


def _variants():
    v = [(13 - i, 1, 1) for i in range(14)]
    v += [(10, 1, 0)] + [(d, 1, 1) for d in range(9, 2, -1)] + [(2, 0, 1)]
    return v


def _host_constants():
    p = np.arange(128)
    ck = p % 64
    half = p // 64
    cq = np.arange(64)
    cs_ = np.clip(cq - 8, 0, 48)
    col_in = (ck[:, None] >= cs_[None, :]) & (ck[:, None] < cs_[None, :] + 16)
    dc = np.clip(ck[:, None] - cq[None, :], -15, 15) + 15
    var = _variants()
    dr_idx = np.stack([np.clip(d + half, 0, 14) for (d, lo, up) in var], axis=1)
    Mk = np.zeros((128, NVAR, 64), np.float32)
    for vi, (d, lo, up) in enumerate(var):
        hv = np.where(half == 0, lo, up).astype(np.float32)
        Mk[:, vi, :] = col_in.astype(np.float32) * hv[:, None]
    ident = np.eye(128, dtype=np.float32)
    BD = np.zeros((128, 128), np.float32)
    BD[:64, :64] = 1.0 / 64
    BD[64:, 64:] = 1.0 / 64
    partner = np.where((p % 64) < 32, p + 32, p - 32)
    Perm = np.zeros((128, 128), np.float32)
    Perm[partner, p] = 1.0
    cbf = np.concatenate([ident, BD, Perm], axis=1).astype(ml_dtypes.bfloat16)
    j = np.arange(128)[:, None]
    i = np.arange(128)[None, :]
    swam = np.concatenate([(j >= i), (j <= i)], axis=1).astype(np.float32)
    return dict(dr_idx=dr_idx, dc=dc, Mk=Mk.reshape(128, NVAR * 64), cbf=cbf, swam=swam)


def _slot_tables(row0):
    p = np.arange(128)
    half = p // 64
    val = np.zeros((128, NVALID), np.float32)
    for bi in range(8):
        r = bi if bi < 4 else RS - 4 + (bi - 4)
        start_local = -4 if bi < 4 else RS - 8
        R_ = row0 + r
        w0 = min(max(R_ - 4, 0), NROWS - 8)
        for j in range(6):
            kr = row0 + start_local + 2 * j + half
            val[:, bi * 6 + j] = ((kr >= w0) & (kr < w0 + 8)).astype(np.float32)
    val[:, 48] = 1.0 if row0 > 0 else 0.0
    val[:, 49] = 1.0 if row0 + RS < NROWS else 0.0
    pos = (row0 - 4) * GW + np.arange(EXT)
    halfd = 32
    inv = (np.float32(10000.0) ** (-(np.arange(halfd, dtype=np.float32)) / np.float32(halfd))).astype(np.float32)
    ang = pos.astype(np.float32)[None, :] * inv[:, None]
    cos = np.cos(ang).astype(np.float32)
    sin = np.sin(ang).astype(np.float32)
    d = p % 64
    cs = np.zeros((128, 2, EXT), np.float32)
    cs[:, 0, :] = cos[d % 32]
    cs[:, 1, :] = sin[d % 32] * np.where(d < 32, -1.0, 1.0)[:, None].astype(np.float32)
    return val, cs


_PROG = {}


def _make_in_maps(inputs, slot_list_per_core):
    hc = _host_constants()
    x_all = [np.asarray(inputs["x_prompt"], np.float32), np.asarray(inputs["x_sample"], np.float32)]
    seqs = [x_all[0][i] for i in range(x_all[0].shape[0])] + [x_all[1][i] for i in range(x_all[1].shape[0])]
    rpb = np.asarray(inputs["rpb_a"], np.float32)[0]
    Gt = rpb[:, hc["dr_idx"][:, :, None], hc["dc"][:, None, :]]
    Gt = np.ascontiguousarray(np.transpose(Gt, (1, 0, 2, 3))).reshape(128, 8 * NVAR * 64)
    p = np.arange(128)
    gains = np.stack([np.asarray(inputs[k], np.float32)[0][p % 64] for k in ("qn_a", "kn_a", "qn_b", "kn_b")], axis=1)
    common = {
        "w_in": np.ascontiguousarray(np.asarray(inputs["w_in"], np.float32)[0]),
        "w_out_a": np.ascontiguousarray(np.asarray(inputs["w_out_a"], np.float32)[0]),
        "w_out_b": np.ascontiguousarray(np.asarray(inputs["w_out_b"], np.float32)[0]),
        "w_o": np.ascontiguousarray(np.asarray(inputs["w_o"], np.float32)[0]),
        "gcol": np.ascontiguousarray(np.asarray(inputs["norm_g"], np.float32)[0].reshape(KC, 128).T),
        "gains": np.ascontiguousarray(gains),
        "sinkb": np.ascontiguousarray(np.broadcast_to(np.asarray(inputs["sink_b"], np.float32)[0][None, :], (128, 8))),
        "Gt": Gt, "Mk": hc["Mk"], "cbf": hc["cbf"], "swam": hc["swam"],
    }
    tabs = {}
    in_maps = []
    for slots in slot_list_per_core:
        ns = len(slots)
        xs = np.zeros((ns, EXT, D), np.float32)
        valid = np.zeros((128, ns * NVALID), np.float32)
        cs = np.zeros((ns, 128, 2, EXT), np.float32)
        for si, (sq_, row0) in enumerate(slots):
            lo = (row0 - 4) * GW
            hi = lo + EXT
            a, b = max(lo, 0), min(hi, SEQ)
            xs[si, a - lo:b - lo] = seqs[sq_][a:b]
            if row0 not in tabs:
                tabs[row0] = _slot_tables(row0)
            valid[:, si * NVALID:(si + 1) * NVALID] = tabs[row0][0]
            cs[si] = tabs[row0][1]
        m = dict(common)
        m.update({"xs": xs, "valid": valid, "cs": cs})
        in_maps.append(m)
    return in_maps


def kernel(**inputs):
    nseq_p = np.asarray(inputs["x_prompt"]).shape[0]
    nseq_s = np.asarray(inputs["x_sample"]).shape[0]
    nseq = nseq_p + nseq_s
    qper = NROWS // RS
    all_slots = [(sq_, q * RS) for sq_ in range(nseq) for q in range(qper)]
    assert len(all_slots) == NCORES * NSLOT
    per_core = [all_slots[c * NSLOT:(c + 1) * NSLOT] for c in range(NCORES)]
    in_maps = _make_in_maps(inputs, per_core)
    if "nc" not in _PROG:
        _PROG["nc"] = build_program(NSLOT)
    res = run_bass_kernel_spmd(_PROG["nc"], in_maps, core_ids=list(range(NCORES)))
    outs = [np.zeros((SEQ, D), np.float32) for _ in range(nseq)]
    for c in range(NCORES):
        yc = res.results[c]["y"]
        for si, (sq_, row0) in enumerate(per_core[c]):
            outs[sq_][row0 * GW:(row0 + RS) * GW] = yc[si]
    y_prompt = np.stack(outs[:nseq_p], axis=0)
    y_sample = np.stack(outs[nseq_p:], axis=0)
    return (y_prompt, y_sample)
```
